# Optimizing a Trainium2 kernel written in Bass

```python
import math
import jax, jax.numpy as jnp
from jax import lax
import numpy as np

D_MODEL = 2048
BATCH = 16
SEQ = 2048
DEPTH = 4

N_A = DEPTH // 2
N_B = DEPTH - N_A
A_QK_DIM = 128
A_V_DIM = 2 * A_QK_DIM
A_HEADS = D_MODEL // A_V_DIM
A_WIDTH = A_HEADS * A_V_DIM
B_HEAD_DIM = 128
B_HEADS = D_MODEL // B_HEAD_DIM
B_WIDTH = B_HEADS * B_HEAD_DIM
BLOCK = 128
NEG = -1e30
EPS = 1e-5
DN_ALPHA = (2.0 * DEPTH) ** 0.25
DN_BETA = (8.0 * DEPTH) ** -0.25

kernel_name = "yoco_diffattn_fox_hybrid"


def layer_norm(x, g, b):
    xf = x.astype(jnp.float32)
    mu = jnp.mean(xf, axis=-1, keepdims=True)
    var = jnp.mean(jnp.square(xf - mu), axis=-1, keepdims=True)
    y = (xf - mu) * lax.rsqrt(var + EPS)
    return (y * g.astype(jnp.float32) + b.astype(jnp.float32)).astype(x.dtype)


def rms_norm(x, g):
    xf = x.astype(jnp.float32)
    y = xf * lax.rsqrt(jnp.mean(jnp.square(xf), axis=-1, keepdims=True) + EPS)
    return (y * g.astype(jnp.float32)).astype(x.dtype)


def to_blocks(a):
    b, s = a.shape[:2]
    a = a.reshape((b, s // BLOCK, BLOCK) + a.shape[2:])
    return jnp.moveaxis(a, 1, 0)


def from_blocks(a):
    a = jnp.moveaxis(a, 0, 1)
    return a.reshape((a.shape[0], a.shape[1] * a.shape[2]) + a.shape[3:])


def alibi_slopes(n_heads):
    return jnp.asarray([2.0 ** (-8.0 * (h + 1) / n_heads) for h in range(n_heads)], dtype=jnp.float32)


def diff_attention(h, w_in, w_out, lq1, lk1, lq2, lk2, subln_g, layer_idx):
    B, S, _ = h.shape
    proj = h @ w_in
    q, k, v, g = jnp.split(proj, 4, axis=-1)
    q = q.reshape(B, S, A_HEADS, 2, A_QK_DIM) * (A_QK_DIM ** -0.5)
    k = k.reshape(B, S, A_HEADS, 2, A_QK_DIM)
    v = v.reshape(B, S, A_HEADS, A_V_DIM)
    lam_init = 0.8 - 0.6 * math.exp(-0.3 * layer_idx)
    f32 = jnp.float32
    lam = (jnp.exp(jnp.sum(lq1.astype(f32) * lk1.astype(f32)))
           - jnp.exp(jnp.sum(lq2.astype(f32) * lk2.astype(f32))) + lam_init)
    slopes = alibi_slopes(A_HEADS)
    s_pos = jnp.arange(S)

    def block(args):
        qb, i = args
        t_pos = i * BLOCK + jnp.arange(BLOCK)
        dist = (t_pos[:, None] - s_pos[None, :]).astype(f32)
        bias = -slopes[:, None, None] * dist
        sc = jnp.einsum('bqhcd,bkhcd->bhcqk', qb, k).astype(f32) + bias[None, :, None]
        sc = jnp.where(dist >= 0, sc, NEG)
        p = jax.nn.softmax(sc, axis=-1)
        a = p[:, :, 0] - lam * p[:, :, 1]
        return jnp.einsum('bhqk,bkhe->bqhe', a.astype(v.dtype), v)

    o = from_blocks(lax.map(block, (to_blocks(q), jnp.arange(S // BLOCK))))
    o = rms_norm(o, subln_g) * (1.0 - lam_init)
    o = o.reshape(B, S, A_WIDTH) * jax.nn.silu(g)
    return o @ w_out


def shared_kv(x, c_act, w_mod_kv, b_mod_kv, w_kv, b_f):
    B, S, _ = x.shape
    mod = c_act @ w_mod_kv + b_mod_kv
    shift, scale = jnp.split(mod, 2, axis=-1)
    hk = x * (1.0 + scale[:, None]) + shift[:, None]
    kvf = hk @ w_kv
    k, v, zf = jnp.split(kvf, [B_WIDTH, 2 * B_WIDTH], axis=-1)
    k = k.reshape(B, S, B_HEADS, B_HEAD_DIM)
    v = v.reshape(B, S, B_HEADS, B_HEAD_DIM)
    log_f = jax.nn.log_sigmoid(zf.astype(jnp.float32) + b_f.astype(jnp.float32))
    F = jnp.cumsum(log_f, axis=1)
    return k, v, F


def forgetting_attention(h, w_q, w_out, k, v, F):
    B, S, _ = h.shape
    f32 = jnp.float32
    proj = h @ w_q
    q, g = jnp.split(proj, 2, axis=-1)
    q = q.reshape(B, S, B_HEADS, B_HEAD_DIM) * (B_HEAD_DIM ** -0.5)
    Fk = jnp.moveaxis(F, 1, 2)
    s_pos = jnp.arange(S)

    def block(args):
        qb, Fq, i = args
        t_pos = i * BLOCK + jnp.arange(BLOCK)
        causal = t_pos[:, None] >= s_pos[None, :]
        decay = jnp.moveaxis(Fq, 1, 2)[..., None] - Fk[:, :, None, :]
        sc = jnp.einsum('bqhd,bkhd->bhqk', qb, k).astype(f32) + decay
        sc = jnp.where(causal, sc, NEG)
        p = jax.nn.softmax(sc, axis=-1)
        return jnp.einsum('bhqk,bkhd->bqhd', p.astype(v.dtype), v)

    o = from_blocks(lax.map(block, (to_blocks(q), to_blocks(F), jnp.arange(S // BLOCK))))
    o = o.reshape(B, S, B_WIDTH) * jax.nn.silu(g)
    return o @ w_out


def setup_inputs(seed: int = 0) -> dict:
    key = jax.random.key(seed)
    ks = jax.random.split(key, 20)
    nrm = jax.random.normal
    D = D_MODEL
    x = nrm(ks[0], (BATCH, SEQ, D), jnp.float32)
    c = nrm(ks[1], (BATCH, D), jnp.float32)
    w_mod = nrm(ks[2], (DEPTH, D, 3 * D), jnp.float32) * D ** -0.5
    b_mod = nrm(ks[3], (DEPTH, 3 * D), jnp.float32) * 0.02
    ln_g = 1.0 + 0.02 * nrm(ks[4], (DEPTH, D), jnp.float32)
    ln_b = 0.02 * nrm(ks[5], (DEPTH, D), jnp.float32)
    a_w_in = nrm(ks[6], (N_A, D, 4 * A_WIDTH), jnp.float32) * D ** -0.5
    a_w_out = nrm(ks[7], (N_A, A_WIDTH, D), jnp.float32) * (A_WIDTH ** -0.5 * DN_BETA)
    a_lam_q1 = 0.1 * nrm(ks[8], (N_A, A_QK_DIM), jnp.float32)
    a_lam_k1 = 0.1 * nrm(ks[9], (N_A, A_QK_DIM), jnp.float32)
    a_lam_q2 = 0.1 * nrm(ks[10], (N_A, A_QK_DIM), jnp.float32)
    a_lam_k2 = 0.1 * nrm(ks[11], (N_A, A_QK_DIM), jnp.float32)
    a_subln_g = 1.0 + 0.02 * nrm(ks[12], (N_A, A_V_DIM), jnp.float32)
    kv_w_mod = nrm(ks[13], (D, 2 * D), jnp.float32) * D ** -0.5
    kv_b_mod = nrm(ks[14], (2 * D,), jnp.float32) * 0.02
    kv_w = nrm(ks[15], (D, 2 * B_WIDTH + B_HEADS), jnp.float32) * D ** -0.5
    kv_b_f = jax.random.uniform(ks[16], (B_HEADS,), jnp.float32, minval=1.0, maxval=6.0)
    b_w_in = nrm(ks[17], (N_B, D, 2 * B_WIDTH), jnp.float32) * D ** -0.5
    b_w_out = nrm(ks[18], (N_B, B_WIDTH, D), jnp.float32) * (B_WIDTH ** -0.5 * DN_BETA)
    return {"x": x, "c": c, "w_mod": w_mod, "b_mod": b_mod, "ln_g": ln_g, "ln_b": ln_b,
            "a_w_in": a_w_in, "a_w_out": a_w_out, "a_lam_q1": a_lam_q1, "a_lam_k1": a_lam_k1,
            "a_lam_q2": a_lam_q2, "a_lam_k2": a_lam_k2, "a_subln_g": a_subln_g,
            "kv_w_mod": kv_w_mod, "kv_b_mod": kv_b_mod, "kv_w": kv_w, "kv_b_f": kv_b_f,
            "b_w_in": b_w_in, "b_w_out": b_w_out}


def reference(x, c, w_mod, b_mod, ln_g, ln_b, a_w_in, a_w_out, a_lam_q1, a_lam_k1,
              a_lam_q2, a_lam_k2, a_subln_g, kv_w_mod, kv_b_mod, kv_w, kv_b_f,
              b_w_in, b_w_out):
    c_act = jax.nn.silu(c)
    kv = None
    for l in range(DEPTH):
        mod = c_act @ w_mod[l] + b_mod[l]
        shift, scale, gate = jnp.split(mod, 3, axis=-1)
        h = x * (1.0 + scale[:, None]) + shift[:, None]
        if l < N_A:
            y = diff_attention(h, a_w_in[l], a_w_out[l], a_lam_q1[l], a_lam_k1[l],
                               a_lam_q2[l], a_lam_k2[l], a_subln_g[l], l)
        else:
            if kv is None:
                kv = shared_kv(x, c_act, kv_w_mod, kv_b_mod, kv_w, kv_b_f)
            k_sh, v_sh, F_sh = kv
            y = forgetting_attention(h, b_w_in[l - N_A], b_w_out[l - N_A], k_sh, v_sh, F_sh)
        x = layer_norm(DN_ALPHA * x + gate[:, None] * y, ln_g[l], ln_b[l])
    return x
```

```python
from contextlib import ExitStack
import math
import numpy as np
import concourse.bass as bass
import concourse.mybir as mybir
from concourse.bass_utils import run_bass_kernel_spmd

F32 = mybir.dt.float32
BF16 = mybir.dt.bfloat16
I32 = mybir.dt.int32
AF = mybir.ActivationFunctionType
ALU = mybir.AluOpType

ENGS = ("pe", "act", "dve", "pool", "sp")
EPOCH = 30000

S = 2048
D = 2048
NCH = 16
NTB = 16
NSEQ = 2
DEPTH = 4
EPS = 1e-5
DN_ALPHA = (2.0 * DEPTH) ** 0.25
QSCALE = 128 ** -0.5
SBUF_BASE = 16512
SBUF_END = 229376


class Tk:
    __slots__ = ("key", "val")

    def __init__(self, key, val):
        self.key = key
        self.val = val


class Prog:
    def __init__(self, nc):
        self.nc = nc
        self.q = {e: [] for e in ENGS}
        self.cnt = {}
        self.seen = {}
        self.epoch = {e: 0 for e in ENGS}

    def wait(self, eng, *tickets):
        for t in tickets:
            if t is None:
                continue
            if isinstance(t, (list, tuple)):
                self.wait(eng, *t)
                continue
            if self.seen.get((eng, t.key), 0) >= t.val:
                continue
            self.seen[(eng, t.key)] = t.val
            self.q[eng].append(("wait", t.key, t.val))

    def op(self, eng, fn, inc=True, key=None, amount=1):
        if not inc:
            self.q[eng].append(("op", fn, None, 0))
            return None
        if key is None:
            key = (eng, self.epoch[eng])
            if self.cnt.get(key, 0) >= EPOCH:
                self.epoch[eng] += 1
                key = (eng, self.epoch[eng])
        self.cnt[key] = self.cnt.get(key, 0) + amount
        self.q[eng].append(("op", fn, key, amount))
        return Tk(key, self.cnt[key])

    def barrier(self, exclude=("wsl",)):
        tks = []
        for key, c in self.cnt.items():
            if key[0] == "dma" and str(key[1]).startswith(exclude):
                continue
            tks.append(Tk(key, c))
        for e in ENGS:
            self.wait(e, tks)

    def emit(self):
        nc = self.nc
        with ExitStack() as es:
            semh = {}
            for i, k in enumerate(self.cnt):
                semh[k] = es.enter_context(nc.semaphore("s%d" % i))
            blk = es.enter_context(nc.Block())

            def mk(eng):
                items = self.q[eng]

                def run(e):
                    for it in items:
                        if it[0] == "wait":
                            e.wait_ge(semh[it[1]], it[2])
                        else:
                            ins = it[1](e)
                            if it[2] is not None:
                                ins.then_inc(semh[it[2]], it[3])
                return run

            blk.tensor(mk("pe"))
            blk.scalar(mk("act"))
            blk.vector(mk("dve"))
            blk.gpsimd(mk("pool"))
            blk.sync(mk("sp"))


class Buf:
    _n = 0

    def __init__(self, t=None):
        self.t = t
        self.wt = None
        self.rts = {}
        Buf._n += 1
        self.id = Buf._n


class Ring:
    def __init__(self, bufs):
        self.bufs = bufs
        self.i = 0

    def next(self):
        b = self.bufs[self.i % len(self.bufs)]
        self.i += 1
        return b


class KB:
    def __init__(self, nc):
        self.nc = nc
        self.P = Prog(nc)
        self.off = SBUF_BASE
        self.nname = 0

    def sb(self, shape, dt, at=None):
        el = 2 if dt == BF16 else 4
        nb = int(np.prod(shape[1:])) * el
        nb = (nb + 63) // 64 * 64
        if at is None:
            at = self.off
            self.off += nb
            assert self.off <= SBUF_END, ("SBUF overflow", self.off)
        self.nname += 1
        return self.nc.alloc_sbuf_tensor_at("t%d" % self.nname, list(shape), dt, offset=at)

    def sbuf(self, shape, dt):
        return Buf(self.sb(shape, dt))

    def acquire(self, eng, reads=(), writes=()):
        P = self.P
        for b in reads:
            P.wait(eng, b.wt)
        for b in writes:
            P.wait(eng, b.wt)
            for k, v in b.rts.items():
                P.wait(eng, Tk(k, v))

    @staticmethod
    def release(t, reads=(), writes=()):
        for b in reads:
            b.rts[t.key] = max(b.rts.get(t.key, 0), t.val)
        for b in writes:
            b.wt = t
            b.rts = {}

    def do(self, eng, fn, reads=(), writes=(), key=None, amount=1):
        self.acquire(eng, reads, writes)
        t = self.P.op(eng, fn, key=key, amount=amount)
        self.release(t, reads, writes)
        return t

    def raw(self, eng, fn):
        self.P.op(eng, fn, inc=False)

    def dma(self, eng, out, in_, reads=(), writes=(), key=None):
        assert key is not None
        return self.do(eng, lambda e: e.dma_start(out=out, in_=in_), reads, writes, key=("dma", key), amount=16)

    def mm(self, out, lhsT, rhs, start=True, stop=True):
        return lambda e: e.matmul(out, lhsT=lhsT, rhs=rhs, start=start, stop=stop)

    def tr(self, out, in_, ident):
        return lambda e: e.transpose(out=out, in_=in_, identity=ident)

    def actf(self, out, in_, func, bias=None, scale=None, accum_out=None):
        kw = {}
        if bias is not None:
            kw["bias"] = bias
        if scale is not None:
            kw["scale"] = scale
        if accum_out is not None:
            kw["accum_out"] = accum_out
        return lambda e: e.activation(out=out, in_=in_, func=func, **kw)

    def tt(self, out, in0, in1, op):
        return lambda e: e.tensor_tensor(out=out, in0=in0, in1=in1, op=op)

    def ts(self, out, in0, s1, op0, s2=None, op1=None, accum_out=None):
        kw = {}
        if op1 is not None:
            kw["op1"] = op1
        if accum_out is not None:
            kw["accum_out"] = accum_out
        return lambda e: e.tensor_scalar(out=out, in0=in0, scalar1=s1, scalar2=s2, op0=op0, **kw)

    def stt(self, out, in0, scalar, in1, op0, op1, accum_out=None):
        kw = {}
        if accum_out is not None:
            kw["accum_out"] = accum_out
        return lambda e: e.scalar_tensor_tensor(out=out, in0=in0, scalar=scalar, in1=in1, op0=op0, op1=op1, **kw)

    def cp(self, out, in_):
        return lambda e: e.tensor_copy(out=out, in_=in_)

    def ms(self, ap, val):
        return lambda e: e.memset(ap, val)


def build_program(nlayers=DEPTH, stop=None):
    nc = bass.Bass("TRN2", target_bir_lowering=False)
    K = KB(nc)
    P = K.P

    def din(name, shape, dt=F32):
        return nc.dram_tensor(name, list(shape), dt, kind="ExternalInput").ap()

    def dscr(name, shape, dt):
        return nc.dram_tensor(name, list(shape), dt).ap()

    x_d = din("x", [NSEQ, S, D])
    cT_d = din("cT", [128, 2 * NCH])
    w_mod_d = din("w_mod", [DEPTH, D, 3 * D])
    b_mod_d = din("b_mod", [DEPTH, 3 * D])
    ln_g_d = din("ln_g", [DEPTH, D])
    ln_b_d = din("ln_b", [DEPTH, D])
    a_w_in_d = din("a_w_in", [2, D, 4 * D])
    a_w_out_d = din("a_w_out", [2, D, D])
    lam_d = [din(n, [2, 128]) for n in ("a_lam_q1", "a_lam_k1", "a_lam_q2", "a_lam_k2")]
    subln_d = din("a_subln_g", [2, 256])
    kv_w_mod_d = din("kv_w_mod", [D, 2 * D])
    kv_b_mod_d = din("kv_b_mod", [1, 2 * D])
    kv_w_d = din("kv_w", [D, 2 * D + 16])
    kv_b_f_d = din("kv_b_f", [16, 1])
    b_w_in_d = din("b_w_in", [2, D, 2 * D])
    b_w_out_d = din("b_w_out", [2, D, D])
    out_d = nc.dram_tensor("out", [NSEQ, S, D], F32, kind="ExternalOutput").ap()

    xs_d = dscr("xs", [NSEQ, S, D], F32)
    qT_d = dscr("qT", [NSEQ, D, S], BF16)
    kT_d = dscr("kT", [NSEQ, D, S], BF16)
    v_d = dscr("v", [NSEQ, S, D], BF16)
    g_d = dscr("g", [NSEQ, S, D], BF16)
    kTs_d = dscr("kTs", [NSEQ, D, S], BF16)
    vs_d = dscr("vs", [NSEQ, S, D], BF16)
    FT_d = dscr("FT", [NSEQ, 16, S], F32)
    gate_d = dscr("gate", [DEPTH, 2, D], F32)

    ps = nc.alloc_psum_tensor("ps", [128, 4096], F32)
    banks = [Buf(ps[:, i * 512:(i + 1) * 512]) for i in range(8)]
    for i, bk_ in enumerate(banks):
        bk_.bank = i
    ptb_ap = ps[:, 3584:4096].bitcast(BF16)

    ident = K.sb([128, 128], F32)
    identb = K.sb([128, 128], BF16)
    onesb = K.sb([128, 128], BF16)
    maskd = K.sb([128, 128], F32)
    biasA = K.sb([128, 8, 17], F32)
    colmod = K.sb([128, 5, 64], F32)
    cact = K.sb([128, 2 * NCH], F32)
    neglam = K.sb([128, 2], F32)
    gsub = K.sb([128, 2, 256], F32)
    nlam256 = K.sb([128, 2, 256], F32)
    negF = K.sb([128, NSEQ, 16, 16], F32)
    negbf = K.sb([16, 1], F32)
    onecol = K.sb([128, 1], F32)
    nhalf = K.sb([128, 1], F32)
    small = K.sb([128, 64], F32)
    HT = K.sb([128, NCH, S], BF16)
    HTB = [Buf() for _ in range(NTB)]
    W_BASE = K.off
    K.off += 65536
    H_BASE = K.off
    H_SIZE = SBUF_END - H_BASE
    assert H_SIZE >= 60 * 1024, H_SIZE

    def region(base, size):
        st = {"o": base}

        def take(shape, dt):
            el = 2 if dt == BF16 else 4
            nb = (int(np.prod(shape[1:])) * el + 63) // 64 * 64
            t = K.sb(shape, dt, at=st["o"])
            st["o"] += nb
            assert st["o"] <= base + size, ("region overflow", st["o"] - base, size)
            return t
        return take

    wtake = region(W_BASE, 65536)
    wslot = [Buf(wtake([128, NCH, 512], BF16)) for _ in range(4)]
    wtake2 = region(W_BASE, 65536)
    wf32 = [wtake2([128, NCH, 512], F32) for _ in range(2)]

    ctick = []

    def setup_consts():
        t = P.op("pool", K.ms(ident[:], 1.0))
        P.wait("pool", t)
        t = P.op("pool", lambda e: e.affine_select(out=ident[:], in_=ident[:], pattern=[[-1, 128]],
                                                   compare_op=ALU.is_equal, fill=0.0, base=0, channel_multiplier=1))
        P.wait("pool", t)
        ctick.append(P.op("pool", K.cp(identb[:], ident[:])))
        ctick.append(P.op("pool", K.ms(onesb[:], 1.0)))
        ctick.append(P.op("pool", K.ms(onecol[:], 1.0)))
        ctick.append(P.op("pool", K.ms(nhalf[:], -0.5)))
        t = P.op("pool", K.ms(maskd[:], 0.0))
        P.wait("pool", t)
        ctick.append(P.op("pool", lambda e: e.affine_select(out=maskd[:], in_=maskd[:], pattern=[[1, 128]],
                                                            compare_op=ALU.is_ge, fill=-30000.0, base=0,
                                                            channel_multiplier=-1)))
        io = K.sb([128, 17], I32, at=H_BASE)
        iof = K.sb([128, 17], F32, at=H_BASE + 128)
        t = P.op("pool", lambda e: e.iota(io[:], pattern=[[-128, 17]], base=-64, channel_multiplier=1))
        P.wait("pool", t)
        t = P.op("pool", K.cp(iof[:], io[:]))
        P.wait("pool", t)
        for h in range(8):
            slope = 2.0 ** (-8.0 * (h + 1) / 8)
            t = P.op("pool", K.ts(biasA[:, h, :], iof[:], slope, ALU.mult))
        ctick.append(t)
        craw = K.sb([128, 2 * NCH], F32, at=H_BASE + 256)
        bfr = K.sb([16, 1], F32, at=H_BASE + 512)
        P.op("sp", lambda e: e.dma_start(out=craw[:], in_=cT_d), key=("dma", "c0"), amount=16)
        P.op("sp", lambda e: e.dma_start(out=bfr[:], in_=kv_b_f_d), key=("dma", "c0"), amount=16)
        lt = [K.sb([128, 2, 128], F32, at=H_BASE + 1024 + i * 1024) for i in range(4)]
        sg = K.sb([128, 2, 256], F32, at=H_BASE + 1024 + 4096)
        junk = K.sb([128, 128], F32, at=H_BASE + 1024 + 4096 + 2048)
        ee = K.sb([128, 4], F32, at=H_BASE + 1024 + 4096 + 2048 + 512)
        tl = []
        for i in range(4):
            for l in range(2):
                tl.append(P.op("sp", lambda e, i=i, l=l: e.dma_start(out=lt[i][:, l, :], in_=lam_d[i][l:l + 1, :].partition_broadcast(128)),
                               key=("dma", "c0"), amount=16))
        for l in range(2):
            tl.append(P.op("sp", lambda e, l=l: e.dma_start(out=sg[:, l, :], in_=subln_d[l:l + 1, :].partition_broadcast(128)),
                           key=("dma", "c0"), amount=16))
        P.wait("dve", tl[-1])
        P.wait("act", tl[-1])
        ctick.append(P.op("act", K.actf(cact[:], craw[:], AF.Silu)))
        ctick.append(P.op("dve", K.ts(negbf[:], bfr[:], -1.0, ALU.mult)))
        t = None
        for l in range(2):
            for pr in range(2):
                if t is not None:
                    P.wait("dve", t)
                t = P.op("dve", K.stt(junk[:], lt[2 * pr][:, l, :], 1.0, lt[2 * pr + 1][:, l, :], ALU.mult, ALU.mult,
                                      accum_out=small[:, l * 2 + pr:l * 2 + pr + 1]))
        P.wait("act", t)
        t = P.op("act", K.actf(ee[:], small[:, 0:4], AF.Exp))
        P.wait("dve", t)
        for l in range(2):
            lam_init = 0.8 - 0.6 * math.exp(-0.3 * l)
            t = P.op("dve", K.tt(neglam[:, l:l + 1], ee[:, 2 * l + 1:2 * l + 2], ee[:, 2 * l:2 * l + 1], ALU.subtract))
            P.wait("dve", t)
            t = P.op("dve", K.ts(neglam[:, l:l + 1], neglam[:, l:l + 1], -lam_init, ALU.add))
            t = P.op("dve", K.ts(gsub[:, l, :], sg[:, l, :], 1.0 - lam_init, ALU.mult))
            P.wait("dve", t)
            t = P.op("dve", K.ms(nlam256[:, l, :], 1.0))
            P.wait("dve", t)
            t = P.op("dve", K.ts(nlam256[:, l, :], nlam256[:, l, :], neglam[:, l:l + 1], ALU.mult))
        ctick.append(t)
        for e in ENGS:
            P.wait(e, ctick)

    def mod_prologue():
        mods = [(w_mod_d[l], b_mod_d[l:l + 1, :], 3 * D) for l in range(DEPTH)] + [(kv_w_mod_d, kv_b_mod_d, 2 * D)]
        hb = region(H_BASE, H_SIZE)
        brow = [Buf(hb([2, 512], F32)) for _ in range(2)]
        mrow = [Buf(hb([2, 512], F32)) for _ in range(2)]
        wsl = [Buf(wf32[0]), Buf(wf32[1])]
        pm = [banks[0], banks[1]]
        pc = banks[2]
        colb = Buf()
        si = 0
        for i, (W, Bv, ncols) in enumerate(mods):
            for s in range(ncols // 512):
                n0 = s * 512
                ws = wsl[si % 2]
                br = brow[si % 2]
                mr = mrow[si % 2]
                pmb = pm[si % 2]
                K.acquire("sp", writes=[ws])
                for hh in range(2):
                    t = P.op("sp", lambda e, ws=ws, W=W, n0=n0, hh=hh: e.dma_start(
                        out=ws.t[:, hh * 8:(hh + 1) * 8, :],
                        in_=W[hh * 1024:(hh + 1) * 1024, n0:n0 + 512].rearrange("(c p) n -> p c n", p=128)),
                        key=("dma", "ws%d" % (si % 2)), amount=16)
                K.release(t, writes=[ws])
                K.dma("sp", br.t[:], Bv[:, n0:n0 + 512].partition_broadcast(2), writes=[br], key="br%d" % (si % 2))
                K.acquire("pe", reads=[ws], writes=[pmb])
                for ch in range(NCH):
                    t = P.op("pe", K.mm(pmb.t[0:2, :], cact[:, 2 * ch:2 * ch + 2], ws.t[:, ch, :], ch == 0, ch == NCH - 1),
                             inc=(ch == NCH - 1))
                K.release(t, reads=[ws], writes=[pmb])
                K.do("dve", K.tt(mr.t[:], pmb.t[0:2, :], br.t[:], ALU.add), reads=[pmb, br], writes=[mr])
                if n0 < 2 * D:
                    K.acquire("pe", reads=[mr], writes=([pc] if n0 == 0 else []))
                    for q in range(4):
                        ci = n0 // 128 + q
                        t = P.op("pe", K.mm(pc.t[:, ci * 2:ci * 2 + 2], mr.t[:, q * 128:(q + 1) * 128], ident[0:2, 0:2]),
                                 inc=(q == 3))
                    K.release(t, reads=[mr])
                    if n0 == 2 * D - 512:
                        K.release(t, writes=[pc])
                        K.do("dve", K.cp(colmod[:, i, 0:32], pc.t[:, 0:32]), reads=[pc], writes=[colb])
                        K.do("dve", K.ts(colmod[:, i, 32:64], pc.t[:, 32:64], 1.0, ALU.add), reads=[pc], writes=[colb])
                else:
                    K.dma("sp", gate_d[i, :, n0 - 2 * D:n0 - 2 * D + 512], mr.t[:], reads=[mr], key="gst%d" % (si % 2))
                si += 1
        for e in ENGS:
            P.wait(e, colb.wt)
        return [Tk(("dma", "gst0"), P.cnt.get(("dma", "gst0"), 0)), Tk(("dma", "gst1"), P.cnt.get(("dma", "gst1"), 0))]

    def phase1(modi, b, src):
        hb = region(H_BASE, H_SIZE)
        xr = Ring([Buf(hb([128, D], F32)) for _ in range(2)])
        tring = Ring([banks[6], banks[7]])
        for tb in range(NTB):
            xt = xr.next()
            K.dma("sp", xt.t[:], src[b, tb * 128:(tb + 1) * 128, :], writes=[xt], key="x%d" % (tb % 2))
            for g4 in range(4):
                bk = tring.next()
                K.acquire("pe", reads=[xt], writes=[bk])
                for k in range(4):
                    ch = g4 * 4 + k
                    t = P.op("pe", K.tr(bk.t[:, k * 128:(k + 1) * 128], xt.t[:, ch * 128:(ch + 1) * 128], ident[:]), inc=(k == 3))
                K.release(t, reads=[xt], writes=[bk])
                K.acquire("act", reads=[bk], writes=([HTB[tb]] if g4 == 0 else []))
                for k in range(4):
                    ch = g4 * 4 + k
                    t = P.op("act", K.actf(HT[:, ch, tb * 128:(tb + 1) * 128], bk.t[:, k * 128:(k + 1) * 128], AF.Identity,
                                           bias=colmod[:, modi, ch * 2 + b:ch * 2 + b + 1],
                                           scale=colmod[:, modi, 32 + ch * 2 + b:32 + ch * 2 + b + 1]), inc=(k == 3))
                K.release(t, reads=[bk])
            K.release(t, writes=[HTB[tb]])

    class Slabs:
        def __init__(self, nslots):
            self.slots = wslot[:nslots]
            self.i = 0

        def load(self, W, n0, ncols=512):
            s = self.slots[self.i % len(self.slots)]
            k = self.i % len(self.slots)
            self.i += 1
            K.acquire("pool", writes=[s])
            for hh in range(2):
                t = P.op("pool", lambda e, s=s, W=W, n0=n0, hh=hh, ncols=ncols: e.dma_start(
                    out=s.t[:, hh * 8:(hh + 1) * 8, 0:ncols],
                    in_=W[hh * 1024:(hh + 1) * 1024, n0:n0 + ncols].rearrange("(c p) n -> p c n", p=128)),
                    key=("dma", "wsl%d" % k), amount=16)
            K.release(t, writes=[s])
            return s

    def proj_feature_major(hb_stage, slab, n_f0, dst, b, func, scale, st):
        for fc in range(4):
            grp = st["grp"] % 2
            st["grp"] += 1
            bks = banks[grp * 4:grp * 4 + 4]
            K.acquire("pe", reads=[slab] + HTB, writes=bks)
            for ch in range(NCH):
                for tt in range(4):
                    t = P.op("pe", K.mm(bks[tt].t[:, :], slab.t[:, ch, fc * 128:(fc + 1) * 128], HT[:, ch, tt * 512:(tt + 1) * 512],
                                        ch == 0, ch == NCH - 1), inc=(ch == NCH - 1 and tt == 3))
            K.release(t, reads=[slab] + HTB, writes=bks)
            stg = hb_stage.next()
            K.acquire("act", writes=[stg])
            K.acquire("dve", writes=[stg])
            stg.wt = None
            stg.rts = {}
            parts = []
            for tt in range(4):
                use_act = (func is not None) or ((st["ev"] % 2) == 0)
                st["ev"] += 1
                o = stg.t[:, tt * 512:(tt + 1) * 512]
                if use_act:
                    af = func if func is not None else (AF.Identity if scale is not None else AF.Copy)
                    fn = K.actf(o, bks[tt].t[:, :], af, scale=scale)
                    tl = K.do("act", fn, reads=[bks[tt]])
                else:
                    if scale is not None:
                        fn = K.ts(o, bks[tt].t[:, :], scale, ALU.mult)
                    else:
                        fn = K.cp(o, bks[tt].t[:, :])
                    tl = K.do("dve", fn, reads=[bks[tt]])
                parts.append(tl)
            P.wait("sp", parts)
            f0 = n_f0 + fc * 128
            tstore = K.dma("sp", dst[b, f0:f0 + 128, :], stg.t[:], reads=[stg], key="fst%d" % stg.idx)
            st["stores"].append(tstore)

    def proj_token_major(hb_stage, slab, n0, dst, b, func, st):
        for tb in range(NTB):
            bk = banks[st["tbk"] % 8]
            st["tbk"] += 1
            K.acquire("pe", reads=[slab, HTB[tb]], writes=[bk])
            for ch in range(NCH):
                t = P.op("pe", K.mm(bk.t[:, :], HT[:, ch, tb * 128:(tb + 1) * 128], slab.t[:, ch, :], ch == 0, ch == NCH - 1),
                         inc=(ch == NCH - 1))
            K.release(t, reads=[slab, HTB[tb]], writes=[bk])
            stg = hb_stage.next()
            use_act = (func is not None) or ((st["ev"] % 2) == 0)
            st["ev"] += 1
            if use_act:
                K.do("act", K.actf(stg.t[:], bk.t[:, :], func if func is not None else AF.Copy), reads=[bk], writes=[stg])
            else:
                K.do("dve", K.cp(stg.t[:], bk.t[:, :]), reads=[bk], writes=[stg])
            tstore = K.dma("sp", dst[b, tb * 128:(tb + 1) * 128, n0:n0 + 512], stg.t[:], reads=[stg], key="tst%d" % stg.idx)
            st["stores"].append(tstore)

    def mk_stage(hb, n, shape):
        bufs = []
        for i in range(n):
            bb = Buf(hb(shape, BF16))
            bb.idx = i
            bb.parts = []
            bufs.append(bb)
        return Ring(bufs)

    def plan_of(kind, l):
        if kind == "A":
            return a_w_in_d[l], 16
        if kind == "B":
            return b_w_in_d[l - 2], 8
        return kv_w_d, 8

    def preload(kind, l):
        W, ns = plan_of(kind, l)
        slabs = Slabs(3)
        loaded = {}
        for s in range(min(3, ns)):
            loaded[s] = slabs.load(W, s * 512)
        return slabs, loaded

    def phase2(kind, l, b, hb, pre):
        st = {"grp": 0, "ev": 0, "tbk": 0, "stores": []}
        fstage = mk_stage(hb, 2, [128, S])
        tstage = mk_stage(hb, 4, [128, 512])
        slabs, loaded = pre
        if kind == "A":
            W = a_w_in_d[l]
            plan = [("f", qT_d, None, QSCALE)] * 4 + [("f", kT_d, None, None)] * 4 + [("t", v_d, None, None)] * 4 + [("t", g_d, AF.Silu, None)] * 4
        elif kind == "B":
            W = b_w_in_d[l - 2]
            plan = [("f", qT_d, None, QSCALE)] * 4 + [("f", g_d, AF.Silu, None)] * 4
        else:
            W = kv_w_d
            plan = [("f", kTs_d, None, None)] * 4 + [("t", vs_d, None, None)] * 4
        ns = len(plan)
        for s in range(ns):
            typ, dst, func, scale = plan[s]
            slab = loaded.pop(s)
            if typ == "f":
                proj_feature_major(fstage, slab, (s % 4) * 512, dst, b, func, scale, st)
            else:
                proj_token_major(tstage, slab, (s % 4) * 512, dst, b, func, st)
            if s + 3 < ns:
                loaded[s + 3] = slabs.load(W, (s + 3) * 512)
        return st["stores"]

    def phase3A(l, b, stores, hb):
        P.wait("sp", stores)
        sets = []
        for i in range(2):
            sets.append({"q": Buf(hb([128, 2, S], BF16)), "k": Buf(hb([128, 2, S], BF16)),
                         "v": Buf(hb([128, NTB, 257], BF16)), "i": i})
        for s_ in sets:
            K.do("pool", K.ms(s_["v"].t[:, :, 256:257], 1.0), writes=[s_["v"]])
        NG = 5
        gbufs = [Buf(hb([128, 256], BF16)) for _ in range(NG)]
        for i, g_ in enumerate(gbufs):
            g_.idx = i
        gring = Ring(gbufs)
        ggr = Ring([Buf(hb([128, 256], F32)) for _ in range(4)])
        ptring = Ring([Buf(hb([128, 2, 128], BF16)) for _ in range(6)])
        tmpr = Ring([Buf(hb([128, 256], F32)) for _ in range(2)])
        orr = Ring([Buf(hb([128, 256], F32)) for _ in range(4)])
        ofr = Ring([Buf(hb([128, 256], BF16)) for _ in range(4)])
        smr = Ring([Buf(hb([128, 8], F32)) for _ in range(8)])
        sslots = []
        for bi in range(3):
            sb_ = banks[bi]
            sb_.v = sb_.t[:, 0:256].rearrange("p (m q) -> p m q", m=2)
            sslots.append(sb_)
        sring = Ring(sslots)
        osets = [(banks[3], banks[4]), (banks[5], banks[6])]
        banks[7].v = ptb_ap[:, 0:256].rearrange("p (e t) -> p e t", e=2)
        pslot = banks[7]

        def load_head(h):
            s_ = sets[h % 2]
            K.dma("sp", s_["q"].t[:], qT_d[b, h * 256:(h + 1) * 256, :].rearrange("(m p) t -> p m t", p=128),
                  writes=[s_["q"]], key="hq%d" % s_["i"])
            K.dma("sp", s_["k"].t[:], kT_d[b, h * 256:(h + 1) * 256, :].rearrange("(m p) t -> p m t", p=128),
                  writes=[s_["k"]], key="hk%d" % s_["i"])
            K.acquire("sp", writes=[s_["v"]])
            for hh in range(2):
                t = P.op("sp", lambda e, s_=s_, hh=hh: e.dma_start(
                    out=s_["v"].t[:, hh * 8:(hh + 1) * 8, 0:256],
                    in_=v_d[b, hh * 1024:(hh + 1) * 1024, h * 256:(h + 1) * 256].rearrange("(kb p) e -> p kb e", p=128)),
                    key=("dma", "hv%d" % s_["i"]), amount=16)
            K.release(t, writes=[s_["v"]])

        steps = [(h, qb, j) for h in range(8) for qb in range(NTB) for j in range(qb + 1)]
        pend = {}
        first_w = set()
        state = {"oset": 0, "fid": 0}
        gts = {}
        dq = []
        FMAX = 3

        def load_g(h, qb):
            gt = gring.next()
            K.dma("sp", gt.t[:], g_d[b, qb * 128:(qb + 1) * 128, h * 256:(h + 1) * 256], writes=[gt], key="g%d" % gt.idx)
            gts[(h, qb)] = gt

        def run_due(idx, force_upto=-1):
            i = 0
            while i < len(dq):
                due, fid, fn = dq[i]
                if due <= idx or fid <= force_upto:
                    dq.pop(i)
                    fn(idx)
                    i = 0
                else:
                    i += 1

        def qk(idx):
            h, qb, j = steps[idx]
            s_ = sets[h % 2]
            sl = sring.next()
            K.acquire("pe", reads=[s_["q"], s_["k"]], writes=[sl])
            for m in range(2):
                t = P.op("pe", K.mm(sl.v[:, m, :], s_["k"].t[:, m, j * 128:(j + 1) * 128], s_["q"].t[:, m, qb * 128:(qb + 1) * 128]),
                         inc=(m == 1))
            K.release(t, reads=[s_["q"], s_["k"]], writes=[sl])
            pt = ptring.next()
            K.do("act", K.actf(pt.t[:], sl.v, AF.Exp, bias=biasA[:, h, qb - j:qb - j + 1], scale=1.0), reads=[sl], writes=[pt])
            if j == qb:
                K.do("pool", lambda e, pt=pt: e.affine_select(out=pt.t[:], in_=pt.t[:], pattern=[[0, 2], [1, 128]],
                                                              compare_op=ALU.is_ge, fill=0.0, base=0, channel_multiplier=-1),
                     reads=[], writes=[pt])
            pend[idx] = pt

        def pv(idx, cur):
            h, qb, j = steps[idx]
            s_ = sets[h % 2]
            pt = pend.pop(idx)
            if j == 0:
                state["oset"] ^= 1
            ob = osets[state["oset"]]
            K.acquire("pe", reads=[pt, s_["v"]], writes=(list(ob) if j == 0 else []))
            for m in range(2):
                t = P.op("pe", K.mm(ob[m].t[:, 0:257], pt.t[:, m, :], s_["v"].t[:, j, :], j == 0, j == qb), inc=(m == 1))
            K.release(t, reads=[pt, s_["v"]])
            if j == qb:
                K.release(t, writes=list(ob))
                finalize(h, qb, ob, cur)

        def finalize(h, qb, ob, cur):
            fid = state["fid"]
            state["fid"] += 1
            run_due(cur, force_upto=fid - FMAX)
            sm = smr.next()
            tmp = tmpr.next()
            o = orr.next()
            gg = ggr.next()
            gt = gts.pop((h, qb))
            nxt = (h, qb + 1) if qb + 1 < NTB else ((h + 1, 0) if h + 1 < 8 else None)
            if nxt is not None:
                load_g(*nxt)
            lcol = ps[:, ob[0].bank * 512 + 256:ob[1].bank * 512 + 257:512]
            K.do("dve", lambda e: e.reciprocal(out=sm.t[:, 0:2], in_=lcol), reads=[ob[0], ob[1]], writes=[sm])
            K.do("dve", K.stt(tmp.t[:], ob[1].t[:, 0:256], sm.t[:, 1:2], nlam256[:, l, :], ALU.mult, ALU.mult),
                 reads=[ob[1], sm], writes=[tmp])
            K.do("dve", K.stt(o.t[:], ob[0].t[:, 0:256], sm.t[:, 0:1], tmp.t[:], ALU.mult, ALU.add), reads=[ob[0], tmp, sm], writes=[o])
            K.do("dve", K.stt(tmp.t[:], o.t[:], 1.0, o.t[:], ALU.mult, ALU.mult, accum_out=sm.t[:, 3:4]), reads=[o], writes=[tmp, sm])
            K.do("dve", K.ts(sm.t[:, 4:5], sm.t[:, 3:4], 1.0 / 256, ALU.mult, EPS, ALU.add), reads=[sm], writes=[sm])
            K.do("pool", K.tt(sm.t[:, 5:6], sm.t[:, 4:5], nhalf[:], ALU.pow), reads=[sm], writes=[sm])
            K.do("pool", K.tt(gg.t[:], gt.t[:], gsub[:, l, :], ALU.mult), reads=[gt], writes=[gg])

            def stage1b(cur2):
                of = ofr.next()
                K.do("dve", K.stt(of.t[:], o.t[:], sm.t[:, 5:6], gg.t[:], ALU.mult, ALU.mult), reads=[o, sm, gg], writes=[of])

                def stage2(cur3):
                    K.acquire("pe", reads=[of], writes=[pslot])
                    for e2 in range(2):
                        t = P.op("pe", K.tr(pslot.v[:, e2, :], of.t[:, e2 * 128:(e2 + 1) * 128], identb[:]), inc=(e2 == 1))
                    K.release(t, reads=[of], writes=[pslot])
                    wr = [HTB[qb]] if (qb not in first_w) else []
                    first_w.add(qb)
                    K.acquire("dve", reads=[pslot], writes=wr)
                    t = P.op("dve", K.cp(HT[:, 2 * h:2 * h + 2, qb * 128:(qb + 1) * 128], pslot.v))
                    K.release(t, reads=[pslot], writes=[HTB[qb]])

                dq.append([cur2 + 2, fid, stage2])

            dq.append([cur + 2, fid, stage1b])

        LA = 3
        n = len(steps)
        for idx in range(n + LA):
            if idx == 0:
                load_head(0)
                load_head(1)
                load_g(0, 0)
            if idx < n:
                qk(idx)
            if idx >= LA:
                pv(idx - LA, idx)
                h, qb, j = steps[idx - LA]
                if qb == 0 and j == 0 and h >= 1 and h + 1 < 8:
                    load_head(h + 1)
            run_due(idx)
        while dq:
            run_due(0, force_upto=10 ** 9)

    def phase3B(l, b, stores, hb):
        P.wait("sp", stores)
        sets = []
        for i in range(2):
            sets.append({"q": Buf(hb([128, S], BF16)), "g": Buf(hb([128, S], BF16)), "k": Buf(hb([128, S], BF16)),
                         "v": Buf(hb([128, NTB, 128], BF16)), "f": Buf(hb([128, S], F32)), "i": i})
        Tr = Ring([Buf(hb([128, 512], F32)) for _ in range(3)])
        ptring = Ring([Buf(hb([128, 512], BF16)) for _ in range(4)])
        lsr = Ring([Buf(hb([128, 512], F32)) for _ in range(1)])
        rcr = Ring([Buf(hb([128, 512], F32)) for _ in range(2)])
        onr = Ring([Buf(hb([128, 512], F32)) for _ in range(2)])
        sring = Ring([banks[0], banks[1], banks[2], banks[7]])
        osets = [(banks[3], banks[5]), (banks[4], banks[6])]

        def load_head(h):
            s_ = sets[h % 2]
            i = s_["i"]
            K.dma("sp", s_["q"].t[:], qT_d[b, h * 128:(h + 1) * 128, :], writes=[s_["q"]], key="hq%d" % i)
            K.dma("sp", s_["k"].t[:], kTs_d[b, h * 128:(h + 1) * 128, :], writes=[s_["k"]], key="hk%d" % i)
            K.acquire("sp", writes=[s_["v"]])
            for hh in range(2):
                t = P.op("sp", lambda e, s_=s_, hh=hh: e.dma_start(
                    out=s_["v"].t[:, hh * 8:(hh + 1) * 8, :],
                    in_=vs_d[b, hh * 1024:(hh + 1) * 1024, h * 128:(h + 1) * 128].rearrange("(kb p) e -> p kb e", p=128)),
                    key=("dma", "hv%d" % i), amount=16)
            K.release(t, writes=[s_["v"]])
            K.dma("sp", s_["f"].t[:], FT_d[b, h:h + 1, :].partition_broadcast(128), writes=[s_["f"]], key="hf%d" % i)
            K.dma("sp", s_["g"].t[:], g_d[b, h * 128:(h + 1) * 128, :], writes=[s_["g"]], key="hg%d" % i)

        steps = [(h, qt, j) for h in range(16) for qt in range(4) for j in range(4 * qt + 4)]
        pend = {}
        first_w = set()
        state = {"oset": 0}

        def qk(idx):
            h, qt, j = steps[idx]
            s_ = sets[h % 2]
            jl = j - 4 * qt
            qlo = 128 * jl if jl > 0 else 0
            sl = sring.next()
            K.acquire("pe", reads=[s_["q"], s_["k"]], writes=[sl])
            t = P.op("pe", K.mm(sl.t[:, qlo:512], s_["k"].t[:, j * 128:(j + 1) * 128], s_["q"].t[:, qt * 512 + qlo:(qt + 1) * 512]))
            K.release(t, reads=[s_["q"], s_["k"]], writes=[sl])
            T = Tr.next()
            K.do("dve", K.stt(T.t[:, qlo:512], sl.t[:, qlo:512], negF[:, b, j, h:h + 1], s_["f"].t[:, qt * 512 + qlo:(qt + 1) * 512],
                              ALU.add, ALU.add), reads=[sl, s_["f"]], writes=[T])
            if jl >= 0:
                K.do("pool", K.tt(T.t[:, qlo:qlo + 128], T.t[:, qlo:qlo + 128], maskd[:], ALU.add), reads=[T], writes=[T])
            pt = ptring.next()
            K.do("act", K.actf(pt.t[:, qlo:512], T.t[:, qlo:512], AF.Exp), reads=[T], writes=[pt])
            pend[idx] = (pt, qlo)

        def pv(idx):
            h, qt, j = steps[idx]
            s_ = sets[h % 2]
            pt, qlo = pend.pop(idx)
            last = (j == 4 * qt + 3)
            if j == 0:
                state["oset"] ^= 1
            ob, lb = osets[state["oset"]]
            K.acquire("pe", reads=[pt, s_["v"]], writes=([ob, lb] if j == 0 else []))
            P.op("pe", K.mm(ob.t[:, qlo:512], s_["v"].t[:, j, :], pt.t[:, qlo:512], j == 0, last), inc=False)
            t = P.op("pe", K.mm(lb.t[:, qlo:512], onesb[:], pt.t[:, qlo:512], j == 0, last))
            K.release(t, reads=[pt, s_["v"]])
            if last:
                K.release(t, writes=[ob, lb])
                rc = rcr.next()
                on = onr.next()
                ls = lsr.next()
                K.do("dve", K.cp(ls.t[:], lb.t[:, :]), reads=[lb], writes=[ls])
                K.do("dve", K.cp(on.t[:], ob.t[:, :]), reads=[ob], writes=[on])
                K.do("act", K.actf(rc.t[:], ls.t[:], AF.Ln), reads=[ls], writes=[rc])
                K.do("act", K.actf(rc.t[:], rc.t[:], AF.Exp, scale=-1.0), reads=[rc], writes=[rc])
                K.do("pool", K.tt(ls.t[:], ls.t[:], rc.t[:], ALU.mult), reads=[ls, rc], writes=[ls])
                K.do("pool", K.ts(ls.t[:], ls.t[:], -1.0, ALU.mult), reads=[ls], writes=[ls])
                K.do("pool", K.ts(ls.t[:], ls.t[:], 2.0, ALU.add), reads=[ls], writes=[ls])
                K.do("pool", K.tt(rc.t[:], rc.t[:], ls.t[:], ALU.mult), reads=[rc, ls], writes=[rc])
                K.do("pool", K.tt(on.t[:], on.t[:], rc.t[:], ALU.mult), reads=[on, rc], writes=[on])
                tbs = [HTB[qt * 4 + k] for k in range(4)]
                wr = [x_ for x_ in tbs if x_.id not in first_w]
                for x_ in tbs:
                    first_w.add(x_.id)
                K.acquire("pool", reads=[on, s_["g"]], writes=wr)
                t = P.op("pool", K.tt(HT[:, h, qt * 512:(qt + 1) * 512], on.t[:], s_["g"].t[:, qt * 512:(qt + 1) * 512], ALU.mult))
                K.release(t, reads=[on, s_["g"]], writes=tbs)

        LA = 3
        n = len(steps)
        for idx in range(n + LA):
            if idx == 0:
                load_head(0)
                load_head(1)
            if idx < n:
                qk(idx)
            if idx >= LA:
                pv(idx - LA)
                h, qt, j = steps[idx - LA]
                if qt == 0 and j == 0 and h >= 1 and h + 1 < 16:
                    load_head(h + 1)

    def prefetch_wout(W):
        for n in range(4):
            s = wslot[n]
            K.acquire("pool", writes=[s])
            for hh in range(2):
                t = P.op("pool", lambda e, s=s, W=W, n=n, hh=hh: e.dma_start(
                    out=s.t[:, hh * 8:(hh + 1) * 8, :],
                    in_=W[hh * 1024:(hh + 1) * 1024, n * 512:(n + 1) * 512].rearrange("(c p) n -> p c n", p=128)),
                    key=("dma", "wsl%d" % n), amount=16)
            K.release(t, writes=[s])

    def phase4(l, b, src, dst, gate_t, hb):
        lng = Buf(hb([128, D], F32))
        lnb = Buf(hb([128, D], F32))
        gtb = Buf(hb([128, D], F32))
        xr = Ring([Buf(hb([128, D], F32)) for _ in range(2)])
        zr = Ring([Buf(hb([128, D], F32)) for _ in range(2)])
        smr = Ring([Buf(hb([128, 8], F32)) for _ in range(2)])
        P.wait("sp", gate_t)
        K.dma("sp", lng.t[:], ln_g_d[l:l + 1, :].partition_broadcast(128), writes=[lng], key="lng")
        K.dma("sp", lnb.t[:], ln_b_d[l:l + 1, :].partition_broadcast(128), writes=[lnb], key="lnb")
        K.dma("sp", gtb.t[:], gate_d[l, b:b + 1, :].partition_broadcast(128), writes=[gtb], key="gtb")
        stores = []
        xts = {}

        def ldx(tb):
            xt = xr.next()
            K.dma("sp", xt.t[:], src[b, tb * 128:(tb + 1) * 128, :], writes=[xt], key="x%d" % (tb % 2))
            xts[tb] = xt

        ldx(0)
        for tb in range(NTB):
            if tb + 1 < NTB:
                ldx(tb + 1)
            xt = xts.pop(tb)
            bks = banks[(tb % 2) * 4:(tb % 2) * 4 + 4]
            K.acquire("pe", reads=wslot + [HTB[tb]], writes=bks)
            for fc in range(NCH):
                for n in range(4):
                    t = P.op("pe", K.mm(bks[n].t[:, :], HT[:, fc, tb * 128:(tb + 1) * 128], wslot[n].t[:, fc, :], fc == 0, fc == NCH - 1),
                             inc=(fc == NCH - 1 and n == 3))
            K.release(t, reads=wslot + [HTB[tb]], writes=bks)
            z = zr.next()
            sm = smr.next()
            K.acquire("dve", writes=[z])
            for n in range(4):
                t = K.do("dve", K.tt(z.t[:, n * 512:(n + 1) * 512], bks[n].t[:, :], gtb.t[:, n * 512:(n + 1) * 512], ALU.mult),
                         reads=[bks[n], gtb])
            K.release(t, writes=[z])
            K.do("dve", K.stt(z.t[:], xt.t[:], DN_ALPHA, z.t[:], ALU.mult, ALU.add, accum_out=sm.t[:, 0:1]),
                 reads=[xt, z], writes=[z, sm])
            K.do("act", K.actf(xt.t[:], z.t[:], AF.Square, accum_out=sm.t[:, 1:2]), reads=[z], writes=[xt, sm])
            K.do("dve", K.ts(sm.t[:, 2:3], sm.t[:, 0:1], 1.0 / D, ALU.mult), reads=[sm], writes=[sm])
            K.do("dve", K.tt(sm.t[:, 3:4], sm.t[:, 2:3], sm.t[:, 2:3], ALU.mult), reads=[sm], writes=[sm])
            K.do("dve", K.stt(sm.t[:, 4:5], sm.t[:, 1:2], 1.0 / D, sm.t[:, 3:4], ALU.mult, ALU.subtract), reads=[sm], writes=[sm])
            K.do("dve", K.ts(sm.t[:, 7:8], sm.t[:, 4:5], EPS, ALU.add), reads=[sm], writes=[sm])
            K.do("pool", K.tt(sm.t[:, 5:6], sm.t[:, 7:8], nhalf[:], ALU.pow), reads=[sm], writes=[sm])
            K.do("dve", K.stt(sm.t[:, 6:7], sm.t[:, 2:3], -1.0, sm.t[:, 5:6], ALU.mult, ALU.mult), reads=[sm], writes=[sm])
            K.do("act", K.actf(z.t[:], z.t[:], AF.Identity, bias=sm.t[:, 6:7], scale=sm.t[:, 5:6]), reads=[sm, z], writes=[z])
            K.do("pool", K.tt(z.t[:], z.t[:], lng.t[:], ALU.mult), reads=[z, lng], writes=[z])
            K.do("pool", K.tt(z.t[:], z.t[:], lnb.t[:], ALU.add), reads=[z, lnb], writes=[z])
            stores.append(K.dma("sp", dst[b, tb * 128:(tb + 1) * 128, :], z.t[:], reads=[z], key="zst%d" % (tb % 2)))
        return stores

    def kv_extras(b, hb):
        wz = Buf(hb([128, NCH, 16], BF16))
        e1 = Buf(hb([16, S], F32))
        sp_ = Buf(hb([16, S], F32))
        fpos = Buf(hb([16, S], F32))
        fneg = Buf(hb([16, S], F32))
        K.dma("pool", wz.t[:], kv_w_d[:, 2 * D:2 * D + 16].rearrange("(c p) n -> p c n", p=128), writes=[wz], key="wz")
        bks = banks[0:4]
        K.acquire("pe", reads=[wz] + HTB, writes=bks)
        for ch in range(NCH):
            for tt in range(4):
                t = P.op("pe", K.mm(bks[tt].t[0:16, :], wz.t[:, ch, :], HT[:, ch, tt * 512:(tt + 1) * 512], ch == 0, ch == NCH - 1),
                         inc=(ch == NCH - 1 and tt == 3))
        K.release(t, reads=[wz] + HTB, writes=bks)
        for tt in range(4):
            K.do("act", K.actf(e1.t[:, tt * 512:(tt + 1) * 512], bks[tt].t[0:16, :], AF.Exp, bias=negbf[:], scale=-1.0),
                 reads=[bks[tt]], writes=([e1] if tt == 0 else []))
        K.do("act", K.actf(sp_.t[:], e1.t[:], AF.Ln, bias=onecol[0:16, :], scale=1.0), reads=[e1], writes=[sp_])
        K.do("dve", lambda e: e.tensor_tensor_scan(out=fpos.t[:], data0=sp_.t[:], data1=sp_.t[:], initial=0.0,
                                                   op0=ALU.add, op1=ALU.bypass), reads=[sp_], writes=[fpos])
        K.do("dve", K.ts(fneg.t[:], fpos.t[:], -1.0, ALU.mult), reads=[fpos], writes=[fneg])
        tst = K.dma("sp", FT_d[b], fneg.t[:], reads=[fneg], key="ftst")
        bk = banks[4]
        K.acquire("pe", reads=[fpos], writes=[bk])
        for kb in range(NTB):
            t = P.op("pe", K.tr(bk.t[:, kb * 16:(kb + 1) * 16], fpos.t[:, kb * 128:(kb + 1) * 128], ident[0:16, 0:16]), inc=(kb == NTB - 1))
        K.release(t, reads=[fpos], writes=[bk])
        nb = Buf()
        K.do("dve", K.cp(negF[:, b, :, :], bk.t[:, 0:256].rearrange("p (k h) -> p k h", k=16)), reads=[bk], writes=[nb])
        for e in ENGS:
            P.wait(e, nb.wt)
        return [tst]

    class _Stop(Exception):
        pass

    stage = {"n": 0}

    def chk():
        stage["n"] += 1
        if stop is not None and stage["n"] >= stop:
            raise _Stop()

    def driver():
        setup_consts()
        chk()
        gate_t = mod_prologue()
        chk()
        kvstores = {}
        for l in range(nlayers):
            src = x_d if l == 0 else xs_d
            dst = out_d if l == nlayers - 1 else xs_d
            for b in range(NSEQ):
                if l == 2:
                    pre = preload("KV", l)
                    P.barrier()
                    phase1(4, b, xs_d)
                    P.barrier()
                    hb = region(H_BASE, H_SIZE)
                    st = phase2("KV", l, b, hb, pre)
                    st += kv_extras(b, hb)
                    kvstores[b] = st
                kind = "A" if l < 2 else "B"
                pre = preload(kind, l)
                P.barrier()
                phase1(l, b, src)
                chk()
                P.barrier()
                stores = phase2(kind, l, b, region(H_BASE, H_SIZE), pre)
                chk()
                prefetch_wout(a_w_out_d[l] if l < 2 else b_w_out_d[l - 2])
                P.barrier()
                if l < 2:
                    phase3A(l, b, stores, region(H_BASE, H_SIZE))
                else:
                    phase3B(l, b, stores + kvstores[b], region(H_BASE, H_SIZE))
                chk()
                P.barrier()
                fin = phase4(l, b, src, dst, gate_t, region(H_BASE, H_SIZE))
                P.wait("sp", fin)
                chk()

    try:
        driver()
    except _Stop:
        pass
    P.barrier(exclude=("__none__",))
    P.emit()
    return nc


_NC_CACHE = {}


def kernel(**inputs):
    n = 8
    if "nc" not in _NC_CACHE:
        _NC_CACHE["nc"] = build_program()
    nc = _NC_CACHE["nc"]
    f = lambda a: np.ascontiguousarray(np.asarray(a, dtype=np.float32))
    x = f(inputs["x"])
    c = f(inputs["c"])
    shared = {k: f(inputs[k]) for k in ("w_mod", "b_mod", "ln_g", "ln_b", "a_w_in", "a_w_out", "a_lam_q1", "a_lam_k1",
                                        "a_lam_q2", "a_lam_k2", "a_subln_g", "kv_w_mod", "kv_w", "b_w_in", "b_w_out")}
    shared["kv_b_mod"] = f(inputs["kv_b_mod"]).reshape(1, -1)
    shared["kv_b_f"] = f(inputs["kv_b_f"]).reshape(16, 1)
    in_maps = []
    for i in range(n):
        m = dict(shared)
        m["x"] = np.ascontiguousarray(x[2 * i:2 * i + 2])
        cc = c[2 * i:2 * i + 2]
        m["cT"] = np.ascontiguousarray(cc.reshape(2, NCH, 128).transpose(2, 1, 0).reshape(128, 2 * NCH))
        in_maps.append(m)
    res = run_bass_kernel_spmd(nc, in_maps, core_ids=list(range(n)))
    return np.concatenate([r["out"] for r in res.results], axis=0)
```

```python
from contextlib import ExitStack
import math
import numpy as np
import concourse.bass as bass
import concourse.mybir as mybir
from concourse.bass_utils import run_bass_kernel_spmd

F32 = mybir.dt.float32
BF16 = mybir.dt.bfloat16
I32 = mybir.dt.int32
AF = mybir.ActivationFunctionType
ALU = mybir.AluOpType

ENGS = ("pe", "act", "dve", "pool", "sp")
EPOCH = 30000

S = 2048
D = 2048
NCH = 16
NTB = 16
NSEQ = 2
DEPTH = 4
EPS = 1e-5
DN_ALPHA = (2.0 * DEPTH) ** 0.25
QSCALE = 128 ** -0.5
SBUF_BASE = 16512
SBUF_END = 229376


class Tk:
    __slots__ = ("key", "val")

    def __init__(self, key, val):
        self.key = key
        self.val = val


class Prog:
    def __init__(self, nc):
        self.nc = nc
        self.q = {e: [] for e in ENGS}
        self.cnt = {}
        self.seen = {}
        self.epoch = {e: 0 for e in ENGS}

    def wait(self, eng, *tickets):
        for t in tickets:
            if t is None:
                continue
            if isinstance(t, (list, tuple)):
                self.wait(eng, *t)
                continue
            if self.seen.get((eng, t.key), 0) >= t.val:
                continue
            self.seen[(eng, t.key)] = t.val
            self.q[eng].append(("wait", t.key, t.val))

    def op(self, eng, fn, inc=True, key=None, amount=1):
        if not inc:
            self.q[eng].append(("op", fn, None, 0))
            return None
        if key is None:
            key = (eng, self.epoch[eng])
            if self.cnt.get(key, 0) >= EPOCH:
                self.epoch[eng] += 1
                key = (eng, self.epoch[eng])
        self.cnt[key] = self.cnt.get(key, 0) + amount
        self.q[eng].append(("op", fn, key, amount))
        return Tk(key, self.cnt[key])

    def barrier(self, exclude=("wsl",)):
        tks = []
        for key, c in self.cnt.items():
            if key[0] == "dma" and str(key[1]).startswith(exclude):
                continue
            tks.append(Tk(key, c))
        for e in ENGS:
            self.wait(e, tks)

    def emit(self):
        nc = self.nc
        with ExitStack() as es:
            semh = {}
            for i, k in enumerate(self.cnt):
                semh[k] = es.enter_context(nc.semaphore("s%d" % i))
            blk = es.enter_context(nc.Block())

            def mk(eng):
                items = self.q[eng]

                def run(e):
                    for it in items:
                        if it[0] == "wait":
                            e.wait_ge(semh[it[1]], it[2])
                        else:
                            ins = it[1](e)
                            if it[2] is not None:
                                ins.then_inc(semh[it[2]], it[3])
                return run

            blk.tensor(mk("pe"))
            blk.scalar(mk("act"))
            blk.vector(mk("dve"))
            blk.gpsimd(mk("pool"))
            blk.sync(mk("sp"))


class Buf:
    _n = 0

    def __init__(self, t=None):
        self.t = t
        self.wt = None
        self.rts = {}
        Buf._n += 1
        self.id = Buf._n


class Ring:
    def __init__(self, bufs):
        self.bufs = bufs
        self.i = 0

    def next(self):
        b = self.bufs[self.i % len(self.bufs)]
        self.i += 1
        return b


class KB:
    def __init__(self, nc):
        self.nc = nc
        self.P = Prog(nc)
        self.off = SBUF_BASE
        self.nname = 0

    def sb(self, shape, dt, at=None):
        el = 2 if dt == BF16 else 4
        nb = int(np.prod(shape[1:])) * el
        nb = (nb + 63) // 64 * 64
        if at is None:
            at = self.off
            self.off += nb
            assert self.off <= SBUF_END, ("SBUF overflow", self.off)
        self.nname += 1
        return self.nc.alloc_sbuf_tensor_at("t%d" % self.nname, list(shape), dt, offset=at)

    def sbuf(self, shape, dt):
        return Buf(self.sb(shape, dt))

    def acquire(self, eng, reads=(), writes=()):
        P = self.P
        for b in reads:
            P.wait(eng, b.wt)
        for b in writes:
            P.wait(eng, b.wt)
            for k, v in b.rts.items():
                P.wait(eng, Tk(k, v))

    @staticmethod
    def release(t, reads=(), writes=()):
        for b in reads:
            b.rts[t.key] = max(b.rts.get(t.key, 0), t.val)
        for b in writes:
            b.wt = t
            b.rts = {}

    def do(self, eng, fn, reads=(), writes=(), key=None, amount=1):
        self.acquire(eng, reads, writes)
        t = self.P.op(eng, fn, key=key, amount=amount)
        self.release(t, reads, writes)
        return t

    def raw(self, eng, fn):
        self.P.op(eng, fn, inc=False)

    def dma(self, eng, out, in_, reads=(), writes=(), key=None):
        assert key is not None
        return self.do(eng, lambda e: e.dma_start(out=out, in_=in_), reads, writes, key=("dma", key), amount=16)

    def mm(self, out, lhsT, rhs, start=True, stop=True):
        return lambda e: e.matmul(out, lhsT=lhsT, rhs=rhs, start=start, stop=stop)

    def tr(self, out, in_, ident):
        return lambda e: e.transpose(out=out, in_=in_, identity=ident)

    def actf(self, out, in_, func, bias=None, scale=None, accum_out=None):
        kw = {}
        if bias is not None:
            kw["bias"] = bias
        if scale is not None:
            kw["scale"] = scale
        if accum_out is not None:
            kw["accum_out"] = accum_out
        return lambda e: e.activation(out=out, in_=in_, func=func, **kw)

    def tt(self, out, in0, in1, op):
        return lambda e: e.tensor_tensor(out=out, in0=in0, in1=in1, op=op)

    def ts(self, out, in0, s1, op0, s2=None, op1=None, accum_out=None):
        kw = {}
        if op1 is not None:
            kw["op1"] = op1
        if accum_out is not None:
            kw["accum_out"] = accum_out
        return lambda e: e.tensor_scalar(out=out, in0=in0, scalar1=s1, scalar2=s2, op0=op0, **kw)

    def stt(self, out, in0, scalar, in1, op0, op1, accum_out=None):
        kw = {}
        if accum_out is not None:
            kw["accum_out"] = accum_out
        return lambda e: e.scalar_tensor_tensor(out=out, in0=in0, scalar=scalar, in1=in1, op0=op0, op1=op1, **kw)

    def cp(self, out, in_):
        return lambda e: e.tensor_copy(out=out, in_=in_)

    def ms(self, ap, val):
        return lambda e: e.memset(ap, val)


def build_program(nlayers=DEPTH, stop=None):
    nc = bass.Bass("TRN2", target_bir_lowering=False)
    K = KB(nc)
    P = K.P

    def din(name, shape, dt=F32):
        return nc.dram_tensor(name, list(shape), dt, kind="ExternalInput").ap()

    def dscr(name, shape, dt):
        return nc.dram_tensor(name, list(shape), dt).ap()

    x_d = din("x", [NSEQ, S, D])
    cT_d = din("cT", [128, 2 * NCH])
    w_mod_d = din("w_mod", [DEPTH, D, 3 * D])
    b_mod_d = din("b_mod", [DEPTH, 3 * D])
    ln_g_d = din("ln_g", [DEPTH, D])
    ln_b_d = din("ln_b", [DEPTH, D])
    a_w_in_d = din("a_w_in", [2, D, 4 * D])
    a_w_out_d = din("a_w_out", [2, D, D])
    lam_d = [din(n, [2, 128]) for n in ("a_lam_q1", "a_lam_k1", "a_lam_q2", "a_lam_k2")]
    subln_d = din("a_subln_g", [2, 256])
    kv_w_mod_d = din("kv_w_mod", [D, 2 * D])
    kv_b_mod_d = din("kv_b_mod", [1, 2 * D])
    kv_w_d = din("kv_w", [D, 2 * D + 16])
    kv_b_f_d = din("kv_b_f", [16, 1])
    b_w_in_d = din("b_w_in", [2, D, 2 * D])
    b_w_out_d = din("b_w_out", [2, D, D])
    out_d = nc.dram_tensor("out", [NSEQ, S, D], F32, kind="ExternalOutput").ap()

    xs_d = dscr("xs", [NSEQ, S, D], F32)
    qT_d = dscr("qT", [NSEQ, D, S], BF16)
    kT_d = dscr("kT", [NSEQ, D, S], BF16)
    v_d = dscr("v", [NSEQ, S, D], BF16)
    g_d = dscr("g", [NSEQ, S, D], BF16)
    kTs_d = dscr("kTs", [NSEQ, D, S], BF16)
    vs_d = dscr("vs", [NSEQ, S, D], BF16)
    FT_d = dscr("FT", [NSEQ, 16, S], F32)
    gate_d = dscr("gate", [DEPTH, 2, D], F32)

    ps = nc.alloc_psum_tensor("ps", [128, 4096], F32)
    banks = [Buf(ps[:, i * 512:(i + 1) * 512]) for i in range(8)]
    for i, bk_ in enumerate(banks):
        bk_.bank = i
    ptb_ap = ps[:, 3584:4096].bitcast(BF16)

    ident = K.sb([128, 128], F32)
    identb = K.sb([128, 128], BF16)
    onesb = K.sb([128, 128], BF16)
    maskd = K.sb([128, 128], F32)
    biasA = K.sb([128, 8, 17], F32)
    colmod = K.sb([128, 5, 64], F32)
    cact = K.sb([128, 2 * NCH], F32)
    neglam = K.sb([128, 2], F32)
    gsub = K.sb([128, 2, 256], F32)
    nlam256 = K.sb([128, 2, 256], F32)
    negF = K.sb([128, NSEQ, 16, 16], F32)
    negbf = K.sb([16, 1], F32)
    onecol = K.sb([128, 1], F32)
    nhalf = K.sb([128, 1], F32)
    small = K.sb([128, 64], F32)
    HT = K.sb([128, NCH, S], BF16)
    HTB = [Buf() for _ in range(NTB)]
    W_BASE = K.off
    K.off += 65536
    H_BASE = K.off
    H_SIZE = SBUF_END - H_BASE
    assert H_SIZE >= 60 * 1024, H_SIZE

    def region(base, size):
        st = {"o": base}

        def take(shape, dt):
            el = 2 if dt == BF16 else 4
            nb = (int(np.prod(shape[1:])) * el + 63) // 64 * 64
            t = K.sb(shape, dt, at=st["o"])
            st["o"] += nb
            assert st["o"] <= base + size, ("region overflow", st["o"] - base, size)
            return t
        return take

    wtake = region(W_BASE, 65536)
    wslot = [Buf(wtake([128, NCH, 512], BF16)) for _ in range(4)]
    wtake2 = region(W_BASE, 65536)
    wf32 = [wtake2([128, NCH, 512], F32) for _ in range(2)]

    ctick = []

    def setup_consts():
        t = P.op("pool", K.ms(ident[:], 1.0))
        P.wait("pool", t)
        t = P.op("pool", lambda e: e.affine_select(out=ident[:], in_=ident[:], pattern=[[-1, 128]],
                                                   compare_op=ALU.is_equal, fill=0.0, base=0, channel_multiplier=1))
        P.wait("pool", t)
        ctick.append(P.op("pool", K.cp(identb[:], ident[:])))
        ctick.append(P.op("pool", K.ms(onesb[:], 1.0)))
        ctick.append(P.op("pool", K.ms(onecol[:], 1.0)))
        ctick.append(P.op("pool", K.ms(nhalf[:], -0.5)))
        t = P.op("pool", K.ms(maskd[:], 0.0))
        P.wait("pool", t)
        ctick.append(P.op("pool", lambda e: e.affine_select(out=maskd[:], in_=maskd[:], pattern=[[1, 128]],
                                                            compare_op=ALU.is_ge, fill=-30000.0, base=0,
                                                            channel_multiplier=-1)))
        io = K.sb([128, 17], I32, at=H_BASE)
        iof = K.sb([128, 17], F32, at=H_BASE + 128)
        t = P.op("pool", lambda e: e.iota(io[:], pattern=[[-128, 17]], base=-64, channel_multiplier=1))
        P.wait("pool", t)
        t = P.op("pool", K.cp(iof[:], io[:]))
        P.wait("pool", t)
        for h in range(8):
            slope = 2.0 ** (-8.0 * (h + 1) / 8)
            t = P.op("pool", K.ts(biasA[:, h, :], iof[:], slope, ALU.mult))
        ctick.append(t)
        craw = K.sb([128, 2 * NCH], F32, at=H_BASE + 256)
        bfr = K.sb([16, 1], F32, at=H_BASE + 512)
        P.op("sp", lambda e: e.dma_start(out=craw[:], in_=cT_d), key=("dma", "c0"), amount=16)
        P.op("sp", lambda e: e.dma_start(out=bfr[:], in_=kv_b_f_d), key=("dma", "c0"), amount=16)
        lt = [K.sb([128, 2, 128], F32, at=H_BASE + 1024 + i * 1024) for i in range(4)]
        sg = K.sb([128, 2, 256], F32, at=H_BASE + 1024 + 4096)
        junk = K.sb([128, 128], F32, at=H_BASE + 1024 + 4096 + 2048)
        ee = K.sb([128, 4], F32, at=H_BASE + 1024 + 4096 + 2048 + 512)
        tl = []
        for i in range(4):
            for l in range(2):
                tl.append(P.op("sp", lambda e, i=i, l=l: e.dma_start(out=lt[i][:, l, :], in_=lam_d[i][l:l + 1, :].partition_broadcast(128)),
                               key=("dma", "c0"), amount=16))
        for l in range(2):
            tl.append(P.op("sp", lambda e, l=l: e.dma_start(out=sg[:, l, :], in_=subln_d[l:l + 1, :].partition_broadcast(128)),
                           key=("dma", "c0"), amount=16))
        P.wait("dve", tl[-1])
        P.wait("act", tl[-1])
        ctick.append(P.op("act", K.actf(cact[:], craw[:], AF.Silu)))
        ctick.append(P.op("dve", K.ts(negbf[:], bfr[:], -1.0, ALU.mult)))
        t = None
        for l in range(2):
            for pr in range(2):
                if t is not None:
                    P.wait("dve", t)
                t = P.op("dve", K.stt(junk[:], lt[2 * pr][:, l, :], 1.0, lt[2 * pr + 1][:, l, :], ALU.mult, ALU.mult,
                                      accum_out=small[:, l * 2 + pr:l * 2 + pr + 1]))
        P.wait("act", t)
        t = P.op("act", K.actf(ee[:], small[:, 0:4], AF.Exp))
        P.wait("dve", t)
        for l in range(2):
            lam_init = 0.8 - 0.6 * math.exp(-0.3 * l)
            t = P.op("dve", K.tt(neglam[:, l:l + 1], ee[:, 2 * l + 1:2 * l + 2], ee[:, 2 * l:2 * l + 1], ALU.subtract))
            P.wait("dve", t)
            t = P.op("dve", K.ts(neglam[:, l:l + 1], neglam[:, l:l + 1], -lam_init, ALU.add))
            t = P.op("dve", K.ts(gsub[:, l, :], sg[:, l, :], 1.0 - lam_init, ALU.mult))
            P.wait("dve", t)
            t = P.op("dve", K.ms(nlam256[:, l, :], 1.0))
            P.wait("dve", t)
            t = P.op("dve", K.ts(nlam256[:, l, :], nlam256[:, l, :], neglam[:, l:l + 1], ALU.mult))
        ctick.append(t)
        for e in ENGS:
            P.wait(e, ctick)

    def mod_prologue():
        mods = [(w_mod_d[l], b_mod_d[l:l + 1, :], 3 * D) for l in range(DEPTH)] + [(kv_w_mod_d, kv_b_mod_d, 2 * D)]
        hb = region(H_BASE, H_SIZE)
        brow = [Buf(hb([2, 512], F32)) for _ in range(2)]
        mrow = [Buf(hb([2, 512], F32)) for _ in range(2)]
        wsl = [Buf(wf32[0]), Buf(wf32[1])]
        pm = [banks[0], banks[1]]
        pc = banks[2]
        colb = Buf()
        si = 0
        for i, (W, Bv, ncols) in enumerate(mods):
            for s in range(ncols // 512):
                n0 = s * 512
                ws = wsl[si % 2]
                br = brow[si % 2]
                mr = mrow[si % 2]
                pmb = pm[si % 2]
                K.acquire("sp", writes=[ws])
                for hh in range(2):
                    t = P.op("sp", lambda e, ws=ws, W=W, n0=n0, hh=hh: e.dma_start(
                        out=ws.t[:, hh * 8:(hh + 1) * 8, :],
                        in_=W[hh * 1024:(hh + 1) * 1024, n0:n0 + 512].rearrange("(c p) n -> p c n", p=128)),
                        key=("dma", "ws%d" % (si % 2)), amount=16)
                K.release(t, writes=[ws])
                K.dma("sp", br.t[:], Bv[:, n0:n0 + 512].partition_broadcast(2), writes=[br], key="br%d" % (si % 2))
                K.acquire("pe", reads=[ws], writes=[pmb])
                for ch in range(NCH):
                    t = P.op("pe", K.mm(pmb.t[0:2, :], cact[:, 2 * ch:2 * ch + 2], ws.t[:, ch, :], ch == 0, ch == NCH - 1),
                             inc=(ch == NCH - 1))
                K.release(t, reads=[ws], writes=[pmb])
                K.do("dve", K.tt(mr.t[:], pmb.t[0:2, :], br.t[:], ALU.add), reads=[pmb, br], writes=[mr])
                if n0 < 2 * D:
                    K.acquire("pe", reads=[mr], writes=([pc] if n0 == 0 else []))
                    for q in range(4):
                        ci = n0 // 128 + q
                        t = P.op("pe", K.mm(pc.t[:, ci * 2:ci * 2 + 2], mr.t[:, q * 128:(q + 1) * 128], ident[0:2, 0:2]),
                                 inc=(q == 3))
                    K.release(t, reads=[mr])
                    if n0 == 2 * D - 512:
                        K.release(t, writes=[pc])
                        K.do("dve", K.cp(colmod[:, i, 0:32], pc.t[:, 0:32]), reads=[pc], writes=[colb])
                        K.do("dve", K.ts(colmod[:, i, 32:64], pc.t[:, 32:64], 1.0, ALU.add), reads=[pc], writes=[colb])
                else:
                    K.dma("sp", gate_d[i, :, n0 - 2 * D:n0 - 2 * D + 512], mr.t[:], reads=[mr], key="gst%d" % (si % 2))
                si += 1
        for e in ENGS:
            P.wait(e, colb.wt)
        return [Tk(("dma", "gst0"), P.cnt.get(("dma", "gst0"), 0)), Tk(("dma", "gst1"), P.cnt.get(("dma", "gst1"), 0))]

    def phase1(modi, b, src):
        hb = region(H_BASE, H_SIZE)
        xr = Ring([Buf(hb([128, D], F32)) for _ in range(2)])
        tring = Ring([banks[6], banks[7]])
        for tb in range(NTB):
            xt = xr.next()
            K.dma("sp", xt.t[:], src[b, tb * 128:(tb + 1) * 128, :], writes=[xt], key="x%d" % (tb % 2))
            for g4 in range(4):
                bk = tring.next()
                K.acquire("pe", reads=[xt], writes=[bk])
                for k in range(4):
                    ch = g4 * 4 + k
                    t = P.op("pe", K.tr(bk.t[:, k * 128:(k + 1) * 128], xt.t[:, ch * 128:(ch + 1) * 128], ident[:]), inc=(k == 3))
                K.release(t, reads=[xt], writes=[bk])
                K.acquire("act", reads=[bk], writes=([HTB[tb]] if g4 == 0 else []))
                for k in range(4):
                    ch = g4 * 4 + k
                    t = P.op("act", K.actf(HT[:, ch, tb * 128:(tb + 1) * 128], bk.t[:, k * 128:(k + 1) * 128], AF.Identity,
                                           bias=colmod[:, modi, ch * 2 + b:ch * 2 + b + 1],
                                           scale=colmod[:, modi, 32 + ch * 2 + b:32 + ch * 2 + b + 1]), inc=(k == 3))
                K.release(t, reads=[bk])
            K.release(t, writes=[HTB[tb]])

    class Slabs:
        def __init__(self, nslots):
            self.slots = wslot[:nslots]
            self.i = 0

        def load(self, W, n0, ncols=512):
            s = self.slots[self.i % len(self.slots)]
            k = self.i % len(self.slots)
            self.i += 1
            K.acquire("pool", writes=[s])
            for hh in range(2):
                t = P.op("pool", lambda e, s=s, W=W, n0=n0, hh=hh, ncols=ncols: e.dma_start(
                    out=s.t[:, hh * 8:(hh + 1) * 8, 0:ncols],
                    in_=W[hh * 1024:(hh + 1) * 1024, n0:n0 + ncols].rearrange("(c p) n -> p c n", p=128)),
                    key=("dma", "wsl%d" % k), amount=16)
            K.release(t, writes=[s])
            return s

    def proj_feature_major(hb_stage, slab, n_f0, dst, b, func, scale, st):
        for fc in range(4):
            grp = st["grp"] % 2
            st["grp"] += 1
            bks = banks[grp * 4:grp * 4 + 4]
            K.acquire("pe", reads=[slab] + HTB, writes=bks)
            for ch in range(NCH):
                for tt in range(4):
                    t = P.op("pe", K.mm(bks[tt].t[:, :], slab.t[:, ch, fc * 128:(fc + 1) * 128], HT[:, ch, tt * 512:(tt + 1) * 512],
                                        ch == 0, ch == NCH - 1), inc=(ch == NCH - 1 and tt == 3))
            K.release(t, reads=[slab] + HTB, writes=bks)
            stg = hb_stage.next()
            K.acquire("act", writes=[stg])
            K.acquire("dve", writes=[stg])
            stg.wt = None
            stg.rts = {}
            parts = []
            for tt in range(4):
                use_act = (func is not None) or ((st["ev"] % 2) == 0)
                st["ev"] += 1
                o = stg.t[:, tt * 512:(tt + 1) * 512]
                if use_act:
                    af = func if func is not None else (AF.Identity if scale is not None else AF.Copy)
                    fn = K.actf(o, bks[tt].t[:, :], af, scale=scale)
                    tl = K.do("act", fn, reads=[bks[tt]])
                else:
                    if scale is not None:
                        fn = K.ts(o, bks[tt].t[:, :], scale, ALU.mult)
                    else:
                        fn = K.cp(o, bks[tt].t[:, :])
                    tl = K.do("dve", fn, reads=[bks[tt]])
                parts.append(tl)
            P.wait("sp", parts)
            f0 = n_f0 + fc * 128
            tstore = K.dma("sp", dst[b, f0:f0 + 128, :], stg.t[:], reads=[stg], key="fst%d" % stg.idx)
            st["stores"].append(tstore)

    def proj_token_major(hb_stage, slab, n0, dst, b, func, st):
        for tb in range(NTB):
            bk = banks[st["tbk"] % 8]
            st["tbk"] += 1
            K.acquire("pe", reads=[slab, HTB[tb]], writes=[bk])
            for ch in range(NCH):
                t = P.op("pe", K.mm(bk.t[:, :], HT[:, ch, tb * 128:(tb + 1) * 128], slab.t[:, ch, :], ch == 0, ch == NCH - 1),
                         inc=(ch == NCH - 1))
            K.release(t, reads=[slab, HTB[tb]], writes=[bk])
            stg = hb_stage.next()
            use_act = (func is not None) or ((st["ev"] % 2) == 0)
            st["ev"] += 1
            if use_act:
                K.do("act", K.actf(stg.t[:], bk.t[:, :], func if func is not None else AF.Copy), reads=[bk], writes=[stg])
            else:
                K.do("dve", K.cp(stg.t[:], bk.t[:, :]), reads=[bk], writes=[stg])
            tstore = K.dma("sp", dst[b, tb * 128:(tb + 1) * 128, n0:n0 + 512], stg.t[:], reads=[stg], key="tst%d" % stg.idx)
            st["stores"].append(tstore)

    def mk_stage(hb, n, shape):
        bufs = []
        for i in range(n):
            bb = Buf(hb(shape, BF16))
            bb.idx = i
            bb.parts = []
            bufs.append(bb)
        return Ring(bufs)

    def plan_of(kind, l):
        if kind == "A":
            return a_w_in_d[l], 16
        if kind == "B":
            return b_w_in_d[l - 2], 8
        return kv_w_d, 8

    def preload(kind, l):
        W, ns = plan_of(kind, l)
        slabs = Slabs(3)
        loaded = {}
        for s in range(min(3, ns)):
            loaded[s] = slabs.load(W, s * 512)
        return slabs, loaded

    def phase2(kind, l, b, hb, pre):
        st = {"grp": 0, "ev": 0, "tbk": 0, "stores": []}
        fstage = mk_stage(hb, 2, [128, S])
        tstage = mk_stage(hb, 4, [128, 512])
        slabs, loaded = pre
        if kind == "A":
            W = a_w_in_d[l]
            plan = [("f", qT_d, None, QSCALE)] * 4 + [("f", kT_d, None, None)] * 4 + [("t", v_d, None, None)] * 4 + [("t", g_d, AF.Silu, None)] * 4
        elif kind == "B":
            W = b_w_in_d[l - 2]
            plan = [("f", qT_d, None, QSCALE)] * 4 + [("f", g_d, AF.Silu, None)] * 4
        else:
            W = kv_w_d
            plan = [("f", kTs_d, None, None)] * 4 + [("t", vs_d, None, None)] * 4
        ns = len(plan)
        for s in range(ns):
            typ, dst, func, scale = plan[s]
            slab = loaded.pop(s)
            if typ == "f":
                proj_feature_major(fstage, slab, (s % 4) * 512, dst, b, func, scale, st)
            else:
                proj_token_major(tstage, slab, (s % 4) * 512, dst, b, func, st)
            if s + 3 < ns:
                loaded[s + 3] = slabs.load(W, (s + 3) * 512)
        return st["stores"]

    def phase3A(l, b, stores, hb):
        P.wait("sp", stores)
        sets = []
        for i in range(2):
            sets.append({"q": Buf(hb([128, 2, S], BF16)), "k": Buf(hb([128, 2, S], BF16)),
                         "v": Buf(hb([128, NTB, 257], BF16)), "i": i})
        for s_ in sets:
            K.do("pool", K.ms(s_["v"].t[:, :, 256:257], 1.0), writes=[s_["v"]])
        NG = 5
        gbufs = [Buf(hb([128, 256], BF16)) for _ in range(NG)]
        for i, g_ in enumerate(gbufs):
            g_.idx = i
        gring = Ring(gbufs)
        ggr = Ring([Buf(hb([128, 256], F32)) for _ in range(4)])
        ptring = Ring([Buf(hb([128, 2, 128], BF16)) for _ in range(6)])
        tmpr = Ring([Buf(hb([128, 256], F32)) for _ in range(2)])
        orr = Ring([Buf(hb([128, 256], F32)) for _ in range(4)])
        ofr = Ring([Buf(hb([128, 256], BF16)) for _ in range(4)])
        smr = Ring([Buf(hb([128, 8], F32)) for _ in range(8)])
        sslots = []
        for bi in range(3):
            sb_ = banks[bi]
            sb_.v = sb_.t[:, 0:256].rearrange("p (m q) -> p m q", m=2)
            sslots.append(sb_)
        sring = Ring(sslots)
        osets = [(banks[3], banks[4]), (banks[5], banks[6])]
        banks[7].v = ptb_ap[:, 0:256].rearrange("p (e t) -> p e t", e=2)
        pslot = banks[7]

        def load_head(h):
            s_ = sets[h % 2]
            K.dma("sp", s_["q"].t[:], qT_d[b, h * 256:(h + 1) * 256, :].rearrange("(m p) t -> p m t", p=128),
                  writes=[s_["q"]], key="hq%d" % s_["i"])
            K.dma("sp", s_["k"].t[:], kT_d[b, h * 256:(h + 1) * 256, :].rearrange("(m p) t -> p m t", p=128),
                  writes=[s_["k"]], key="hk%d" % s_["i"])
            K.acquire("sp", writes=[s_["v"]])
            for hh in range(2):
                t = P.op("sp", lambda e, s_=s_, hh=hh: e.dma_start(
                    out=s_["v"].t[:, hh * 8:(hh + 1) * 8, 0:256],
                    in_=v_d[b, hh * 1024:(hh + 1) * 1024, h * 256:(h + 1) * 256].rearrange("(kb p) e -> p kb e", p=128)),
                    key=("dma", "hv%d" % s_["i"]), amount=16)
            K.release(t, writes=[s_["v"]])

        steps = [(h, qb, j) for h in range(8) for qb in range(NTB) for j in range(qb + 1)]
        pend = {}
        first_w = set()
        state = {"oset": 0, "fid": 0}
        gts = {}
        dq = []
        FMAX = 3

        def load_g(h, qb):
            gt = gring.next()
            K.dma("sp", gt.t[:], g_d[b, qb * 128:(qb + 1) * 128, h * 256:(h + 1) * 256], writes=[gt], key="g%d" % gt.idx)
            gts[(h, qb)] = gt

        def run_due(idx, force_upto=-1):
            i = 0
            while i < len(dq):
                due, fid, fn = dq[i]
                if due <= idx or fid <= force_upto:
                    dq.pop(i)
                    fn(idx)
                    i = 0
                else:
                    i += 1

        def qk(idx):
            h, qb, j = steps[idx]
            s_ = sets[h % 2]
            sl = sring.next()
            K.acquire("pe", reads=[s_["q"], s_["k"]], writes=[sl])
            for m in range(2):
                t = P.op("pe", K.mm(sl.v[:, m, :], s_["k"].t[:, m, j * 128:(j + 1) * 128], s_["q"].t[:, m, qb * 128:(qb + 1) * 128]),
                         inc=(m == 1))
            K.release(t, reads=[s_["q"], s_["k"]], writes=[sl])
            pt = ptring.next()
            K.do("act", K.actf(pt.t[:], sl.v, AF.Exp, bias=biasA[:, h, qb - j:qb - j + 1], scale=1.0), reads=[sl], writes=[pt])
            if j == qb:
                K.do("pool", lambda e, pt=pt: e.affine_select(out=pt.t[:], in_=pt.t[:], pattern=[[0, 2], [1, 128]],
                                                              compare_op=ALU.is_ge, fill=0.0, base=0, channel_multiplier=-1),
                     reads=[], writes=[pt])
            pend[idx] = pt

        def pv(idx, cur):
            h, qb, j = steps[idx]
            s_ = sets[h % 2]
            pt = pend.pop(idx)
            if j == 0:
                state["oset"] ^= 1
            ob = osets[state["oset"]]
            K.acquire("pe", reads=[pt, s_["v"]], writes=(list(ob) if j == 0 else []))
            for m in range(2):
                t = P.op("pe", K.mm(ob[m].t[:, 0:257], pt.t[:, m, :], s_["v"].t[:, j, :], j == 0, j == qb), inc=(m == 1))
            K.release(t, reads=[pt, s_["v"]])
            if j == qb:
                K.release(t, writes=list(ob))
                finalize(h, qb, ob, cur)

        def finalize(h, qb, ob, cur):
            fid = state["fid"]
            state["fid"] += 1
            run_due(cur, force_upto=fid - FMAX)
            sm = smr.next()
            tmp = tmpr.next()
            o = orr.next()
            gg = ggr.next()
            gt = gts.pop((h, qb))
            nxt = (h, qb + 1) if qb + 1 < NTB else ((h + 1, 0) if h + 1 < 8 else None)
            if nxt is not None:
                load_g(*nxt)
            lcol = ps[:, ob[0].bank * 512 + 256:ob[1].bank * 512 + 257:512]
            K.do("dve", lambda e: e.reciprocal(out=sm.t[:, 0:2], in_=lcol), reads=[ob[0], ob[1]], writes=[sm])
            K.do("dve", K.stt(tmp.t[:], ob[1].t[:, 0:256], sm.t[:, 1:2], nlam256[:, l, :], ALU.mult, ALU.mult),
                 reads=[ob[1], sm], writes=[tmp])
            K.do("dve", K.stt(o.t[:], ob[0].t[:, 0:256], sm.t[:, 0:1], tmp.t[:], ALU.mult, ALU.add), reads=[ob[0], tmp, sm], writes=[o])
            K.do("dve", K.stt(tmp.t[:], o.t[:], 1.0, o.t[:], ALU.mult, ALU.mult, accum_out=sm.t[:, 3:4]), reads=[o], writes=[tmp, sm])
            K.do("dve", K.ts(sm.t[:, 4:5], sm.t[:, 3:4], 1.0 / 256, ALU.mult, EPS, ALU.add), reads=[sm], writes=[sm])
            K.do("pool", K.tt(sm.t[:, 5:6], sm.t[:, 4:5], nhalf[:], ALU.pow), reads=[sm], writes=[sm])
            K.do("pool", K.tt(gg.t[:], gt.t[:], gsub[:, l, :], ALU.mult), reads=[gt], writes=[gg])

            def stage1b(cur2):
                of = ofr.next()
                K.do("dve", K.stt(of.t[:], o.t[:], sm.t[:, 5:6], gg.t[:], ALU.mult, ALU.mult), reads=[o, sm, gg], writes=[of])

                def stage2(cur3):
                    K.acquire("pe", reads=[of], writes=[pslot])
                    for e2 in range(2):
                        t = P.op("pe", K.tr(pslot.v[:, e2, :], of.t[:, e2 * 128:(e2 + 1) * 128], identb[:]), inc=(e2 == 1))
                    K.release(t, reads=[of], writes=[pslot])
                    wr = [HTB[qb]] if (qb not in first_w) else []
                    first_w.add(qb)
                    K.acquire("dve", reads=[pslot], writes=wr)
                    t = P.op("dve", K.cp(HT[:, 2 * h:2 * h + 2, qb * 128:(qb + 1) * 128], pslot.v))
                    K.release(t, reads=[pslot], writes=[HTB[qb]])

                dq.append([cur2 + 2, fid, stage2])

            dq.append([cur + 2, fid, stage1b])

        LA = 3
        n = len(steps)
        for idx in range(n + LA):
            if idx == 0:
                load_head(0)
                load_head(1)
                load_g(0, 0)
            if idx < n:
                qk(idx)
            if idx >= LA:
                pv(idx - LA, idx)
                h, qb, j = steps[idx - LA]
                if qb == 0 and j == 0 and h >= 1 and h + 1 < 8:
                    load_head(h + 1)
            run_due(idx)
        while dq:
            run_due(0, force_upto=10 ** 9)

    def phase3B(l, b, stores, hb):
        P.wait("sp", stores)
        sets = []
        for i in range(2):
            sets.append({"q": Buf(hb([128, S], BF16)), "g": Buf(hb([128, S], BF16)), "k": Buf(hb([128, S], BF16)),
                         "v": Buf(hb([128, NTB, 128], BF16)), "f": Buf(hb([128, S], F32)), "i": i})
        Tr = Ring([Buf(hb([128, 512], F32)) for _ in range(3)])
        ptring = Ring([Buf(hb([128, 512], BF16)) for _ in range(4)])
        rcr = Ring([Buf(hb([128, 512], F32)) for _ in range(2)])
        onr = Ring([Buf(hb([128, 512], F32)) for _ in range(2)])
        sring = Ring([banks[0], banks[1], banks[2], banks[7]])
        osets = [(banks[3], banks[5]), (banks[4], banks[6])]

        def load_head(h):
            s_ = sets[h % 2]
            i = s_["i"]
            K.dma("sp", s_["q"].t[:], qT_d[b, h * 128:(h + 1) * 128, :], writes=[s_["q"]], key="hq%d" % i)
            K.dma("sp", s_["k"].t[:], kTs_d[b, h * 128:(h + 1) * 128, :], writes=[s_["k"]], key="hk%d" % i)
            K.acquire("sp", writes=[s_["v"]])
            for hh in range(2):
                t = P.op("sp", lambda e, s_=s_, hh=hh: e.dma_start(
                    out=s_["v"].t[:, hh * 8:(hh + 1) * 8, :],
                    in_=vs_d[b, hh * 1024:(hh + 1) * 1024, h * 128:(h + 1) * 128].rearrange("(kb p) e -> p kb e", p=128)),
                    key=("dma", "hv%d" % i), amount=16)
            K.release(t, writes=[s_["v"]])
            K.dma("sp", s_["f"].t[:], FT_d[b, h:h + 1, :].partition_broadcast(128), writes=[s_["f"]], key="hf%d" % i)
            K.dma("sp", s_["g"].t[:], g_d[b, h * 128:(h + 1) * 128, :], writes=[s_["g"]], key="hg%d" % i)

        steps = [(h, qt, j) for h in range(16) for qt in range(4) for j in range(4 * qt + 4)]
        pend = {}
        first_w = set()
        state = {"oset": 0}

        def qk(idx):
            h, qt, j = steps[idx]
            s_ = sets[h % 2]
            jl = j - 4 * qt
            qlo = 128 * jl if jl > 0 else 0
            sl = sring.next()
            K.acquire("pe", reads=[s_["q"], s_["k"]], writes=[sl])
            t = P.op("pe", K.mm(sl.t[:, qlo:512], s_["k"].t[:, j * 128:(j + 1) * 128], s_["q"].t[:, qt * 512 + qlo:(qt + 1) * 512]))
            K.release(t, reads=[s_["q"], s_["k"]], writes=[sl])
            T = Tr.next()
            K.do("dve", K.stt(T.t[:, qlo:512], sl.t[:, qlo:512], negF[:, b, j, h:h + 1], s_["f"].t[:, qt * 512 + qlo:(qt + 1) * 512],
                              ALU.add, ALU.add), reads=[sl, s_["f"]], writes=[T])
            if jl >= 0:
                K.do("pool", K.tt(T.t[:, qlo:qlo + 128], T.t[:, qlo:qlo + 128], maskd[:], ALU.add), reads=[T], writes=[T])
            pt = ptring.next()
            K.do("act", K.actf(pt.t[:, qlo:512], T.t[:, qlo:512], AF.Exp), reads=[T], writes=[pt])
            pend[idx] = (pt, qlo)

        def pv(idx):
            h, qt, j = steps[idx]
            s_ = sets[h % 2]
            pt, qlo = pend.pop(idx)
            last = (j == 4 * qt + 3)
            if j == 0:
                state["oset"] ^= 1
            ob, lb = osets[state["oset"]]
            K.acquire("pe", reads=[pt, s_["v"]], writes=([ob, lb] if j == 0 else []))
            P.op("pe", K.mm(ob.t[:, qlo:512], s_["v"].t[:, j, :], pt.t[:, qlo:512], j == 0, last), inc=False)
            t = P.op("pe", K.mm(lb.t[:, qlo:512], onesb[:], pt.t[:, qlo:512], j == 0, last))
            K.release(t, reads=[pt, s_["v"]])
            if last:
                K.release(t, writes=[ob, lb])
                rc = rcr.next()
                on = onr.next()
                K.do("dve", lambda e: e.reciprocal(out=rc.t[:], in_=lb.t[:, :]), reads=[lb], writes=[rc])
                K.do("dve", K.tt(on.t[:], ob.t[:, :], rc.t[:], ALU.mult), reads=[ob, rc], writes=[on])
                tbs = [HTB[qt * 4 + k] for k in range(4)]
                wr = [x_ for x_ in tbs if x_.id not in first_w]
                for x_ in tbs:
                    first_w.add(x_.id)
                K.acquire("pool", reads=[on, s_["g"]], writes=wr)
                t = P.op("pool", K.tt(HT[:, h, qt * 512:(qt + 1) * 512], on.t[:], s_["g"].t[:, qt * 512:(qt + 1) * 512], ALU.mult))
                K.release(t, reads=[on, s_["g"]], writes=tbs)

        LA = 3
        n = len(steps)
        for idx in range(n + LA):
            if idx == 0:
                load_head(0)
                load_head(1)
            if idx < n:
                qk(idx)
            if idx >= LA:
                pv(idx - LA)
                h, qt, j = steps[idx - LA]
                if qt == 0 and j == 0 and h >= 1 and h + 1 < 16:
                    load_head(h + 1)

    def prefetch_wout(W):
        for n in range(4):
            s = wslot[n]
            K.acquire("pool", writes=[s])
            for hh in range(2):
                t = P.op("pool", lambda e, s=s, W=W, n=n, hh=hh: e.dma_start(
                    out=s.t[:, hh * 8:(hh + 1) * 8, :],
                    in_=W[hh * 1024:(hh + 1) * 1024, n * 512:(n + 1) * 512].rearrange("(c p) n -> p c n", p=128)),
                    key=("dma", "wsl%d" % n), amount=16)
            K.release(t, writes=[s])

    def phase4(l, b, src, dst, gate_t, hb):
        lng = Buf(hb([128, D], F32))
        lnb = Buf(hb([128, D], F32))
        gtb = Buf(hb([128, D], F32))
        xr = Ring([Buf(hb([128, D], F32)) for _ in range(2)])
        zr = Ring([Buf(hb([128, D], F32)) for _ in range(2)])
        smr = Ring([Buf(hb([128, 8], F32)) for _ in range(2)])
        P.wait("sp", gate_t)
        K.dma("sp", lng.t[:], ln_g_d[l:l + 1, :].partition_broadcast(128), writes=[lng], key="lng")
        K.dma("sp", lnb.t[:], ln_b_d[l:l + 1, :].partition_broadcast(128), writes=[lnb], key="lnb")
        K.dma("sp", gtb.t[:], gate_d[l, b:b + 1, :].partition_broadcast(128), writes=[gtb], key="gtb")
        stores = []
        xts = {}

        def ldx(tb):
            xt = xr.next()
            K.dma("sp", xt.t[:], src[b, tb * 128:(tb + 1) * 128, :], writes=[xt], key="x%d" % (tb % 2))
            xts[tb] = xt

        ldx(0)
        for tb in range(NTB):
            if tb + 1 < NTB:
                ldx(tb + 1)
            xt = xts.pop(tb)
            bks = banks[(tb % 2) * 4:(tb % 2) * 4 + 4]
            K.acquire("pe", reads=wslot + [HTB[tb]], writes=bks)
            for fc in range(NCH):
                for n in range(4):
                    t = P.op("pe", K.mm(bks[n].t[:, :], HT[:, fc, tb * 128:(tb + 1) * 128], wslot[n].t[:, fc, :], fc == 0, fc == NCH - 1),
                             inc=(fc == NCH - 1 and n == 3))
            K.release(t, reads=wslot + [HTB[tb]], writes=bks)
            z = zr.next()
            sm = smr.next()
            K.acquire("dve", writes=[z])
            for n in range(4):
                t = K.do("dve", K.tt(z.t[:, n * 512:(n + 1) * 512], bks[n].t[:, :], gtb.t[:, n * 512:(n + 1) * 512], ALU.mult),
                         reads=[bks[n], gtb])
            K.release(t, writes=[z])
            K.do("dve", K.stt(z.t[:], xt.t[:], DN_ALPHA, z.t[:], ALU.mult, ALU.add, accum_out=sm.t[:, 0:1]),
                 reads=[xt, z], writes=[z, sm])
            K.do("act", K.actf(xt.t[:], z.t[:], AF.Square, accum_out=sm.t[:, 1:2]), reads=[z], writes=[xt, sm])
            K.do("dve", K.ts(sm.t[:, 2:3], sm.t[:, 0:1], 1.0 / D, ALU.mult), reads=[sm], writes=[sm])
            K.do("dve", K.tt(sm.t[:, 3:4], sm.t[:, 2:3], sm.t[:, 2:3], ALU.mult), reads=[sm], writes=[sm])
            K.do("dve", K.stt(sm.t[:, 4:5], sm.t[:, 1:2], 1.0 / D, sm.t[:, 3:4], ALU.mult, ALU.subtract), reads=[sm], writes=[sm])
            K.do("dve", K.ts(sm.t[:, 7:8], sm.t[:, 4:5], EPS, ALU.add), reads=[sm], writes=[sm])
            K.do("pool", K.tt(sm.t[:, 5:6], sm.t[:, 7:8], nhalf[:], ALU.pow), reads=[sm], writes=[sm])
            K.do("dve", K.stt(sm.t[:, 6:7], sm.t[:, 2:3], -1.0, sm.t[:, 5:6], ALU.mult, ALU.mult), reads=[sm], writes=[sm])
            K.do("act", K.actf(z.t[:], z.t[:], AF.Identity, bias=sm.t[:, 6:7], scale=sm.t[:, 5:6]), reads=[sm, z], writes=[z])
            K.do("pool", K.tt(z.t[:], z.t[:], lng.t[:], ALU.mult), reads=[z, lng], writes=[z])
            K.do("pool", K.tt(z.t[:], z.t[:], lnb.t[:], ALU.add), reads=[z, lnb], writes=[z])
            stores.append(K.dma("sp", dst[b, tb * 128:(tb + 1) * 128, :], z.t[:], reads=[z], key="zst%d" % (tb % 2)))
        return stores

    def kv_extras(b, hb):
        wz = Buf(hb([128, NCH, 16], BF16))
        e1 = Buf(hb([16, S], F32))
        sp_ = Buf(hb([16, S], F32))
        fpos = Buf(hb([16, S], F32))
        fneg = Buf(hb([16, S], F32))
        K.dma("pool", wz.t[:], kv_w_d[:, 2 * D:2 * D + 16].rearrange("(c p) n -> p c n", p=128), writes=[wz], key="wz")
        bks = banks[0:4]
        K.acquire("pe", reads=[wz] + HTB, writes=bks)
        for ch in range(NCH):
            for tt in range(4):
                t = P.op("pe", K.mm(bks[tt].t[0:16, :], wz.t[:, ch, :], HT[:, ch, tt * 512:(tt + 1) * 512], ch == 0, ch == NCH - 1),
                         inc=(ch == NCH - 1 and tt == 3))
        K.release(t, reads=[wz] + HTB, writes=bks)
        for tt in range(4):
            K.do("act", K.actf(e1.t[:, tt * 512:(tt + 1) * 512], bks[tt].t[0:16, :], AF.Exp, bias=negbf[:], scale=-1.0),
                 reads=[bks[tt]], writes=([e1] if tt == 0 else []))
        K.do("act", K.actf(sp_.t[:], e1.t[:], AF.Ln, bias=onecol[0:16, :], scale=1.0), reads=[e1], writes=[sp_])
        K.do("dve", lambda e: e.tensor_tensor_scan(out=fpos.t[:], data0=sp_.t[:], data1=sp_.t[:], initial=0.0,
                                                   op0=ALU.add, op1=ALU.bypass), reads=[sp_], writes=[fpos])
        K.do("dve", K.ts(fneg.t[:], fpos.t[:], -1.0, ALU.mult), reads=[fpos], writes=[fneg])
        tst = K.dma("sp", FT_d[b], fneg.t[:], reads=[fneg], key="ftst")
        bk = banks[4]
        K.acquire("pe", reads=[fpos], writes=[bk])
        for kb in range(NTB):
            t = P.op("pe", K.tr(bk.t[:, kb * 16:(kb + 1) * 16], fpos.t[:, kb * 128:(kb + 1) * 128], ident[0:16, 0:16]), inc=(kb == NTB - 1))
        K.release(t, reads=[fpos], writes=[bk])
        nb = Buf()
        K.do("dve", K.cp(negF[:, b, :, :], bk.t[:, 0:256].rearrange("p (k h) -> p k h", k=16)), reads=[bk], writes=[nb])
        for e in ENGS:
            P.wait(e, nb.wt)
        return [tst]

    class _Stop(Exception):
        pass

    stage = {"n": 0}

    def chk():
        stage["n"] += 1
        if stop is not None and stage["n"] >= stop:
            raise _Stop()

    def driver():
        setup_consts()
        chk()
        gate_t = mod_prologue()
        chk()
        kvstores = {}
        for l in range(nlayers):
            src = x_d if l == 0 else xs_d
            dst = out_d if l == nlayers - 1 else xs_d
            for b in range(NSEQ):
                if l == 2:
                    pre = preload("KV", l)
                    P.barrier()
                    phase1(4, b, xs_d)
                    P.barrier()
                    hb = region(H_BASE, H_SIZE)
                    st = phase2("KV", l, b, hb, pre)
                    st += kv_extras(b, hb)
                    kvstores[b] = st
                kind = "A" if l < 2 else "B"
                pre = preload(kind, l)
                P.barrier()
                phase1(l, b, src)
                chk()
                P.barrier()
                stores = phase2(kind, l, b, region(H_BASE, H_SIZE), pre)
                chk()
                prefetch_wout(a_w_out_d[l] if l < 2 else b_w_out_d[l - 2])
                P.barrier()
                if l < 2:
                    phase3A(l, b, stores, region(H_BASE, H_SIZE))
                else:
                    phase3B(l, b, stores + kvstores[b], region(H_BASE, H_SIZE))
                chk()
                P.barrier()
                fin = phase4(l, b, src, dst, gate_t, region(H_BASE, H_SIZE))
                P.wait("sp", fin)
                chk()

    try:
        driver()
    except _Stop:
        pass
    P.barrier(exclude=("__none__",))
    P.emit()
    return nc


_NC_CACHE = {}


def kernel(**inputs):
    n = 8
    if "nc" not in _NC_CACHE:
        _NC_CACHE["nc"] = build_program()
    nc = _NC_CACHE["nc"]
    f = lambda a: np.ascontiguousarray(np.asarray(a, dtype=np.float32))
    x = f(inputs["x"])
    c = f(inputs["c"])
    shared = {k: f(inputs[k]) for k in ("w_mod", "b_mod", "ln_g", "ln_b", "a_w_in", "a_w_out", "a_lam_q1", "a_lam_k1",
                                        "a_lam_q2", "a_lam_k2", "a_subln_g", "kv_w_mod", "kv_w", "b_w_in", "b_w_out")}
    shared["kv_b_mod"] = f(inputs["kv_b_mod"]).reshape(1, -1)
    shared["kv_b_f"] = f(inputs["kv_b_f"]).reshape(16, 1)
    in_maps = []
    for i in range(n):
        m = dict(shared)
        m["x"] = np.ascontiguousarray(x[2 * i:2 * i + 2])
        cc = c[2 * i:2 * i + 2]
        m["cT"] = np.ascontiguousarray(cc.reshape(2, NCH, 128).transpose(2, 1, 0).reshape(128, 2 * NCH))
        in_maps.append(m)
    res = run_bass_kernel_spmd(nc, in_maps, core_ids=list(range(n)))
    return np.concatenate([r["out"] for r in res.results], axis=0)
```

```python
from contextlib import ExitStack
import math
import numpy as np
import concourse.bass as bass
import concourse.mybir as mybir
from concourse.bass_utils import run_bass_kernel_spmd

F32 = mybir.dt.float32
BF16 = mybir.dt.bfloat16
I32 = mybir.dt.int32
AF = mybir.ActivationFunctionType
ALU = mybir.AluOpType

ENGS = ("pe", "act", "dve", "pool", "sp")
EPOCH = 30000

S = 2048
D = 2048
NCH = 16
NTB = 16
NSEQ = 2
DEPTH = 4
EPS = 1e-5
DN_ALPHA = (2.0 * DEPTH) ** 0.25
QSCALE = 128 ** -0.5
SBUF_BASE = 16512
SBUF_END = 229376


class Tk:
    __slots__ = ("key", "val")

    def __init__(self, key, val):
        self.key = key
        self.val = val


class Prog:
    def __init__(self, nc):
        self.nc = nc
        self.q = {e: [] for e in ENGS}
        self.cnt = {}
        self.seen = {}
        self.epoch = {e: 0 for e in ENGS}

    def wait(self, eng, *tickets):
        for t in tickets:
            if t is None:
                continue
            if isinstance(t, (list, tuple)):
                self.wait(eng, *t)
                continue
            if self.seen.get((eng, t.key), 0) >= t.val:
                continue
            self.seen[(eng, t.key)] = t.val
            self.q[eng].append(("wait", t.key, t.val))

    def op(self, eng, fn, inc=True, key=None, amount=1):
        if not inc:
            self.q[eng].append(("op", fn, None, 0))
            return None
        if key is None:
            key = (eng, self.epoch[eng])
            if self.cnt.get(key, 0) >= EPOCH:
                self.epoch[eng] += 1
                key = (eng, self.epoch[eng])
        self.cnt[key] = self.cnt.get(key, 0) + amount
        self.q[eng].append(("op", fn, key, amount))
        return Tk(key, self.cnt[key])

    def barrier(self, exclude=("wsl",)):
        tks = []
        for key, c in self.cnt.items():
            if key[0] == "dma" and str(key[1]).startswith(exclude):
                continue
            tks.append(Tk(key, c))
        for e in ENGS:
            self.wait(e, tks)

    def emit(self):
        nc = self.nc
        with ExitStack() as es:
            semh = {}
            for i, k in enumerate(self.cnt):
                semh[k] = es.enter_context(nc.semaphore("s%d" % i))
            blk = es.enter_context(nc.Block())

            def mk(eng):
                items = self.q[eng]

                def run(e):
                    for it in items:
                        if it[0] == "wait":
                            e.wait_ge(semh[it[1]], it[2])
                        else:
                            ins = it[1](e)
                            if it[2] is not None:
                                ins.then_inc(semh[it[2]], it[3])
                return run

            blk.tensor(mk("pe"))
            blk.scalar(mk("act"))
            blk.vector(mk("dve"))
            blk.gpsimd(mk("pool"))
            blk.sync(mk("sp"))


class Buf:
    _n = 0

    def __init__(self, t=None):
        self.t = t
        self.wt = None
        self.rts = {}
        Buf._n += 1
        self.id = Buf._n


class Ring:
    def __init__(self, bufs):
        self.bufs = bufs
        self.i = 0

    def next(self):
        b = self.bufs[self.i % len(self.bufs)]
        self.i += 1
        return b


class KB:
    def __init__(self, nc):
        self.nc = nc
        self.P = Prog(nc)
        self.off = SBUF_BASE
        self.nname = 0

    def sb(self, shape, dt, at=None):
        el = 2 if dt == BF16 else 4
        nb = int(np.prod(shape[1:])) * el
        nb = (nb + 63) // 64 * 64
        if at is None:
            at = self.off
            self.off += nb
            assert self.off <= SBUF_END, ("SBUF overflow", self.off)
        self.nname += 1
        return self.nc.alloc_sbuf_tensor_at("t%d" % self.nname, list(shape), dt, offset=at)

    def sbuf(self, shape, dt):
        return Buf(self.sb(shape, dt))

    def acquire(self, eng, reads=(), writes=()):
        P = self.P
        for b in reads:
            P.wait(eng, b.wt)
        for b in writes:
            P.wait(eng, b.wt)
            for k, v in b.rts.items():
                P.wait(eng, Tk(k, v))

    @staticmethod
    def release(t, reads=(), writes=()):
        for b in reads:
            b.rts[t.key] = max(b.rts.get(t.key, 0), t.val)
        for b in writes:
            b.wt = t
            b.rts = {}

    def do(self, eng, fn, reads=(), writes=(), key=None, amount=1):
        self.acquire(eng, reads, writes)
        t = self.P.op(eng, fn, key=key, amount=amount)
        self.release(t, reads, writes)
        return t

    def raw(self, eng, fn):
        self.P.op(eng, fn, inc=False)

    def dma(self, eng, out, in_, reads=(), writes=(), key=None):
        assert key is not None
        return self.do(eng, lambda e: e.dma_start(out=out, in_=in_), reads, writes, key=("dma", key), amount=16)

    def mm(self, out, lhsT, rhs, start=True, stop=True):
        return lambda e: e.matmul(out, lhsT=lhsT, rhs=rhs, start=start, stop=stop)

    def tr(self, out, in_, ident):
        return lambda e: e.transpose(out=out, in_=in_, identity=ident)

    def actf(self, out, in_, func, bias=None, scale=None, accum_out=None):
        kw = {}
        if bias is not None:
            kw["bias"] = bias
        if scale is not None:
            kw["scale"] = scale
        if accum_out is not None:
            kw["accum_out"] = accum_out
        return lambda e: e.activation(out=out, in_=in_, func=func, **kw)

    def tt(self, out, in0, in1, op):
        return lambda e: e.tensor_tensor(out=out, in0=in0, in1=in1, op=op)

    def ts(self, out, in0, s1, op0, s2=None, op1=None, accum_out=None):
        kw = {}
        if op1 is not None:
            kw["op1"] = op1
        if accum_out is not None:
            kw["accum_out"] = accum_out
        return lambda e: e.tensor_scalar(out=out, in0=in0, scalar1=s1, scalar2=s2, op0=op0, **kw)

    def stt(self, out, in0, scalar, in1, op0, op1, accum_out=None):
        kw = {}
        if accum_out is not None:
            kw["accum_out"] = accum_out
        return lambda e: e.scalar_tensor_tensor(out=out, in0=in0, scalar=scalar, in1=in1, op0=op0, op1=op1, **kw)

    def cp(self, out, in_):
        return lambda e: e.tensor_copy(out=out, in_=in_)

    def ms(self, ap, val):
        return lambda e: e.memset(ap, val)


def build_program(nlayers=DEPTH, stop=None):
    nc = bass.Bass("TRN2", target_bir_lowering=False)
    K = KB(nc)
    P = K.P

    def din(name, shape, dt=F32):
        return nc.dram_tensor(name, list(shape), dt, kind="ExternalInput").ap()

    def dscr(name, shape, dt):
        return nc.dram_tensor(name, list(shape), dt).ap()

    x_d = din("x", [NSEQ, S, D])
    cT_d = din("cT", [128, 2 * NCH])
    w_mod_d = din("w_mod", [DEPTH, D, 3 * D])
    b_mod_d = din("b_mod", [DEPTH, 3 * D])
    ln_g_d = din("ln_g", [DEPTH, D])
    ln_b_d = din("ln_b", [DEPTH, D])
    a_w_in_d = din("a_w_in", [2, D, 4 * D])
    a_w_out_d = din("a_w_out", [2, D, D])
    lam_d = [din(n, [2, 128]) for n in ("a_lam_q1", "a_lam_k1", "a_lam_q2", "a_lam_k2")]
    subln_d = din("a_subln_g", [2, 256])
    kv_w_mod_d = din("kv_w_mod", [D, 2 * D])
    kv_b_mod_d = din("kv_b_mod", [1, 2 * D])
    kv_w_d = din("kv_w", [D, 2 * D + 16])
    kv_b_f_d = din("kv_b_f", [16, 1])
    b_w_in_d = din("b_w_in", [2, D, 2 * D])
    b_w_out_d = din("b_w_out", [2, D, D])
    out_d = nc.dram_tensor("out", [NSEQ, S, D], F32, kind="ExternalOutput").ap()

    xs_d = dscr("xs", [NSEQ, S, D], F32)
    qT_d = dscr("qT", [NSEQ, D, S], BF16)
    kT_d = dscr("kT", [NSEQ, D, S], BF16)
    v_d = dscr("v", [NSEQ, S, D], BF16)
    g_d = dscr("g", [NSEQ, S, D], BF16)
    kTs_d = dscr("kTs", [NSEQ, D, S], BF16)
    vs_d = dscr("vs", [NSEQ, S, D], BF16)
    FT_d = dscr("FT", [NSEQ, 16, S], F32)
    gate_d = dscr("gate", [DEPTH, 2, D], F32)

    ps = nc.alloc_psum_tensor("ps", [128, 4096], F32)
    banks = [Buf(ps[:, i * 512:(i + 1) * 512]) for i in range(8)]
    for i, bk_ in enumerate(banks):
        bk_.bank = i
    ptb_ap = ps[:, 3584:4096].bitcast(BF16)

    ident = K.sb([128, 128], F32)
    identb = K.sb([128, 128], BF16)
    onesb = K.sb([128, 128], BF16)
    maskd = K.sb([128, 128], F32)
    biasA = K.sb([128, 8, 17], F32)
    colmod = K.sb([128, 5, 64], F32)
    cact = K.sb([128, 2 * NCH], F32)
    neglam = K.sb([128, 2], F32)
    gsub = K.sb([128, 2, 256], F32)
    nlam256 = K.sb([128, 2, 256], F32)
    negF = K.sb([128, NSEQ, 16, 16], F32)
    negbf = K.sb([16, 1], F32)
    onecol = K.sb([128, 1], F32)
    nhalf = K.sb([128, 1], F32)
    small = K.sb([128, 64], F32)
    HT = K.sb([128, NCH, S], BF16)
    HTB = [Buf() for _ in range(NTB)]
    W_BASE = K.off
    K.off += 65536
    H_BASE = K.off
    H_SIZE = SBUF_END - H_BASE
    assert H_SIZE >= 60 * 1024, H_SIZE

    def region(base, size):
        st = {"o": base}

        def take(shape, dt):
            el = 2 if dt == BF16 else 4
            nb = (int(np.prod(shape[1:])) * el + 63) // 64 * 64
            t = K.sb(shape, dt, at=st["o"])
            st["o"] += nb
            assert st["o"] <= base + size, ("region overflow", st["o"] - base, size)
            return t
        return take

    wtake = region(W_BASE, 65536)
    wslot = [Buf(wtake([128, NCH, 512], BF16)) for _ in range(4)]
    wtake2 = region(W_BASE, 65536)
    wf32 = [wtake2([128, NCH, 512], F32) for _ in range(2)]

    ctick = []

    def setup_consts():
        t = P.op("pool", K.ms(ident[:], 1.0))
        P.wait("pool", t)
        t = P.op("pool", lambda e: e.affine_select(out=ident[:], in_=ident[:], pattern=[[-1, 128]],
                                                   compare_op=ALU.is_equal, fill=0.0, base=0, channel_multiplier=1))
        P.wait("pool", t)
        ctick.append(P.op("pool", K.cp(identb[:], ident[:])))
        ctick.append(P.op("pool", K.ms(onesb[:], 1.0)))
        ctick.append(P.op("pool", K.ms(onecol[:], 1.0)))
        ctick.append(P.op("pool", K.ms(nhalf[:], -0.5)))
        t = P.op("pool", K.ms(maskd[:], 0.0))
        P.wait("pool", t)
        ctick.append(P.op("pool", lambda e: e.affine_select(out=maskd[:], in_=maskd[:], pattern=[[1, 128]],
                                                            compare_op=ALU.is_ge, fill=-30000.0, base=0,
                                                            channel_multiplier=-1)))
        io = K.sb([128, 17], I32, at=H_BASE)
        iof = K.sb([128, 17], F32, at=H_BASE + 128)
        t = P.op("pool", lambda e: e.iota(io[:], pattern=[[-128, 17]], base=-64, channel_multiplier=1))
        P.wait("pool", t)
        t = P.op("pool", K.cp(iof[:], io[:]))
        P.wait("pool", t)
        for h in range(8):
            slope = 2.0 ** (-8.0 * (h + 1) / 8)
            t = P.op("pool", K.ts(biasA[:, h, :], iof[:], slope, ALU.mult))
        ctick.append(t)
        craw = K.sb([128, 2 * NCH], F32, at=H_BASE + 256)
        bfr = K.sb([16, 1], F32, at=H_BASE + 512)
        P.op("sp", lambda e: e.dma_start(out=craw[:], in_=cT_d), key=("dma", "c0"), amount=16)
        P.op("sp", lambda e: e.dma_start(out=bfr[:], in_=kv_b_f_d), key=("dma", "c0"), amount=16)
        lt = [K.sb([128, 2, 128], F32, at=H_BASE + 1024 + i * 1024) for i in range(4)]
        sg = K.sb([128, 2, 256], F32, at=H_BASE + 1024 + 4096)
        junk = K.sb([128, 128], F32, at=H_BASE + 1024 + 4096 + 2048)
        ee = K.sb([128, 4], F32, at=H_BASE + 1024 + 4096 + 2048 + 512)
        tl = []
        for i in range(4):
            for l in range(2):
                tl.append(P.op("sp", lambda e, i=i, l=l: e.dma_start(out=lt[i][:, l, :], in_=lam_d[i][l:l + 1, :].partition_broadcast(128)),
                               key=("dma", "c0"), amount=16))
        for l in range(2):
            tl.append(P.op("sp", lambda e, l=l: e.dma_start(out=sg[:, l, :], in_=subln_d[l:l + 1, :].partition_broadcast(128)),
                           key=("dma", "c0"), amount=16))
        P.wait("dve", tl[-1])
        P.wait("act", tl[-1])
        ctick.append(P.op("act", K.actf(cact[:], craw[:], AF.Silu)))
        ctick.append(P.op("dve", K.ts(negbf[:], bfr[:], -1.0, ALU.mult)))
        t = None
        for l in range(2):
            for pr in range(2):
                if t is not None:
                    P.wait("dve", t)
                t = P.op("dve", K.stt(junk[:], lt[2 * pr][:, l, :], 1.0, lt[2 * pr + 1][:, l, :], ALU.mult, ALU.mult,
                                      accum_out=small[:, l * 2 + pr:l * 2 + pr + 1]))
        P.wait("act", t)
        t = P.op("act", K.actf(ee[:], small[:, 0:4], AF.Exp))
        P.wait("dve", t)
        for l in range(2):
            lam_init = 0.8 - 0.6 * math.exp(-0.3 * l)
            t = P.op("dve", K.tt(neglam[:, l:l + 1], ee[:, 2 * l + 1:2 * l + 2], ee[:, 2 * l:2 * l + 1], ALU.subtract))
            P.wait("dve", t)
            t = P.op("dve", K.ts(neglam[:, l:l + 1], neglam[:, l:l + 1], -lam_init, ALU.add))
            t = P.op("dve", K.ts(gsub[:, l, :], sg[:, l, :], 1.0 - lam_init, ALU.mult))
            P.wait("dve", t)
            t = P.op("dve", K.ms(nlam256[:, l, :], 1.0))
            P.wait("dve", t)
            t = P.op("dve", K.ts(nlam256[:, l, :], nlam256[:, l, :], neglam[:, l:l + 1], ALU.mult))
        ctick.append(t)
        for e in ENGS:
            P.wait(e, ctick)

    def mod_prologue():
        mods = [(w_mod_d[l], b_mod_d[l:l + 1, :], 3 * D) for l in range(DEPTH)] + [(kv_w_mod_d, kv_b_mod_d, 2 * D)]
        hb = region(H_BASE, H_SIZE)
        brow = [Buf(hb([2, 512], F32)) for _ in range(2)]
        mrow = [Buf(hb([2, 512], F32)) for _ in range(2)]
        wsl = [Buf(wf32[0]), Buf(wf32[1])]
        pm = [banks[0], banks[1]]
        pc = banks[2]
        colb = Buf()
        si = 0
        for i, (W, Bv, ncols) in enumerate(mods):
            for s in range(ncols // 512):
                n0 = s * 512
                ws = wsl[si % 2]
                br = brow[si % 2]
                mr = mrow[si % 2]
                pmb = pm[si % 2]
                K.acquire("sp", writes=[ws])
                for hh in range(2):
                    t = P.op("sp", lambda e, ws=ws, W=W, n0=n0, hh=hh: e.dma_start(
                        out=ws.t[:, hh * 8:(hh + 1) * 8, :],
                        in_=W[hh * 1024:(hh + 1) * 1024, n0:n0 + 512].rearrange("(c p) n -> p c n", p=128)),
                        key=("dma", "ws%d" % (si % 2)), amount=16)
                K.release(t, writes=[ws])
                K.dma("sp", br.t[:], Bv[:, n0:n0 + 512].partition_broadcast(2), writes=[br], key="br%d" % (si % 2))
                K.acquire("pe", reads=[ws], writes=[pmb])
                for ch in range(NCH):
                    t = P.op("pe", K.mm(pmb.t[0:2, :], cact[:, 2 * ch:2 * ch + 2], ws.t[:, ch, :], ch == 0, ch == NCH - 1),
                             inc=(ch == NCH - 1))
                K.release(t, reads=[ws], writes=[pmb])
                K.do("dve", K.tt(mr.t[:], pmb.t[0:2, :], br.t[:], ALU.add), reads=[pmb, br], writes=[mr])
                if n0 < 2 * D:
                    K.acquire("pe", reads=[mr], writes=([pc] if n0 == 0 else []))
                    for q in range(4):
                        ci = n0 // 128 + q
                        t = P.op("pe", K.mm(pc.t[:, ci * 2:ci * 2 + 2], mr.t[:, q * 128:(q + 1) * 128], ident[0:2, 0:2]),
                                 inc=(q == 3))
                    K.release(t, reads=[mr])
                    if n0 == 2 * D - 512:
                        K.release(t, writes=[pc])
                        K.do("dve", K.cp(colmod[:, i, 0:32], pc.t[:, 0:32]), reads=[pc], writes=[colb])
                        K.do("dve", K.ts(colmod[:, i, 32:64], pc.t[:, 32:64], 1.0, ALU.add), reads=[pc], writes=[colb])
                else:
                    K.dma("sp", gate_d[i, :, n0 - 2 * D:n0 - 2 * D + 512], mr.t[:], reads=[mr], key="gst%d" % (si % 2))
                si += 1
        for e in ENGS:
            P.wait(e, colb.wt)
        return [Tk(("dma", "gst0"), P.cnt.get(("dma", "gst0"), 0)), Tk(("dma", "gst1"), P.cnt.get(("dma", "gst1"), 0))]

    def phase1(modi, b, src):
        hb = region(H_BASE, H_SIZE)
        xr = Ring([Buf(hb([128, D], F32)) for _ in range(2)])
        tring = Ring([banks[6], banks[7]])
        for tb in range(NTB):
            xt = xr.next()
            K.dma("sp", xt.t[:], src[b, tb * 128:(tb + 1) * 128, :], writes=[xt], key="x%d" % (tb % 2))
            for g4 in range(4):
                bk = tring.next()
                K.acquire("pe", reads=[xt], writes=[bk])
                for k in range(4):
                    ch = g4 * 4 + k
                    t = P.op("pe", K.tr(bk.t[:, k * 128:(k + 1) * 128], xt.t[:, ch * 128:(ch + 1) * 128], ident[:]), inc=(k == 3))
                K.release(t, reads=[xt], writes=[bk])
                K.acquire("act", reads=[bk], writes=([HTB[tb]] if g4 == 0 else []))
                for k in range(4):
                    ch = g4 * 4 + k
                    t = P.op("act", K.actf(HT[:, ch, tb * 128:(tb + 1) * 128], bk.t[:, k * 128:(k + 1) * 128], AF.Identity,
                                           bias=colmod[:, modi, ch * 2 + b:ch * 2 + b + 1],
                                           scale=colmod[:, modi, 32 + ch * 2 + b:32 + ch * 2 + b + 1]), inc=(k == 3))
                K.release(t, reads=[bk])
            K.release(t, writes=[HTB[tb]])

    class Slabs:
        def __init__(self, nslots):
            self.slots = wslot[:nslots]
            self.i = 0

        def load(self, W, n0, ncols=512):
            s = self.slots[self.i % len(self.slots)]
            k = self.i % len(self.slots)
            self.i += 1
            K.acquire("pool", writes=[s])
            for hh in range(2):
                t = P.op("pool", lambda e, s=s, W=W, n0=n0, hh=hh, ncols=ncols: e.dma_start(
                    out=s.t[:, hh * 8:(hh + 1) * 8, 0:ncols],
                    in_=W[hh * 1024:(hh + 1) * 1024, n0:n0 + ncols].rearrange("(c p) n -> p c n", p=128)),
                    key=("dma", "wsl%d" % k), amount=16)
            K.release(t, writes=[s])
            return s

    def proj_feature_major(hb_stage, slab, n_f0, dst, b, func, scale, st):
        for fc in range(4):
            grp = st["grp"] % 2
            st["grp"] += 1
            bks = banks[grp * 4:grp * 4 + 4]
            K.acquire("pe", reads=[slab] + HTB, writes=bks)
            for ch in range(NCH):
                for tt in range(4):
                    t = P.op("pe", K.mm(bks[tt].t[:, :], slab.t[:, ch, fc * 128:(fc + 1) * 128], HT[:, ch, tt * 512:(tt + 1) * 512],
                                        ch == 0, ch == NCH - 1), inc=(ch == NCH - 1 and tt == 3))
            K.release(t, reads=[slab] + HTB, writes=bks)
            stg = hb_stage.next()
            K.acquire("act", writes=[stg])
            K.acquire("dve", writes=[stg])
            stg.wt = None
            stg.rts = {}
            parts = []
            for tt in range(4):
                use_act = (func is not None) or ((st["ev"] % 2) == 0)
                st["ev"] += 1
                o = stg.t[:, tt * 512:(tt + 1) * 512]
                if use_act:
                    af = func if func is not None else (AF.Identity if scale is not None else AF.Copy)
                    fn = K.actf(o, bks[tt].t[:, :], af, scale=scale)
                    tl = K.do("act", fn, reads=[bks[tt]])
                else:
                    if scale is not None:
                        fn = K.ts(o, bks[tt].t[:, :], scale, ALU.mult)
                    else:
                        fn = K.cp(o, bks[tt].t[:, :])
                    tl = K.do("dve", fn, reads=[bks[tt]])
                parts.append(tl)
            P.wait("sp", parts)
            f0 = n_f0 + fc * 128
            tstore = K.dma("sp", dst[b, f0:f0 + 128, :], stg.t[:], reads=[stg], key="fst%d" % stg.idx)
            st["stores"].append(tstore)

    def proj_token_major(hb_stage, slab, n0, dst, b, func, st):
        for tb in range(NTB):
            bk = banks[st["tbk"] % 8]
            st["tbk"] += 1
            K.acquire("pe", reads=[slab, HTB[tb]], writes=[bk])
            for ch in range(NCH):
                t = P.op("pe", K.mm(bk.t[:, :], HT[:, ch, tb * 128:(tb + 1) * 128], slab.t[:, ch, :], ch == 0, ch == NCH - 1),
                         inc=(ch == NCH - 1))
            K.release(t, reads=[slab, HTB[tb]], writes=[bk])
            stg = hb_stage.next()
            use_act = (func is not None) or ((st["ev"] % 2) == 0)
            st["ev"] += 1
            if use_act:
                K.do("act", K.actf(stg.t[:], bk.t[:, :], func if func is not None else AF.Copy), reads=[bk], writes=[stg])
            else:
                K.do("dve", K.cp(stg.t[:], bk.t[:, :]), reads=[bk], writes=[stg])
            tstore = K.dma("sp", dst[b, tb * 128:(tb + 1) * 128, n0:n0 + 512], stg.t[:], reads=[stg], key="tst%d" % stg.idx)
            st["stores"].append(tstore)

    def mk_stage(hb, n, shape):
        bufs = []
        for i in range(n):
            bb = Buf(hb(shape, BF16))
            bb.idx = i
            bb.parts = []
            bufs.append(bb)
        return Ring(bufs)

    def plan_of(kind, l):
        if kind == "A":
            return a_w_in_d[l], 16
        if kind == "B":
            return b_w_in_d[l - 2], 8
        return kv_w_d, 8

    def preload(kind, l):
        W, ns = plan_of(kind, l)
        slabs = Slabs(3)
        loaded = {}
        for s in range(min(3, ns)):
            loaded[s] = slabs.load(W, s * 512)
        return slabs, loaded

    def phase2(kind, l, b, hb, pre):
        st = {"grp": 0, "ev": 0, "tbk": 0, "stores": []}
        fstage = mk_stage(hb, 2, [128, S])
        tstage = mk_stage(hb, 4, [128, 512])
        slabs, loaded = pre
        if kind == "A":
            W = a_w_in_d[l]
            plan = [("f", qT_d, None, QSCALE)] * 4 + [("f", kT_d, None, None)] * 4 + [("t", v_d, None, None)] * 4 + [("t", g_d, AF.Silu, None)] * 4
        elif kind == "B":
            W = b_w_in_d[l - 2]
            plan = [("f", qT_d, None, QSCALE)] * 4 + [("f", g_d, AF.Silu, None)] * 4
        else:
            W = kv_w_d
            plan = [("f", kTs_d, None, None)] * 4 + [("t", vs_d, None, None)] * 4
        ns = len(plan)
        for s in range(ns):
            typ, dst, func, scale = plan[s]
            slab = loaded.pop(s)
            if typ == "f":
                proj_feature_major(fstage, slab, (s % 4) * 512, dst, b, func, scale, st)
            else:
                proj_token_major(tstage, slab, (s % 4) * 512, dst, b, func, st)
            if s + 3 < ns:
                loaded[s + 3] = slabs.load(W, (s + 3) * 512)
        return st["stores"]

    def phase3A(l, b, stores, hb):
        P.wait("sp", stores)
        sets = []
        for i in range(2):
            sets.append({"q": Buf(hb([128, 2, S], BF16)), "k": Buf(hb([128, 2, S], BF16)),
                         "v": Buf(hb([128, NTB, 257], BF16)), "i": i})
        for s_ in sets:
            K.do("pool", K.ms(s_["v"].t[:, :, 256:257], 1.0), writes=[s_["v"]])
        NG = 5
        gbufs = [Buf(hb([128, 256], BF16)) for _ in range(NG)]
        for i, g_ in enumerate(gbufs):
            g_.idx = i
        gring = Ring(gbufs)
        ggr = Ring([Buf(hb([128, 256], F32)) for _ in range(4)])
        ptring = Ring([Buf(hb([128, 2, 128], BF16)) for _ in range(10)])
        tmpr = Ring([Buf(hb([128, 256], F32)) for _ in range(2)])
        orr = Ring([Buf(hb([128, 256], F32)) for _ in range(4)])
        ofr = Ring([Buf(hb([128, 256], BF16)) for _ in range(4)])
        smr = Ring([Buf(hb([128, 8], F32)) for _ in range(8)])
        sslots = []
        for bi in range(3):
            sb_ = banks[bi]
            sb_.v = sb_.t[:, 0:512].rearrange("p (m q) -> p m q", m=2)
            sslots.append(sb_)
        sring = Ring(sslots)
        osets = [(banks[3], banks[4]), (banks[5], banks[6])]
        banks[7].v = ptb_ap[:, 0:256].rearrange("p (e t) -> p e t", e=2)
        pslot = banks[7]

        def load_head(h):
            s_ = sets[h % 2]
            K.dma("sp", s_["q"].t[:], qT_d[b, h * 256:(h + 1) * 256, :].rearrange("(m p) t -> p m t", p=128),
                  writes=[s_["q"]], key="hq%d" % s_["i"])
            K.dma("sp", s_["k"].t[:], kT_d[b, h * 256:(h + 1) * 256, :].rearrange("(m p) t -> p m t", p=128),
                  writes=[s_["k"]], key="hk%d" % s_["i"])
            K.acquire("sp", writes=[s_["v"]])
            for hh in range(2):
                t = P.op("sp", lambda e, s_=s_, hh=hh: e.dma_start(
                    out=s_["v"].t[:, hh * 8:(hh + 1) * 8, 0:256],
                    in_=v_d[b, hh * 1024:(hh + 1) * 1024, h * 256:(h + 1) * 256].rearrange("(kb p) e -> p kb e", p=128)),
                    key=("dma", "hv%d" % s_["i"]), amount=16)
            K.release(t, writes=[s_["v"]])

        steps = [(h, qt2, j) for h in range(8) for qt2 in range(NTB // 2) for j in range(2 * qt2 + 2)]
        pend = {}
        first_w = set()
        state = {"fid": 0}
        gts = {}
        dq = []
        FMAX = 3

        def load_g(h, qb):
            gt = gring.next()
            K.dma("sp", gt.t[:], g_d[b, qb * 128:(qb + 1) * 128, h * 256:(h + 1) * 256], writes=[gt], key="g%d" % gt.idx)
            gts[(h, qb)] = gt

        def run_due(idx, force_upto=-1):
            i = 0
            while i < len(dq):
                due, fid, fn = dq[i]
                if due <= idx or fid <= force_upto:
                    dq.pop(i)
                    fn(idx)
                    i = 0
                else:
                    i += 1

        def qk(idx):
            h, qt2, j = steps[idx]
            s_ = sets[h % 2]
            sl = sring.next()
            qlo = 128 if j == 2 * qt2 + 1 else 0
            K.acquire("pe", reads=[s_["q"], s_["k"]], writes=[sl])
            for m in range(2):
                t = P.op("pe", K.mm(sl.v[:, m, qlo:256], s_["k"].t[:, m, j * 128:(j + 1) * 128],
                                    s_["q"].t[:, m, qt2 * 256 + qlo:(qt2 + 1) * 256]), inc=(m == 1))
            K.release(t, reads=[s_["q"], s_["k"]], writes=[sl])
            pts = {}
            for r in range(2):
                qb = 2 * qt2 + r
                if j > qb:
                    continue
                pt = ptring.next()
                K.do("act", K.actf(pt.t[:], sl.v[:, :, r * 128:(r + 1) * 128], AF.Exp, bias=biasA[:, h, qb - j:qb - j + 1], scale=1.0),
                     reads=[sl], writes=[pt])
                if j == qb:
                    K.do("pool", lambda e, pt=pt: e.affine_select(out=pt.t[:], in_=pt.t[:], pattern=[[0, 2], [1, 128]],
                                                                  compare_op=ALU.is_ge, fill=0.0, base=0, channel_multiplier=-1),
                         reads=[], writes=[pt])
                pts[r] = pt
            pend[idx] = pts

        def pv(idx, cur):
            h, qt2, j = steps[idx]
            s_ = sets[h % 2]
            pts = pend.pop(idx)
            for r in range(2):
                if r not in pts:
                    continue
                qb = 2 * qt2 + r
                pt = pts[r]
                ob = osets[r]
                K.acquire("pe", reads=[pt, s_["v"]], writes=(list(ob) if j == 0 else []))
                for m in range(2):
                    t = P.op("pe", K.mm(ob[m].t[:, 0:257], pt.t[:, m, :], s_["v"].t[:, j, :], j == 0, j == qb), inc=(m == 1))
                K.release(t, reads=[pt, s_["v"]])
                if j == qb:
                    K.release(t, writes=list(ob))
                    finalize(h, qb, ob, cur)

        def finalize(h, qb, ob, cur):
            fid = state["fid"]
            state["fid"] += 1
            run_due(cur, force_upto=fid - FMAX)
            sm = smr.next()
            tmp = tmpr.next()
            o = orr.next()
            gg = ggr.next()
            gt = gts.pop((h, qb))
            nxt = (h, qb + 1) if qb + 1 < NTB else ((h + 1, 0) if h + 1 < 8 else None)
            if nxt is not None:
                load_g(*nxt)
            lcol = ps[:, ob[0].bank * 512 + 256:ob[1].bank * 512 + 257:512]
            K.do("dve", lambda e: e.reciprocal(out=sm.t[:, 0:2], in_=lcol), reads=[ob[0], ob[1]], writes=[sm])
            K.do("dve", K.stt(tmp.t[:], ob[1].t[:, 0:256], sm.t[:, 1:2], nlam256[:, l, :], ALU.mult, ALU.mult),
                 reads=[ob[1], sm], writes=[tmp])
            K.do("dve", K.stt(o.t[:], ob[0].t[:, 0:256], sm.t[:, 0:1], tmp.t[:], ALU.mult, ALU.add), reads=[ob[0], tmp, sm], writes=[o])
            K.do("dve", K.stt(tmp.t[:], o.t[:], 1.0, o.t[:], ALU.mult, ALU.mult, accum_out=sm.t[:, 3:4]), reads=[o], writes=[tmp, sm])
            K.do("dve", K.ts(sm.t[:, 4:5], sm.t[:, 3:4], 1.0 / 256, ALU.mult, EPS, ALU.add), reads=[sm], writes=[sm])
            K.do("pool", K.tt(sm.t[:, 5:6], sm.t[:, 4:5], nhalf[:], ALU.pow), reads=[sm], writes=[sm])
            K.do("pool", K.tt(gg.t[:], gt.t[:], gsub[:, l, :], ALU.mult), reads=[gt], writes=[gg])

            def stage1b(cur2):
                of = ofr.next()
                K.do("dve", K.stt(of.t[:], o.t[:], sm.t[:, 5:6], gg.t[:], ALU.mult, ALU.mult), reads=[o, sm, gg], writes=[of])

                def stage2(cur3):
                    K.acquire("pe", reads=[of], writes=[pslot])
                    for e2 in range(2):
                        t = P.op("pe", K.tr(pslot.v[:, e2, :], of.t[:, e2 * 128:(e2 + 1) * 128], identb[:]), inc=(e2 == 1))
                    K.release(t, reads=[of], writes=[pslot])
                    wr = [HTB[qb]] if (qb not in first_w) else []
                    first_w.add(qb)
                    K.acquire("dve", reads=[pslot], writes=wr)
                    t = P.op("dve", K.cp(HT[:, 2 * h:2 * h + 2, qb * 128:(qb + 1) * 128], pslot.v))
                    K.release(t, reads=[pslot], writes=[HTB[qb]])

                dq.append([cur2 + 2, fid, stage2])

            dq.append([cur + 2, fid, stage1b])

        LA = 3
        n = len(steps)
        for idx in range(n + LA):
            if idx == 0:
                load_head(0)
                load_head(1)
                load_g(0, 0)
            if idx < n:
                qk(idx)
            if idx >= LA:
                pv(idx - LA, idx)
                h, qb, j = steps[idx - LA]
                if qb == 0 and j == 0 and h >= 1 and h + 1 < 8:
                    load_head(h + 1)
            run_due(idx)
        while dq:
            run_due(0, force_upto=10 ** 9)

    def phase3B(l, b, stores, hb):
        P.wait("sp", stores)
        sets = []
        for i in range(2):
            sets.append({"q": Buf(hb([128, S], BF16)), "g": Buf(hb([128, S], BF16)), "k": Buf(hb([128, S], BF16)),
                         "v": Buf(hb([128, NTB, 128], BF16)), "f": Buf(hb([128, S], F32)), "i": i})
        Tr = Ring([Buf(hb([128, 512], F32)) for _ in range(3)])
        ptring = Ring([Buf(hb([128, 512], BF16)) for _ in range(4)])
        rcr = Ring([Buf(hb([128, 512], F32)) for _ in range(2)])
        onr = Ring([Buf(hb([128, 512], F32)) for _ in range(2)])
        sring = Ring([banks[0], banks[1], banks[2], banks[7]])
        osets = [(banks[3], banks[5]), (banks[4], banks[6])]

        def load_head(h):
            s_ = sets[h % 2]
            i = s_["i"]
            K.dma("sp", s_["q"].t[:], qT_d[b, h * 128:(h + 1) * 128, :], writes=[s_["q"]], key="hq%d" % i)
            K.dma("sp", s_["k"].t[:], kTs_d[b, h * 128:(h + 1) * 128, :], writes=[s_["k"]], key="hk%d" % i)
            K.acquire("sp", writes=[s_["v"]])
            for hh in range(2):
                t = P.op("sp", lambda e, s_=s_, hh=hh: e.dma_start(
                    out=s_["v"].t[:, hh * 8:(hh + 1) * 8, :],
                    in_=vs_d[b, hh * 1024:(hh + 1) * 1024, h * 128:(h + 1) * 128].rearrange("(kb p) e -> p kb e", p=128)),
                    key=("dma", "hv%d" % i), amount=16)
            K.release(t, writes=[s_["v"]])
            K.dma("sp", s_["f"].t[:], FT_d[b, h:h + 1, :].partition_broadcast(128), writes=[s_["f"]], key="hf%d" % i)
            K.dma("sp", s_["g"].t[:], g_d[b, h * 128:(h + 1) * 128, :], writes=[s_["g"]], key="hg%d" % i)

        steps = [(h, qt, j) for h in range(16) for qt in range(4) for j in range(4 * qt + 4)]
        pend = {}
        first_w = set()
        state = {"oset": 0}

        def qk(idx):
            h, qt, j = steps[idx]
            s_ = sets[h % 2]
            jl = j - 4 * qt
            qlo = 128 * jl if jl > 0 else 0
            sl = sring.next()
            K.acquire("pe", reads=[s_["q"], s_["k"]], writes=[sl])
            t = P.op("pe", K.mm(sl.t[:, qlo:512], s_["k"].t[:, j * 128:(j + 1) * 128], s_["q"].t[:, qt * 512 + qlo:(qt + 1) * 512]))
            K.release(t, reads=[s_["q"], s_["k"]], writes=[sl])
            T = Tr.next()
            K.do("dve", K.stt(T.t[:, qlo:512], sl.t[:, qlo:512], negF[:, b, j, h:h + 1], s_["f"].t[:, qt * 512 + qlo:(qt + 1) * 512],
                              ALU.add, ALU.add), reads=[sl, s_["f"]], writes=[T])
            if jl >= 0:
                K.do("pool", K.tt(T.t[:, qlo:qlo + 128], T.t[:, qlo:qlo + 128], maskd[:], ALU.add), reads=[T], writes=[T])
            pt = ptring.next()
            K.do("act", K.actf(pt.t[:, qlo:512], T.t[:, qlo:512], AF.Exp), reads=[T], writes=[pt])
            pend[idx] = (pt, qlo)

        def pv(idx):
            h, qt, j = steps[idx]
            s_ = sets[h % 2]
            pt, qlo = pend.pop(idx)
            last = (j == 4 * qt + 3)
            if j == 0:
                state["oset"] ^= 1
            ob, lb = osets[state["oset"]]
            K.acquire("pe", reads=[pt, s_["v"]], writes=([ob, lb] if j == 0 else []))
            P.op("pe", K.mm(ob.t[:, qlo:512], s_["v"].t[:, j, :], pt.t[:, qlo:512], j == 0, last), inc=False)
            t = P.op("pe", K.mm(lb.t[:, qlo:512], onesb[:], pt.t[:, qlo:512], j == 0, last))
            K.release(t, reads=[pt, s_["v"]])
            if last:
                K.release(t, writes=[ob, lb])
                rc = rcr.next()
                on = onr.next()
                K.do("dve", lambda e: e.reciprocal(out=rc.t[:], in_=lb.t[:, :]), reads=[lb], writes=[rc])
                K.do("dve", K.tt(on.t[:], ob.t[:, :], rc.t[:], ALU.mult), reads=[ob, rc], writes=[on])
                tbs = [HTB[qt * 4 + k] for k in range(4)]
                wr = [x_ for x_ in tbs if x_.id not in first_w]
                for x_ in tbs:
                    first_w.add(x_.id)
                K.acquire("pool", reads=[on, s_["g"]], writes=wr)
                t = P.op("pool", K.tt(HT[:, h, qt * 512:(qt + 1) * 512], on.t[:], s_["g"].t[:, qt * 512:(qt + 1) * 512], ALU.mult))
                K.release(t, reads=[on, s_["g"]], writes=tbs)

        LA = 3
        n = len(steps)
        for idx in range(n + LA):
            if idx == 0:
                load_head(0)
                load_head(1)
            if idx < n:
                qk(idx)
            if idx >= LA:
                pv(idx - LA)
                h, qt, j = steps[idx - LA]
                if qt == 0 and j == 0 and h >= 1 and h + 1 < 16:
                    load_head(h + 1)

    def prefetch_wout(W):
        for n in range(4):
            s = wslot[n]
            K.acquire("pool", writes=[s])
            for hh in range(2):
                t = P.op("pool", lambda e, s=s, W=W, n=n, hh=hh: e.dma_start(
                    out=s.t[:, hh * 8:(hh + 1) * 8, :],
                    in_=W[hh * 1024:(hh + 1) * 1024, n * 512:(n + 1) * 512].rearrange("(c p) n -> p c n", p=128)),
                    key=("dma", "wsl%d" % n), amount=16)
            K.release(t, writes=[s])

    def phase4(l, b, src, dst, gate_t, hb):
        lng = Buf(hb([128, D], F32))
        lnb = Buf(hb([128, D], F32))
        gtb = Buf(hb([128, D], F32))
        xr = Ring([Buf(hb([128, D], F32)) for _ in range(2)])
        zr = Ring([Buf(hb([128, D], F32)) for _ in range(2)])
        smr = Ring([Buf(hb([128, 8], F32)) for _ in range(2)])
        P.wait("sp", gate_t)
        K.dma("sp", lng.t[:], ln_g_d[l:l + 1, :].partition_broadcast(128), writes=[lng], key="lng")
        K.dma("sp", lnb.t[:], ln_b_d[l:l + 1, :].partition_broadcast(128), writes=[lnb], key="lnb")
        K.dma("sp", gtb.t[:], gate_d[l, b:b + 1, :].partition_broadcast(128), writes=[gtb], key="gtb")
        stores = []
        xts = {}

        def ldx(tb):
            xt = xr.next()
            K.dma("sp", xt.t[:], src[b, tb * 128:(tb + 1) * 128, :], writes=[xt], key="x%d" % (tb % 2))
            xts[tb] = xt

        ldx(0)
        for tb in range(NTB):
            if tb + 1 < NTB:
                ldx(tb + 1)
            xt = xts.pop(tb)
            bks = banks[(tb % 2) * 4:(tb % 2) * 4 + 4]
            K.acquire("pe", reads=wslot + [HTB[tb]], writes=bks)
            for fc in range(NCH):
                for n in range(4):
                    t = P.op("pe", K.mm(bks[n].t[:, :], HT[:, fc, tb * 128:(tb + 1) * 128], wslot[n].t[:, fc, :], fc == 0, fc == NCH - 1),
                             inc=(fc == NCH - 1 and n == 3))
            K.release(t, reads=wslot + [HTB[tb]], writes=bks)
            z = zr.next()
            sm = smr.next()
            K.acquire("dve", writes=[z])
            for n in range(4):
                t = K.do("dve", K.tt(z.t[:, n * 512:(n + 1) * 512], bks[n].t[:, :], gtb.t[:, n * 512:(n + 1) * 512], ALU.mult),
                         reads=[bks[n], gtb])
            K.release(t, writes=[z])
            K.do("dve", K.stt(z.t[:], xt.t[:], DN_ALPHA, z.t[:], ALU.mult, ALU.add, accum_out=sm.t[:, 0:1]),
                 reads=[xt, z], writes=[z, sm])
            K.do("act", K.actf(xt.t[:], z.t[:], AF.Square, accum_out=sm.t[:, 1:2]), reads=[z], writes=[xt, sm])
            K.do("dve", K.ts(sm.t[:, 2:3], sm.t[:, 0:1], 1.0 / D, ALU.mult), reads=[sm], writes=[sm])
            K.do("dve", K.tt(sm.t[:, 3:4], sm.t[:, 2:3], sm.t[:, 2:3], ALU.mult), reads=[sm], writes=[sm])
            K.do("dve", K.stt(sm.t[:, 4:5], sm.t[:, 1:2], 1.0 / D, sm.t[:, 3:4], ALU.mult, ALU.subtract), reads=[sm], writes=[sm])
            K.do("dve", K.ts(sm.t[:, 7:8], sm.t[:, 4:5], EPS, ALU.add), reads=[sm], writes=[sm])
            K.do("pool", K.tt(sm.t[:, 5:6], sm.t[:, 7:8], nhalf[:], ALU.pow), reads=[sm], writes=[sm])
            K.do("dve", K.stt(sm.t[:, 6:7], sm.t[:, 2:3], -1.0, sm.t[:, 5:6], ALU.mult, ALU.mult), reads=[sm], writes=[sm])
            K.do("act", K.actf(z.t[:], z.t[:], AF.Identity, bias=sm.t[:, 6:7], scale=sm.t[:, 5:6]), reads=[sm, z], writes=[z])
            K.do("pool", K.tt(z.t[:], z.t[:], lng.t[:], ALU.mult), reads=[z, lng], writes=[z])
            K.do("pool", K.tt(z.t[:], z.t[:], lnb.t[:], ALU.add), reads=[z, lnb], writes=[z])
            stores.append(K.dma("sp", dst[b, tb * 128:(tb + 1) * 128, :], z.t[:], reads=[z], key="zst%d" % (tb % 2)))
        return stores

    def kv_extras(b, hb):
        wz = Buf(hb([128, NCH, 16], BF16))
        e1 = Buf(hb([16, S], F32))
        sp_ = Buf(hb([16, S], F32))
        fpos = Buf(hb([16, S], F32))
        fneg = Buf(hb([16, S], F32))
        K.dma("pool", wz.t[:], kv_w_d[:, 2 * D:2 * D + 16].rearrange("(c p) n -> p c n", p=128), writes=[wz], key="wz")
        bks = banks[0:4]
        K.acquire("pe", reads=[wz] + HTB, writes=bks)
        for ch in range(NCH):
            for tt in range(4):
                t = P.op("pe", K.mm(bks[tt].t[0:16, :], wz.t[:, ch, :], HT[:, ch, tt * 512:(tt + 1) * 512], ch == 0, ch == NCH - 1),
                         inc=(ch == NCH - 1 and tt == 3))
        K.release(t, reads=[wz] + HTB, writes=bks)
        for tt in range(4):
            K.do("act", K.actf(e1.t[:, tt * 512:(tt + 1) * 512], bks[tt].t[0:16, :], AF.Exp, bias=negbf[:], scale=-1.0),
                 reads=[bks[tt]], writes=([e1] if tt == 0 else []))
        K.do("act", K.actf(sp_.t[:], e1.t[:], AF.Ln, bias=onecol[0:16, :], scale=1.0), reads=[e1], writes=[sp_])
        K.do("dve", lambda e: e.tensor_tensor_scan(out=fpos.t[:], data0=sp_.t[:], data1=sp_.t[:], initial=0.0,
                                                   op0=ALU.add, op1=ALU.bypass), reads=[sp_], writes=[fpos])
        K.do("dve", K.ts(fneg.t[:], fpos.t[:], -1.0, ALU.mult), reads=[fpos], writes=[fneg])
        tst = K.dma("sp", FT_d[b], fneg.t[:], reads=[fneg], key="ftst")
        bk = banks[4]
        K.acquire("pe", reads=[fpos], writes=[bk])
        for kb in range(NTB):
            t = P.op("pe", K.tr(bk.t[:, kb * 16:(kb + 1) * 16], fpos.t[:, kb * 128:(kb + 1) * 128], ident[0:16, 0:16]), inc=(kb == NTB - 1))
        K.release(t, reads=[fpos], writes=[bk])
        nb = Buf()
        K.do("dve", K.cp(negF[:, b, :, :], bk.t[:, 0:256].rearrange("p (k h) -> p k h", k=16)), reads=[bk], writes=[nb])
        for e in ENGS:
            P.wait(e, nb.wt)
        return [tst]

    class _Stop(Exception):
        pass

    stage = {"n": 0}

    def chk():
        stage["n"] += 1
        if stop is not None and stage["n"] >= stop:
            raise _Stop()

    def driver():
        setup_consts()
        chk()
        gate_t = mod_prologue()
        chk()
        kvstores = {}
        for l in range(nlayers):
            src = x_d if l == 0 else xs_d
            dst = out_d if l == nlayers - 1 else xs_d
            for b in range(NSEQ):
                if l == 2:
                    pre = preload("KV", l)
                    P.barrier()
                    phase1(4, b, xs_d)
                    P.barrier()
                    hb = region(H_BASE, H_SIZE)
                    st = phase2("KV", l, b, hb, pre)
                    st += kv_extras(b, hb)
                    kvstores[b] = st
                kind = "A" if l < 2 else "B"
                pre = preload(kind, l)
                P.barrier()
                phase1(l, b, src)
                chk()
                P.barrier()
                stores = phase2(kind, l, b, region(H_BASE, H_SIZE), pre)
                chk()
                prefetch_wout(a_w_out_d[l] if l < 2 else b_w_out_d[l - 2])
                P.barrier()
                if l < 2:
                    phase3A(l, b, stores, region(H_BASE, H_SIZE))
                else:
                    phase3B(l, b, stores + kvstores[b], region(H_BASE, H_SIZE))
                chk()
                P.barrier()
                fin = phase4(l, b, src, dst, gate_t, region(H_BASE, H_SIZE))
                P.wait("sp", fin)
                chk()

    try:
        driver()
    except _Stop:
        pass
    P.barrier(exclude=("__none__",))
    P.emit()
    return nc


_NC_CACHE = {}


def kernel(**inputs):
    n = 8
    if "nc" not in _NC_CACHE:
        _NC_CACHE["nc"] = build_program()
    nc = _NC_CACHE["nc"]
    f = lambda a: np.ascontiguousarray(np.asarray(a, dtype=np.float32))
    x = f(inputs["x"])
    c = f(inputs["c"])
    shared = {k: f(inputs[k]) for k in ("w_mod", "b_mod", "ln_g", "ln_b", "a_w_in", "a_w_out", "a_lam_q1", "a_lam_k1",
                                        "a_lam_q2", "a_lam_k2", "a_subln_g", "kv_w_mod", "kv_w", "b_w_in", "b_w_out")}
    shared["kv_b_mod"] = f(inputs["kv_b_mod"]).reshape(1, -1)
    shared["kv_b_f"] = f(inputs["kv_b_f"]).reshape(16, 1)
    in_maps = []
    for i in range(n):
        m = dict(shared)
        m["x"] = np.ascontiguousarray(x[2 * i:2 * i + 2])
        cc = c[2 * i:2 * i + 2]
        m["cT"] = np.ascontiguousarray(cc.reshape(2, NCH, 128).transpose(2, 1, 0).reshape(128, 2 * NCH))
        in_maps.append(m)
    res = run_bass_kernel_spmd(nc, in_maps, core_ids=list(range(n)))
    return np.concatenate([r["out"] for r in res.results], axis=0)
```

```python
from contextlib import ExitStack
import math
import numpy as np
import concourse.bass as bass
import concourse.mybir as mybir
from concourse.bass_utils import run_bass_kernel_spmd

F32 = mybir.dt.float32
BF16 = mybir.dt.bfloat16
I32 = mybir.dt.int32
AF = mybir.ActivationFunctionType
ALU = mybir.AluOpType

ENGS = ("pe", "act", "dve", "pool", "sp")
EPOCH = 30000

S = 2048
D = 2048
NCH = 16
NTB = 16
NSEQ = 2
DEPTH = 4
EPS = 1e-5
DN_ALPHA = (2.0 * DEPTH) ** 0.25
QSCALE = 128 ** -0.5
SBUF_BASE = 16512
SBUF_END = 229376


class Tk:
    __slots__ = ("key", "val")

    def __init__(self, key, val):
        self.key = key
        self.val = val


class Prog:
    def __init__(self, nc):
        self.nc = nc
        self.q = {e: [] for e in ENGS}
        self.cnt = {}
        self.seen = {}
        self.epoch = {e: 0 for e in ENGS}

    def wait(self, eng, *tickets):
        for t in tickets:
            if t is None:
                continue
            if isinstance(t, (list, tuple)):
                self.wait(eng, *t)
                continue
            if self.seen.get((eng, t.key), 0) >= t.val:
                continue
            self.seen[(eng, t.key)] = t.val
            self.q[eng].append(("wait", t.key, t.val))

    def op(self, eng, fn, inc=True, key=None, amount=1):
        if not inc:
            self.q[eng].append(("op", fn, None, 0))
            return None
        if key is None:
            key = (eng, self.epoch[eng])
            if self.cnt.get(key, 0) >= EPOCH:
                self.epoch[eng] += 1
                key = (eng, self.epoch[eng])
        self.cnt[key] = self.cnt.get(key, 0) + amount
        self.q[eng].append(("op", fn, key, amount))
        return Tk(key, self.cnt[key])

    def barrier(self, exclude=("wsl",)):
        tks = []
        for key, c in self.cnt.items():
            if key[0] == "dma" and str(key[1]).startswith(exclude):
                continue
            tks.append(Tk(key, c))
        for e in ENGS:
            self.wait(e, tks)

    def emit(self):
        nc = self.nc
        with ExitStack() as es:
            semh = {}
            for i, k in enumerate(self.cnt):
                semh[k] = es.enter_context(nc.semaphore("s%d" % i))
            blk = es.enter_context(nc.Block())

            def mk(eng):
                items = self.q[eng]

                def run(e):
                    for it in items:
                        if it[0] == "wait":
                            e.wait_ge(semh[it[1]], it[2])
                        else:
                            ins = it[1](e)
                            if it[2] is not None:
                                ins.then_inc(semh[it[2]], it[3])
                return run

            blk.tensor(mk("pe"))
            blk.scalar(mk("act"))
            blk.vector(mk("dve"))
            blk.gpsimd(mk("pool"))
            blk.sync(mk("sp"))


class Buf:
    _n = 0

    def __init__(self, t=None):
        self.t = t
        self.wt = None
        self.rts = {}
        Buf._n += 1
        self.id = Buf._n


class Ring:
    def __init__(self, bufs):
        self.bufs = bufs
        self.i = 0

    def next(self):
        b = self.bufs[self.i % len(self.bufs)]
        self.i += 1
        return b


class KB:
    def __init__(self, nc):
        self.nc = nc
        self.P = Prog(nc)
        self.off = SBUF_BASE
        self.nname = 0

    def sb(self, shape, dt, at=None):
        el = 2 if dt == BF16 else 4
        nb = int(np.prod(shape[1:])) * el
        nb = (nb + 63) // 64 * 64
        if at is None:
            at = self.off
            self.off += nb
            assert self.off <= SBUF_END, ("SBUF overflow", self.off)
        self.nname += 1
        return self.nc.alloc_sbuf_tensor_at("t%d" % self.nname, list(shape), dt, offset=at)

    def sbuf(self, shape, dt):
        return Buf(self.sb(shape, dt))

    def acquire(self, eng, reads=(), writes=()):
        P = self.P
        for b in reads:
            P.wait(eng, b.wt)
        for b in writes:
            P.wait(eng, b.wt)
            for k, v in b.rts.items():
                P.wait(eng, Tk(k, v))

    @staticmethod
    def release(t, reads=(), writes=()):
        for b in reads:
            b.rts[t.key] = max(b.rts.get(t.key, 0), t.val)
        for b in writes:
            b.wt = t
            b.rts = {}

    def do(self, eng, fn, reads=(), writes=(), key=None, amount=1):
        self.acquire(eng, reads, writes)
        t = self.P.op(eng, fn, key=key, amount=amount)
        self.release(t, reads, writes)
        return t

    def raw(self, eng, fn):
        self.P.op(eng, fn, inc=False)

    def dma(self, eng, out, in_, reads=(), writes=(), key=None):
        assert key is not None
        return self.do(eng, lambda e: e.dma_start(out=out, in_=in_), reads, writes, key=("dma", key), amount=16)

    def mm(self, out, lhsT, rhs, start=True, stop=True):
        return lambda e: e.matmul(out, lhsT=lhsT, rhs=rhs, start=start, stop=stop)

    def tr(self, out, in_, ident):
        return lambda e: e.transpose(out=out, in_=in_, identity=ident)

    def actf(self, out, in_, func, bias=None, scale=None, accum_out=None):
        kw = {}
        if bias is not None:
            kw["bias"] = bias
        if scale is not None:
            kw["scale"] = scale
        if accum_out is not None:
            kw["accum_out"] = accum_out
        return lambda e: e.activation(out=out, in_=in_, func=func, **kw)

    def tt(self, out, in0, in1, op):
        return lambda e: e.tensor_tensor(out=out, in0=in0, in1=in1, op=op)

    def ts(self, out, in0, s1, op0, s2=None, op1=None, accum_out=None):
        kw = {}
        if op1 is not None:
            kw["op1"] = op1
        if accum_out is not None:
            kw["accum_out"] = accum_out
        return lambda e: e.tensor_scalar(out=out, in0=in0, scalar1=s1, scalar2=s2, op0=op0, **kw)

    def stt(self, out, in0, scalar, in1, op0, op1, accum_out=None):
        kw = {}
        if accum_out is not None:
            kw["accum_out"] = accum_out
        return lambda e: e.scalar_tensor_tensor(out=out, in0=in0, scalar=scalar, in1=in1, op0=op0, op1=op1, **kw)

    def cp(self, out, in_):
        return lambda e: e.tensor_copy(out=out, in_=in_)

    def ms(self, ap, val):
        return lambda e: e.memset(ap, val)


def build_program(nlayers=DEPTH, stop=None):
    nc = bass.Bass("TRN2", target_bir_lowering=False)
    K = KB(nc)
    P = K.P

    def din(name, shape, dt=F32):
        return nc.dram_tensor(name, list(shape), dt, kind="ExternalInput").ap()

    def dscr(name, shape, dt):
        return nc.dram_tensor(name, list(shape), dt).ap()

    x_d = din("x", [NSEQ, S, D])
    cT_d = din("cT", [128, 2 * NCH])
    w_mod_d = din("w_mod", [DEPTH, D, 3 * D])
    b_mod_d = din("b_mod", [DEPTH, 3 * D])
    ln_g_d = din("ln_g", [DEPTH, D])
    ln_b_d = din("ln_b", [DEPTH, D])
    a_w_in_d = din("a_w_in", [2, D, 4 * D])
    a_w_out_d = din("a_w_out", [2, D, D])
    lam_d = [din(n, [2, 128]) for n in ("a_lam_q1", "a_lam_k1", "a_lam_q2", "a_lam_k2")]
    subln_d = din("a_subln_g", [2, 256])
    kv_w_mod_d = din("kv_w_mod", [D, 2 * D])
    kv_b_mod_d = din("kv_b_mod", [1, 2 * D])
    kv_w_d = din("kv_w", [D, 2 * D + 16])
    kv_b_f_d = din("kv_b_f", [16, 1])
    b_w_in_d = din("b_w_in", [2, D, 2 * D])
    b_w_out_d = din("b_w_out", [2, D, D])
    out_d = nc.dram_tensor("out", [NSEQ, S, D], F32, kind="ExternalOutput").ap()

    xs_d = dscr("xs", [NSEQ, S, D], F32)
    qT_d = dscr("qT", [NSEQ, D, S], BF16)
    kT_d = dscr("kT", [NSEQ, D, S], BF16)
    v_d = dscr("v", [NSEQ, S, D], BF16)
    g_d = dscr("g", [NSEQ, S, D], BF16)
    kTs_d = dscr("kTs", [NSEQ, D, S], BF16)
    vs_d = dscr("vs", [NSEQ, S, D], BF16)
    FT_d = dscr("FT", [NSEQ, 16, S], F32)
    gate_d = dscr("gate", [DEPTH, 2, D], F32)

    ps = nc.alloc_psum_tensor("ps", [128, 4096], F32)
    banks = [Buf(ps[:, i * 512:(i + 1) * 512]) for i in range(8)]
    for i, bk_ in enumerate(banks):
        bk_.bank = i
    ptb_ap = ps[:, 3584:4096].bitcast(BF16)

    ident = K.sb([128, 128], F32)
    identb = K.sb([128, 128], BF16)
    onesb = K.sb([128, 128], BF16)
    maskd = K.sb([128, 128], F32)
    biasA = K.sb([128, 8, 17], F32)
    biasB = K.sb([128, 8, 18], F32)
    colmod = K.sb([128, 5, 64], F32)
    cact = K.sb([128, 2 * NCH], F32)
    neglam = K.sb([128, 2], F32)
    gsub = K.sb([128, 2, 256], F32)
    nlam256 = K.sb([128, 2, 256], F32)
    negF = K.sb([128, NSEQ, 16, 16], F32)
    negbf = K.sb([16, 1], F32)
    onecol = K.sb([128, 1], F32)
    nhalf = K.sb([128, 1], F32)
    small = K.sb([128, 64], F32)
    HT = K.sb([128, NCH, S], BF16)
    HTB = [Buf() for _ in range(NTB)]
    W_BASE = K.off
    K.off += 65536
    H_BASE = K.off
    H_SIZE = SBUF_END - H_BASE
    assert H_SIZE >= 60 * 1024, H_SIZE

    def region(base, size):
        st = {"o": base}

        def take(shape, dt):
            el = 2 if dt == BF16 else 4
            nb = (int(np.prod(shape[1:])) * el + 63) // 64 * 64
            t = K.sb(shape, dt, at=st["o"])
            st["o"] += nb
            assert st["o"] <= base + size, ("region overflow", st["o"] - base, size)
            return t
        return take

    wtake = region(W_BASE, 65536)
    wslot = [Buf(wtake([128, NCH, 512], BF16)) for _ in range(4)]
    wtake2 = region(W_BASE, 65536)
    wf32 = [wtake2([128, NCH, 512], F32) for _ in range(2)]

    ctick = []

    def setup_consts():
        t = P.op("pool", K.ms(ident[:], 1.0))
        P.wait("pool", t)
        t = P.op("pool", lambda e: e.affine_select(out=ident[:], in_=ident[:], pattern=[[-1, 128]],
                                                   compare_op=ALU.is_equal, fill=0.0, base=0, channel_multiplier=1))
        P.wait("pool", t)
        ctick.append(P.op("pool", K.cp(identb[:], ident[:])))
        ctick.append(P.op("pool", K.ms(onesb[:], 1.0)))
        ctick.append(P.op("pool", K.ms(onecol[:], 1.0)))
        ctick.append(P.op("pool", K.ms(nhalf[:], -0.5)))
        t = P.op("pool", K.ms(maskd[:], 0.0))
        P.wait("pool", t)
        ctick.append(P.op("pool", lambda e: e.affine_select(out=maskd[:], in_=maskd[:], pattern=[[1, 128]],
                                                            compare_op=ALU.is_ge, fill=-30000.0, base=0,
                                                            channel_multiplier=-1)))
        io = K.sb([128, 17], I32, at=H_BASE)
        iof = K.sb([128, 17], F32, at=H_BASE + 128)
        t = P.op("pool", lambda e: e.iota(io[:], pattern=[[-128, 17]], base=-64, channel_multiplier=1))
        P.wait("pool", t)
        t = P.op("pool", K.cp(iof[:], io[:]))
        P.wait("pool", t)
        for h in range(8):
            slope = 2.0 ** (-8.0 * (h + 1) / 8)
            t = P.op("pool", K.ts(biasA[:, h, :], iof[:], slope, ALU.mult))
        ctick.append(t)
        P.wait("pool", t)
        io2 = K.sb([128, 18], I32, at=H_BASE + 8192)
        iof2 = K.sb([128, 18], F32, at=H_BASE + 8192 + 128)
        t = P.op("pool", lambda e: e.iota(io2[:], pattern=[[-128, 18]], base=0, channel_multiplier=1))
        P.wait("pool", t)
        t = P.op("pool", K.cp(iof2[:], io2[:]))
        P.wait("pool", t)
        for h in range(8):
            slope = 2.0 ** (-8.0 * (h + 1) / 8)
            t = P.op("pool", K.ts(biasB[:, h, :], iof2[:], slope, ALU.mult))
        ctick.append(t)
        craw = K.sb([128, 2 * NCH], F32, at=H_BASE + 256)
        bfr = K.sb([16, 1], F32, at=H_BASE + 512)
        P.op("sp", lambda e: e.dma_start(out=craw[:], in_=cT_d), key=("dma", "c0"), amount=16)
        P.op("sp", lambda e: e.dma_start(out=bfr[:], in_=kv_b_f_d), key=("dma", "c0"), amount=16)
        lt = [K.sb([128, 2, 128], F32, at=H_BASE + 1024 + i * 1024) for i in range(4)]
        sg = K.sb([128, 2, 256], F32, at=H_BASE + 1024 + 4096)
        junk = K.sb([128, 128], F32, at=H_BASE + 1024 + 4096 + 2048)
        ee = K.sb([128, 4], F32, at=H_BASE + 1024 + 4096 + 2048 + 512)
        tl = []
        for i in range(4):
            for l in range(2):
                tl.append(P.op("sp", lambda e, i=i, l=l: e.dma_start(out=lt[i][:, l, :], in_=lam_d[i][l:l + 1, :].partition_broadcast(128)),
                               key=("dma", "c0"), amount=16))
        for l in range(2):
            tl.append(P.op("sp", lambda e, l=l: e.dma_start(out=sg[:, l, :], in_=subln_d[l:l + 1, :].partition_broadcast(128)),
                           key=("dma", "c0"), amount=16))
        P.wait("dve", tl[-1])
        P.wait("act", tl[-1])
        ctick.append(P.op("act", K.actf(cact[:], craw[:], AF.Silu)))
        ctick.append(P.op("dve", K.ts(negbf[:], bfr[:], -1.0, ALU.mult)))
        t = None
        for l in range(2):
            for pr in range(2):
                if t is not None:
                    P.wait("dve", t)
                t = P.op("dve", K.stt(junk[:], lt[2 * pr][:, l, :], 1.0, lt[2 * pr + 1][:, l, :], ALU.mult, ALU.mult,
                                      accum_out=small[:, l * 2 + pr:l * 2 + pr + 1]))
        P.wait("act", t)
        t = P.op("act", K.actf(ee[:], small[:, 0:4], AF.Exp))
        P.wait("dve", t)
        for l in range(2):
            lam_init = 0.8 - 0.6 * math.exp(-0.3 * l)
            t = P.op("dve", K.tt(neglam[:, l:l + 1], ee[:, 2 * l + 1:2 * l + 2], ee[:, 2 * l:2 * l + 1], ALU.subtract))
            P.wait("dve", t)
            t = P.op("dve", K.ts(neglam[:, l:l + 1], neglam[:, l:l + 1], -lam_init, ALU.add))
            t = P.op("dve", K.ts(gsub[:, l, :], sg[:, l, :], 1.0 - lam_init, ALU.mult))
            P.wait("dve", t)
            t = P.op("dve", K.ms(nlam256[:, l, :], 1.0))
            P.wait("dve", t)
            t = P.op("dve", K.ts(nlam256[:, l, :], nlam256[:, l, :], neglam[:, l:l + 1], ALU.mult))
        ctick.append(t)
        for e in ENGS:
            P.wait(e, ctick)

    def mod_prologue():
        mods = [(w_mod_d[l], b_mod_d[l:l + 1, :], 3 * D) for l in range(DEPTH)] + [(kv_w_mod_d, kv_b_mod_d, 2 * D)]
        hb = region(H_BASE, H_SIZE)
        brow = [Buf(hb([2, 512], F32)) for _ in range(2)]
        mrow = [Buf(hb([2, 512], F32)) for _ in range(2)]
        wsl = [Buf(wf32[0]), Buf(wf32[1])]
        pm = [banks[0], banks[1]]
        pc = banks[2]
        colb = Buf()
        si = 0
        for i, (W, Bv, ncols) in enumerate(mods):
            for s in range(ncols // 512):
                n0 = s * 512
                ws = wsl[si % 2]
                br = brow[si % 2]
                mr = mrow[si % 2]
                pmb = pm[si % 2]
                K.acquire("sp", writes=[ws])
                for hh in range(2):
                    t = P.op("sp", lambda e, ws=ws, W=W, n0=n0, hh=hh: e.dma_start(
                        out=ws.t[:, hh * 8:(hh + 1) * 8, :],
                        in_=W[hh * 1024:(hh + 1) * 1024, n0:n0 + 512].rearrange("(c p) n -> p c n", p=128)),
                        key=("dma", "ws%d" % (si % 2)), amount=16)
                K.release(t, writes=[ws])
                K.dma("sp", br.t[:], Bv[:, n0:n0 + 512].partition_broadcast(2), writes=[br], key="br%d" % (si % 2))
                K.acquire("pe", reads=[ws], writes=[pmb])
                for ch in range(NCH):
                    t = P.op("pe", K.mm(pmb.t[0:2, :], cact[:, 2 * ch:2 * ch + 2], ws.t[:, ch, :], ch == 0, ch == NCH - 1),
                             inc=(ch == NCH - 1))
                K.release(t, reads=[ws], writes=[pmb])
                K.do("dve", K.tt(mr.t[:], pmb.t[0:2, :], br.t[:], ALU.add), reads=[pmb, br], writes=[mr])
                if n0 < 2 * D:
                    K.acquire("pe", reads=[mr], writes=([pc] if n0 == 0 else []))
                    for q in range(4):
                        ci = n0 // 128 + q
                        t = P.op("pe", K.mm(pc.t[:, ci * 2:ci * 2 + 2], mr.t[:, q * 128:(q + 1) * 128], ident[0:2, 0:2]),
                                 inc=(q == 3))
                    K.release(t, reads=[mr])
                    if n0 == 2 * D - 512:
                        K.release(t, writes=[pc])
                        K.do("dve", K.cp(colmod[:, i, 0:32], pc.t[:, 0:32]), reads=[pc], writes=[colb])
                        K.do("dve", K.ts(colmod[:, i, 32:64], pc.t[:, 32:64], 1.0, ALU.add), reads=[pc], writes=[colb])
                else:
                    K.dma("sp", gate_d[i, :, n0 - 2 * D:n0 - 2 * D + 512], mr.t[:], reads=[mr], key="gst%d" % (si % 2))
                si += 1
        for e in ENGS:
            P.wait(e, colb.wt)
        return [Tk(("dma", "gst0"), P.cnt.get(("dma", "gst0"), 0)), Tk(("dma", "gst1"), P.cnt.get(("dma", "gst1"), 0))]

    def phase1(modi, b, src):
        hb = region(H_BASE, H_SIZE)
        xr = Ring([Buf(hb([128, D], F32)) for _ in range(2)])
        tring = Ring([banks[6], banks[7]])
        for tb in range(NTB):
            xt = xr.next()
            K.dma("sp", xt.t[:], src[b, tb * 128:(tb + 1) * 128, :], writes=[xt], key="x%d" % (tb % 2))
            for g4 in range(4):
                bk = tring.next()
                K.acquire("pe", reads=[xt], writes=[bk])
                for k in range(4):
                    ch = g4 * 4 + k
                    t = P.op("pe", K.tr(bk.t[:, k * 128:(k + 1) * 128], xt.t[:, ch * 128:(ch + 1) * 128], ident[:]), inc=(k == 3))
                K.release(t, reads=[xt], writes=[bk])
                K.acquire("act", reads=[bk], writes=([HTB[tb]] if g4 == 0 else []))
                for k in range(4):
                    ch = g4 * 4 + k
                    t = P.op("act", K.actf(HT[:, ch, tb * 128:(tb + 1) * 128], bk.t[:, k * 128:(k + 1) * 128], AF.Identity,
                                           bias=colmod[:, modi, ch * 2 + b:ch * 2 + b + 1],
                                           scale=colmod[:, modi, 32 + ch * 2 + b:32 + ch * 2 + b + 1]), inc=(k == 3))
                K.release(t, reads=[bk])
            K.release(t, writes=[HTB[tb]])

    class Slabs:
        def __init__(self, nslots):
            self.slots = wslot[:nslots]
            self.i = 0

        def load(self, W, n0, ncols=512):
            s = self.slots[self.i % len(self.slots)]
            k = self.i % len(self.slots)
            self.i += 1
            K.acquire("pool", writes=[s])
            for hh in range(2):
                t = P.op("pool", lambda e, s=s, W=W, n0=n0, hh=hh, ncols=ncols: e.dma_start(
                    out=s.t[:, hh * 8:(hh + 1) * 8, 0:ncols],
                    in_=W[hh * 1024:(hh + 1) * 1024, n0:n0 + ncols].rearrange("(c p) n -> p c n", p=128)),
                    key=("dma", "wsl%d" % k), amount=16)
            K.release(t, writes=[s])
            return s

    def proj_feature_major(hb_stage, slab, n_f0, dst, b, func, scale, st):
        for fc in range(4):
            grp = st["grp"] % 2
            st["grp"] += 1
            bks = banks[grp * 4:grp * 4 + 4]
            K.acquire("pe", reads=[slab] + HTB, writes=bks)
            for ch in range(NCH):
                for tt in range(4):
                    t = P.op("pe", K.mm(bks[tt].t[:, :], slab.t[:, ch, fc * 128:(fc + 1) * 128], HT[:, ch, tt * 512:(tt + 1) * 512],
                                        ch == 0, ch == NCH - 1), inc=(ch == NCH - 1 and tt == 3))
            K.release(t, reads=[slab] + HTB, writes=bks)
            stg = hb_stage.next()
            K.acquire("act", writes=[stg])
            K.acquire("dve", writes=[stg])
            stg.wt = None
            stg.rts = {}
            parts = []
            for tt in range(4):
                use_act = (func is not None) or ((st["ev"] % 2) == 0)
                st["ev"] += 1
                o = stg.t[:, tt * 512:(tt + 1) * 512]
                if use_act:
                    af = func if func is not None else (AF.Identity if scale is not None else AF.Copy)
                    fn = K.actf(o, bks[tt].t[:, :], af, scale=scale)
                    tl = K.do("act", fn, reads=[bks[tt]])
                else:
                    if scale is not None:
                        fn = K.ts(o, bks[tt].t[:, :], scale, ALU.mult)
                    else:
                        fn = K.cp(o, bks[tt].t[:, :])
                    tl = K.do("dve", fn, reads=[bks[tt]])
                parts.append(tl)
            P.wait("sp", parts)
            f0 = n_f0 + fc * 128
            tstore = K.dma("sp", dst[b, f0:f0 + 128, :], stg.t[:], reads=[stg], key="fst%d" % stg.idx)
            st["stores"].append(tstore)

    def proj_token_major(hb_stage, slab, n0, dst, b, func, st):
        for tb in range(NTB):
            bk = banks[st["tbk"] % 8]
            st["tbk"] += 1
            K.acquire("pe", reads=[slab, HTB[tb]], writes=[bk])
            for ch in range(NCH):
                t = P.op("pe", K.mm(bk.t[:, :], HT[:, ch, tb * 128:(tb + 1) * 128], slab.t[:, ch, :], ch == 0, ch == NCH - 1),
                         inc=(ch == NCH - 1))
            K.release(t, reads=[slab, HTB[tb]], writes=[bk])
            stg = hb_stage.next()
            use_act = (func is not None) or ((st["ev"] % 2) == 0)
            st["ev"] += 1
            if use_act:
                K.do("act", K.actf(stg.t[:], bk.t[:, :], func if func is not None else AF.Copy), reads=[bk], writes=[stg])
            else:
                K.do("dve", K.cp(stg.t[:], bk.t[:, :]), reads=[bk], writes=[stg])
            tstore = K.dma("sp", dst[b, tb * 128:(tb + 1) * 128, n0:n0 + 512], stg.t[:], reads=[stg], key="tst%d" % stg.idx)
            st["stores"].append(tstore)

    def mk_stage(hb, n, shape):
        bufs = []
        for i in range(n):
            bb = Buf(hb(shape, BF16))
            bb.idx = i
            bb.parts = []
            bufs.append(bb)
        return Ring(bufs)

    def plan_of(kind, l):
        if kind == "A":
            return a_w_in_d[l], 16
        if kind == "B":
            return b_w_in_d[l - 2], 8
        return kv_w_d, 8

    def preload(kind, l):
        W, ns = plan_of(kind, l)
        slabs = Slabs(3)
        loaded = {}
        for s in range(min(3, ns)):
            loaded[s] = slabs.load(W, s * 512)
        return slabs, loaded

    def phase2(kind, l, b, hb, pre):
        st = {"grp": 0, "ev": 0, "tbk": 0, "stores": []}
        fstage = mk_stage(hb, 2, [128, S])
        tstage = mk_stage(hb, 4, [128, 512])
        slabs, loaded = pre
        if kind == "A":
            W = a_w_in_d[l]
            plan = [("f", qT_d, None, QSCALE)] * 4 + [("f", kT_d, None, None)] * 4 + [("t", v_d, None, None)] * 4 + [("t", g_d, AF.Silu, None)] * 4
        elif kind == "B":
            W = b_w_in_d[l - 2]
            plan = [("f", qT_d, None, QSCALE)] * 4 + [("f", g_d, AF.Silu, None)] * 4
        else:
            W = kv_w_d
            plan = [("f", kTs_d, None, None)] * 4 + [("t", vs_d, None, None)] * 4
        ns = len(plan)
        for s in range(ns):
            typ, dst, func, scale = plan[s]
            slab = loaded.pop(s)
            if typ == "f":
                proj_feature_major(fstage, slab, (s % 4) * 512, dst, b, func, scale, st)
            else:
                proj_token_major(tstage, slab, (s % 4) * 512, dst, b, func, st)
            if s + 3 < ns:
                loaded[s + 3] = slabs.load(W, (s + 3) * 512)
        return st["stores"]

    def phase3A(l, b, stores, hb):
        P.wait("sp", stores)
        sets = []
        for i in range(2):
            sets.append({"q": Buf(hb([128, 2, S], BF16)), "k": Buf(hb([128, 2, S], BF16)),
                         "v": Buf(hb([128, NTB, 257], BF16)), "i": i})
        for s_ in sets:
            K.do("pool", K.ms(s_["v"].t[:, :, 256:257], 1.0), writes=[s_["v"]])
        NG = 5
        gbufs = [Buf(hb([128, 256], BF16)) for _ in range(NG)]
        for i, g_ in enumerate(gbufs):
            g_.idx = i
        gring = Ring(gbufs)
        ggr = Ring([Buf(hb([128, 256], F32)) for _ in range(4)])
        ptring = Ring([Buf(hb([128, 2, 256], BF16)) for _ in range(6)])
        tmpr = Ring([Buf(hb([128, 256], F32)) for _ in range(2)])
        orr = Ring([Buf(hb([128, 256], F32)) for _ in range(4)])
        ofr = Ring([Buf(hb([128, 256], BF16)) for _ in range(4)])
        smr = Ring([Buf(hb([128, 8], F32)) for _ in range(8)])
        sslots = []
        for bi in range(3):
            sb_ = banks[bi]
            sb_.v = sb_.t[:, 0:512].rearrange("p (m q) -> p m q", m=2)
            sslots.append(sb_)
        sring = Ring(sslots)
        osets = [(banks[3], banks[4]), (banks[5], banks[6])]
        banks[7].v = ptb_ap[:, 0:256].rearrange("p (e t) -> p e t", e=2)
        pslot = banks[7]

        def load_head(h):
            s_ = sets[h % 2]
            K.dma("sp", s_["q"].t[:], qT_d[b, h * 256:(h + 1) * 256, :].rearrange("(m p) t -> p m t", p=128),
                  writes=[s_["q"]], key="hq%d" % s_["i"])
            K.dma("sp", s_["k"].t[:], kT_d[b, h * 256:(h + 1) * 256, :].rearrange("(m p) t -> p m t", p=128),
                  writes=[s_["k"]], key="hk%d" % s_["i"])
            K.acquire("sp", writes=[s_["v"]])
            for hh in range(2):
                t = P.op("sp", lambda e, s_=s_, hh=hh: e.dma_start(
                    out=s_["v"].t[:, hh * 8:(hh + 1) * 8, 0:256],
                    in_=v_d[b, hh * 1024:(hh + 1) * 1024, h * 256:(h + 1) * 256].rearrange("(kb p) e -> p kb e", p=128)),
                    key=("dma", "hv%d" % s_["i"]), amount=16)
            K.release(t, writes=[s_["v"]])

        steps = [(h, qt2, j) for h in range(8) for qt2 in range(NTB // 2) for j in range(2 * qt2 + 2)]
        pend = {}
        first_w = set()
        state = {"fid": 0}
        gts = {}
        dq = []
        FMAX = 3

        def load_g(h, qb):
            gt = gring.next()
            K.dma("sp", gt.t[:], g_d[b, qb * 128:(qb + 1) * 128, h * 256:(h + 1) * 256], writes=[gt], key="g%d" % gt.idx)
            gts[(h, qb)] = gt

        def run_due(idx, force_upto=-1):
            i = 0
            while i < len(dq):
                due, fid, fn = dq[i]
                if due <= idx or fid <= force_upto:
                    dq.pop(i)
                    fn(idx)
                    i = 0
                else:
                    i += 1

        def qk(idx):
            h, qt2, j = steps[idx]
            s_ = sets[h % 2]
            sl = sring.next()
            qlo = 128 if j == 2 * qt2 + 1 else 0
            K.acquire("pe", reads=[s_["q"], s_["k"]], writes=[sl])
            for m in range(2):
                t = P.op("pe", K.mm(sl.v[:, m, qlo:256], s_["k"].t[:, m, j * 128:(j + 1) * 128],
                                    s_["q"].t[:, m, qt2 * 256 + qlo:(qt2 + 1) * 256]), inc=(m == 1))
            K.release(t, reads=[s_["q"], s_["k"]], writes=[sl])
            pts = {}
            pt = ptring.next()

            def mask(view):
                K.do("pool", lambda e: e.affine_select(out=view, in_=view, pattern=[[0, 2], [1, 128]],
                                                       compare_op=ALU.is_ge, fill=0.0, base=0, channel_multiplier=-1),
                     reads=[], writes=[pt])

            if j == 2 * qt2 + 1:
                bdiag = biasA[:, h, 0:1] if h == 0 else biasB[:, h, 0:1]
                K.do("act", K.actf(pt.t[:, :, 128:256], sl.v[:, :, 128:256], AF.Exp, bias=bdiag, scale=1.0),
                     reads=[sl], writes=[pt])
                mask(pt.t[:, :, 128:256])
                pts[1] = (pt, 128)
            else:
                d0 = 2 * qt2 - j
                if h >= 1:
                    K.do("act", K.actf(pt.t[:], sl.v, AF.Exp, bias=biasB[:, h, d0 + 1:d0 + 2], scale=1.0), reads=[sl], writes=[pt])
                else:
                    for r in range(2):
                        K.do("act", K.actf(pt.t[:, :, r * 128:(r + 1) * 128], sl.v[:, :, r * 128:(r + 1) * 128], AF.Exp,
                                           bias=biasA[:, h, d0 + r:d0 + r + 1], scale=1.0), reads=[sl], writes=[pt])
                if d0 == 0:
                    mask(pt.t[:, :, 0:128])
                pts[0] = (pt, 0)
                pts[1] = (pt, 128)
            pend[idx] = pts

        def pv(idx, cur):
            h, qt2, j = steps[idx]
            s_ = sets[h % 2]
            pts = pend.pop(idx)
            for r in range(2):
                if r not in pts:
                    continue
                qb = 2 * qt2 + r
                pt, off = pts[r]
                ob = osets[r]
                K.acquire("pe", reads=[pt, s_["v"]], writes=(list(ob) if j == 0 else []))
                for m in range(2):
                    t = P.op("pe", K.mm(ob[m].t[:, 0:257], pt.t[:, m, off:off + 128], s_["v"].t[:, j, :], j == 0, j == qb), inc=(m == 1))
                K.release(t, reads=[pt, s_["v"]])
                if j == qb:
                    K.release(t, writes=list(ob))
                    finalize(h, qb, ob, cur)

        def finalize(h, qb, ob, cur):
            fid = state["fid"]
            state["fid"] += 1
            run_due(cur, force_upto=fid - FMAX)
            sm = smr.next()
            tmp = tmpr.next()
            o = orr.next()
            gg = ggr.next()
            gt = gts.pop((h, qb))
            nxt = (h, qb + 1) if qb + 1 < NTB else ((h + 1, 0) if h + 1 < 8 else None)
            if nxt is not None:
                load_g(*nxt)
            lcol = ps[:, ob[0].bank * 512 + 256:ob[1].bank * 512 + 257:512]
            K.do("dve", lambda e: e.reciprocal(out=sm.t[:, 0:2], in_=lcol), reads=[ob[0], ob[1]], writes=[sm])
            K.do("dve", K.stt(tmp.t[:], ob[1].t[:, 0:256], sm.t[:, 1:2], nlam256[:, l, :], ALU.mult, ALU.mult),
                 reads=[ob[1], sm], writes=[tmp])
            K.do("dve", K.stt(o.t[:], ob[0].t[:, 0:256], sm.t[:, 0:1], tmp.t[:], ALU.mult, ALU.add), reads=[ob[0], tmp, sm], writes=[o])
            K.do("dve", K.stt(tmp.t[:], o.t[:], 1.0, o.t[:], ALU.mult, ALU.mult, accum_out=sm.t[:, 3:4]), reads=[o], writes=[tmp, sm])
            K.do("dve", K.ts(sm.t[:, 4:5], sm.t[:, 3:4], 1.0 / 256, ALU.mult, EPS, ALU.add), reads=[sm], writes=[sm])
            K.do("pool", K.tt(sm.t[:, 5:6], sm.t[:, 4:5], nhalf[:], ALU.pow), reads=[sm], writes=[sm])
            K.do("pool", K.tt(gg.t[:], gt.t[:], gsub[:, l, :], ALU.mult), reads=[gt], writes=[gg])

            def stage1b(cur2):
                of = ofr.next()
                K.do("dve", K.stt(of.t[:], o.t[:], sm.t[:, 5:6], gg.t[:], ALU.mult, ALU.mult), reads=[o, sm, gg], writes=[of])

                def stage2(cur3):
                    K.acquire("pe", reads=[of], writes=[pslot])
                    for e2 in range(2):
                        t = P.op("pe", K.tr(pslot.v[:, e2, :], of.t[:, e2 * 128:(e2 + 1) * 128], identb[:]), inc=(e2 == 1))
                    K.release(t, reads=[of], writes=[pslot])
                    wr = [HTB[qb]] if (qb not in first_w) else []
                    first_w.add(qb)
                    K.acquire("dve", reads=[pslot], writes=wr)
                    t = P.op("dve", K.cp(HT[:, 2 * h:2 * h + 2, qb * 128:(qb + 1) * 128], pslot.v))
                    K.release(t, reads=[pslot], writes=[HTB[qb]])

                dq.append([cur2 + 2, fid, stage2])

            dq.append([cur + 2, fid, stage1b])

        LA = 3
        n = len(steps)
        for idx in range(n + LA):
            if idx == 0:
                load_head(0)
                load_head(1)
                load_g(0, 0)
            if idx < n:
                qk(idx)
            if idx >= LA:
                pv(idx - LA, idx)
                h, qb, j = steps[idx - LA]
                if qb == 0 and j == 0 and h >= 1 and h + 1 < 8:
                    load_head(h + 1)
            run_due(idx)
        while dq:
            run_due(0, force_upto=10 ** 9)

    def phase3B(l, b, stores, hb):
        P.wait("sp", stores)
        sets = []
        for i in range(2):
            sets.append({"q": Buf(hb([128, S], BF16)), "g": Buf(hb([128, S], BF16)), "k": Buf(hb([128, S], BF16)),
                         "v": Buf(hb([128, NTB, 128], BF16)), "f": Buf(hb([128, S], F32)), "i": i})
        Tr = Ring([Buf(hb([128, 512], F32)) for _ in range(3)])
        ptring = Ring([Buf(hb([128, 512], BF16)) for _ in range(4)])
        rcr = Ring([Buf(hb([128, 512], F32)) for _ in range(2)])
        onr = Ring([Buf(hb([128, 512], F32)) for _ in range(2)])
        sring = Ring([banks[0], banks[1], banks[2], banks[7]])
        osets = [(banks[3], banks[5]), (banks[4], banks[6])]

        def load_head(h):
            s_ = sets[h % 2]
            i = s_["i"]
            K.dma("sp", s_["q"].t[:], qT_d[b, h * 128:(h + 1) * 128, :], writes=[s_["q"]], key="hq%d" % i)
            K.dma("sp", s_["k"].t[:], kTs_d[b, h * 128:(h + 1) * 128, :], writes=[s_["k"]], key="hk%d" % i)
            K.acquire("sp", writes=[s_["v"]])
            for hh in range(2):
                t = P.op("sp", lambda e, s_=s_, hh=hh: e.dma_start(
                    out=s_["v"].t[:, hh * 8:(hh + 1) * 8, :],
                    in_=vs_d[b, hh * 1024:(hh + 1) * 1024, h * 128:(h + 1) * 128].rearrange("(kb p) e -> p kb e", p=128)),
                    key=("dma", "hv%d" % i), amount=16)
            K.release(t, writes=[s_["v"]])
            K.dma("sp", s_["f"].t[:], FT_d[b, h:h + 1, :].partition_broadcast(128), writes=[s_["f"]], key="hf%d" % i)
            K.dma("sp", s_["g"].t[:], g_d[b, h * 128:(h + 1) * 128, :], writes=[s_["g"]], key="hg%d" % i)

        steps = [(h, qt, j) for h in range(16) for qt in range(4) for j in range(4 * qt + 4)]
        pend = {}
        first_w = set()
        state = {"oset": 0}

        def qk(idx):
            h, qt, j = steps[idx]
            s_ = sets[h % 2]
            jl = j - 4 * qt
            qlo = 128 * jl if jl > 0 else 0
            sl = sring.next()
            K.acquire("pe", reads=[s_["q"], s_["k"]], writes=[sl])
            t = P.op("pe", K.mm(sl.t[:, qlo:512], s_["k"].t[:, j * 128:(j + 1) * 128], s_["q"].t[:, qt * 512 + qlo:(qt + 1) * 512]))
            K.release(t, reads=[s_["q"], s_["k"]], writes=[sl])
            T = Tr.next()
            K.do("dve", K.stt(T.t[:, qlo:512], sl.t[:, qlo:512], negF[:, b, j, h:h + 1], s_["f"].t[:, qt * 512 + qlo:(qt + 1) * 512],
                              ALU.add, ALU.add), reads=[sl, s_["f"]], writes=[T])
            if jl >= 0:
                K.do("pool", K.tt(T.t[:, qlo:qlo + 128], T.t[:, qlo:qlo + 128], maskd[:], ALU.add), reads=[T], writes=[T])
            pt = ptring.next()
            K.do("act", K.actf(pt.t[:, qlo:512], T.t[:, qlo:512], AF.Exp), reads=[T], writes=[pt])
            pend[idx] = (pt, qlo)

        def pv(idx):
            h, qt, j = steps[idx]
            s_ = sets[h % 2]
            pt, qlo = pend.pop(idx)
            last = (j == 4 * qt + 3)
            if j == 0:
                state["oset"] ^= 1
            ob, lb = osets[state["oset"]]
            K.acquire("pe", reads=[pt, s_["v"]], writes=([ob, lb] if j == 0 else []))
            P.op("pe", K.mm(ob.t[:, qlo:512], s_["v"].t[:, j, :], pt.t[:, qlo:512], j == 0, last), inc=False)
            t = P.op("pe", K.mm(lb.t[:, qlo:512], onesb[:], pt.t[:, qlo:512], j == 0, last))
            K.release(t, reads=[pt, s_["v"]])
            if last:
                K.release(t, writes=[ob, lb])
                rc = rcr.next()
                on = onr.next()
                K.do("dve", lambda e: e.reciprocal(out=rc.t[:], in_=lb.t[:, :]), reads=[lb], writes=[rc])
                K.do("dve", K.tt(on.t[:], ob.t[:, :], rc.t[:], ALU.mult), reads=[ob, rc], writes=[on])
                tbs = [HTB[qt * 4 + k] for k in range(4)]
                wr = [x_ for x_ in tbs if x_.id not in first_w]
                for x_ in tbs:
                    first_w.add(x_.id)
                K.acquire("pool", reads=[on, s_["g"]], writes=wr)
                t = P.op("pool", K.tt(HT[:, h, qt * 512:(qt + 1) * 512], on.t[:], s_["g"].t[:, qt * 512:(qt + 1) * 512], ALU.mult))
                K.release(t, reads=[on, s_["g"]], writes=tbs)

        LA = 3
        n = len(steps)
        for idx in range(n + LA):
            if idx == 0:
                load_head(0)
                load_head(1)
            if idx < n:
                qk(idx)
            if idx >= LA:
                pv(idx - LA)
                h, qt, j = steps[idx - LA]
                if qt == 0 and j == 0 and h >= 1 and h + 1 < 16:
                    load_head(h + 1)

    def prefetch_wout(W):
        for n in range(4):
            s = wslot[n]
            K.acquire("pool", writes=[s])
            for hh in range(2):
                t = P.op("pool", lambda e, s=s, W=W, n=n, hh=hh: e.dma_start(
                    out=s.t[:, hh * 8:(hh + 1) * 8, :],
                    in_=W[hh * 1024:(hh + 1) * 1024, n * 512:(n + 1) * 512].rearrange("(c p) n -> p c n", p=128)),
                    key=("dma", "wsl%d" % n), amount=16)
            K.release(t, writes=[s])

    def phase4(l, b, src, dst, gate_t, hb):
        lng = Buf(hb([128, D], F32))
        lnb = Buf(hb([128, D], F32))
        gtb = Buf(hb([128, D], F32))
        xr = Ring([Buf(hb([128, D], F32)) for _ in range(2)])
        zr = Ring([Buf(hb([128, D], F32)) for _ in range(2)])
        smr = Ring([Buf(hb([128, 8], F32)) for _ in range(2)])
        P.wait("sp", gate_t)
        K.dma("sp", lng.t[:], ln_g_d[l:l + 1, :].partition_broadcast(128), writes=[lng], key="lng")
        K.dma("sp", lnb.t[:], ln_b_d[l:l + 1, :].partition_broadcast(128), writes=[lnb], key="lnb")
        K.dma("sp", gtb.t[:], gate_d[l, b:b + 1, :].partition_broadcast(128), writes=[gtb], key="gtb")
        stores = []
        xts = {}

        def ldx(tb):
            xt = xr.next()
            K.dma("sp", xt.t[:], src[b, tb * 128:(tb + 1) * 128, :], writes=[xt], key="x%d" % (tb % 2))
            xts[tb] = xt

        ldx(0)
        for tb in range(NTB):
            if tb + 1 < NTB:
                ldx(tb + 1)
            xt = xts.pop(tb)
            bks = banks[(tb % 2) * 4:(tb % 2) * 4 + 4]
            K.acquire("pe", reads=wslot + [HTB[tb]], writes=bks)
            for fc in range(NCH):
                for n in range(4):
                    t = P.op("pe", K.mm(bks[n].t[:, :], HT[:, fc, tb * 128:(tb + 1) * 128], wslot[n].t[:, fc, :], fc == 0, fc == NCH - 1),
                             inc=(fc == NCH - 1 and n == 3))
            K.release(t, reads=wslot + [HTB[tb]], writes=bks)
            z = zr.next()
            sm = smr.next()
            K.acquire("dve", writes=[z])
            for n in range(4):
                t = K.do("dve", K.tt(z.t[:, n * 512:(n + 1) * 512], bks[n].t[:, :], gtb.t[:, n * 512:(n + 1) * 512], ALU.mult),
                         reads=[bks[n], gtb])
            K.release(t, writes=[z])
            K.do("dve", K.stt(z.t[:], xt.t[:], DN_ALPHA, z.t[:], ALU.mult, ALU.add, accum_out=sm.t[:, 0:1]),
                 reads=[xt, z], writes=[z, sm])
            K.do("act", K.actf(xt.t[:], z.t[:], AF.Square, accum_out=sm.t[:, 1:2]), reads=[z], writes=[xt, sm])
            K.do("dve", K.ts(sm.t[:, 2:3], sm.t[:, 0:1], 1.0 / D, ALU.mult), reads=[sm], writes=[sm])
            K.do("dve", K.tt(sm.t[:, 3:4], sm.t[:, 2:3], sm.t[:, 2:3], ALU.mult), reads=[sm], writes=[sm])
            K.do("dve", K.stt(sm.t[:, 4:5], sm.t[:, 1:2], 1.0 / D, sm.t[:, 3:4], ALU.mult, ALU.subtract), reads=[sm], writes=[sm])
            K.do("dve", K.ts(sm.t[:, 7:8], sm.t[:, 4:5], EPS, ALU.add), reads=[sm], writes=[sm])
            K.do("pool", K.tt(sm.t[:, 5:6], sm.t[:, 7:8], nhalf[:], ALU.pow), reads=[sm], writes=[sm])
            K.do("dve", K.stt(sm.t[:, 6:7], sm.t[:, 2:3], -1.0, sm.t[:, 5:6], ALU.mult, ALU.mult), reads=[sm], writes=[sm])
            K.do("act", K.actf(z.t[:], z.t[:], AF.Identity, bias=sm.t[:, 6:7], scale=sm.t[:, 5:6]), reads=[sm, z], writes=[z])
            K.do("pool", K.tt(z.t[:], z.t[:], lng.t[:], ALU.mult), reads=[z, lng], writes=[z])
            K.do("pool", K.tt(z.t[:], z.t[:], lnb.t[:], ALU.add), reads=[z, lnb], writes=[z])
            stores.append(K.dma("sp", dst[b, tb * 128:(tb + 1) * 128, :], z.t[:], reads=[z], key="zst%d" % (tb % 2)))
        return stores

    def kv_extras(b, hb):
        wz = Buf(hb([128, NCH, 16], BF16))
        e1 = Buf(hb([16, S], F32))
        sp_ = Buf(hb([16, S], F32))
        fpos = Buf(hb([16, S], F32))
        fneg = Buf(hb([16, S], F32))
        K.dma("pool", wz.t[:], kv_w_d[:, 2 * D:2 * D + 16].rearrange("(c p) n -> p c n", p=128), writes=[wz], key="wz")
        bks = banks[0:4]
        K.acquire("pe", reads=[wz] + HTB, writes=bks)
        for ch in range(NCH):
            for tt in range(4):
                t = P.op("pe", K.mm(bks[tt].t[0:16, :], wz.t[:, ch, :], HT[:, ch, tt * 512:(tt + 1) * 512], ch == 0, ch == NCH - 1),
                         inc=(ch == NCH - 1 and tt == 3))
        K.release(t, reads=[wz] + HTB, writes=bks)
        for tt in range(4):
            K.do("act", K.actf(e1.t[:, tt * 512:(tt + 1) * 512], bks[tt].t[0:16, :], AF.Exp, bias=negbf[:], scale=-1.0),
                 reads=[bks[tt]], writes=([e1] if tt == 0 else []))
        K.do("act", K.actf(sp_.t[:], e1.t[:], AF.Ln, bias=onecol[0:16, :], scale=1.0), reads=[e1], writes=[sp_])
        K.do("dve", lambda e: e.tensor_tensor_scan(out=fpos.t[:], data0=sp_.t[:], data1=sp_.t[:], initial=0.0,
                                                   op0=ALU.add, op1=ALU.bypass), reads=[sp_], writes=[fpos])
        K.do("dve", K.ts(fneg.t[:], fpos.t[:], -1.0, ALU.mult), reads=[fpos], writes=[fneg])
        tst = K.dma("sp", FT_d[b], fneg.t[:], reads=[fneg], key="ftst")
        bk = banks[4]
        K.acquire("pe", reads=[fpos], writes=[bk])
        for kb in range(NTB):
            t = P.op("pe", K.tr(bk.t[:, kb * 16:(kb + 1) * 16], fpos.t[:, kb * 128:(kb + 1) * 128], ident[0:16, 0:16]), inc=(kb == NTB - 1))
        K.release(t, reads=[fpos], writes=[bk])
        nb = Buf()
        K.do("dve", K.cp(negF[:, b, :, :], bk.t[:, 0:256].rearrange("p (k h) -> p k h", k=16)), reads=[bk], writes=[nb])
        for e in ENGS:
            P.wait(e, nb.wt)
        return [tst]

    class _Stop(Exception):
        pass

    stage = {"n": 0}

    def chk():
        stage["n"] += 1
        if stop is not None and stage["n"] >= stop:
            raise _Stop()

    def driver():
        setup_consts()
        chk()
        gate_t = mod_prologue()
        chk()
        kvstores = {}
        for l in range(nlayers):
            src = x_d if l == 0 else xs_d
            dst = out_d if l == nlayers - 1 else xs_d
            for b in range(NSEQ):
                if l == 2:
                    pre = preload("KV", l)
                    P.barrier()
                    phase1(4, b, xs_d)
                    P.barrier()
                    hb = region(H_BASE, H_SIZE)
                    st = phase2("KV", l, b, hb, pre)
                    st += kv_extras(b, hb)
                    kvstores[b] = st
                kind = "A" if l < 2 else "B"
                pre = preload(kind, l)
                P.barrier()
                phase1(l, b, src)
                chk()
                P.barrier()
                stores = phase2(kind, l, b, region(H_BASE, H_SIZE), pre)
                chk()
                prefetch_wout(a_w_out_d[l] if l < 2 else b_w_out_d[l - 2])
                P.barrier()
                if l < 2:
                    phase3A(l, b, stores, region(H_BASE, H_SIZE))
                else:
                    phase3B(l, b, stores + kvstores[b], region(H_BASE, H_SIZE))
                chk()
                P.barrier()
                fin = phase4(l, b, src, dst, gate_t, region(H_BASE, H_SIZE))
                P.wait("sp", fin)
                chk()

    try:
        driver()
    except _Stop:
        pass
    P.barrier(exclude=("__none__",))
    P.emit()
    return nc


_NC_CACHE = {}


def kernel(**inputs):
    n = 8
    if "nc" not in _NC_CACHE:
        _NC_CACHE["nc"] = build_program()
    nc = _NC_CACHE["nc"]
    f = lambda a: np.ascontiguousarray(np.asarray(a, dtype=np.float32))
    x = f(inputs["x"])
    c = f(inputs["c"])
    shared = {k: f(inputs[k]) for k in ("w_mod", "b_mod", "ln_g", "ln_b", "a_w_in", "a_w_out", "a_lam_q1", "a_lam_k1",
                                        "a_lam_q2", "a_lam_k2", "a_subln_g", "kv_w_mod", "kv_w", "b_w_in", "b_w_out")}
    shared["kv_b_mod"] = f(inputs["kv_b_mod"]).reshape(1, -1)
    shared["kv_b_f"] = f(inputs["kv_b_f"]).reshape(16, 1)
    in_maps = []
    for i in range(n):
        m = dict(shared)
        m["x"] = np.ascontiguousarray(x[2 * i:2 * i + 2])
        cc = c[2 * i:2 * i + 2]
        m["cT"] = np.ascontiguousarray(cc.reshape(2, NCH, 128).transpose(2, 1, 0).reshape(128, 2 * NCH))
        in_maps.append(m)
    res = run_bass_kernel_spmd(nc, in_maps, core_ids=list(range(n)))
    return np.concatenate([r["out"] for r in res.results], axis=0)
```
